# Optimizing a Trainium2 kernel written in Bass

```python
import math
import jax, jax.numpy as jnp
from jax import lax
import numpy as np

D_MODEL = 1024
BATCH = 8
SEQ = 2048
DEPTH = 4
DEC_BATCH = 128
DEC_SEQ = 1
PAST_LEN = 16384
PAGE_SIZE = 128

D_MIX = D_MODEL
GROUP_W = D_MIX // 4
S5_CH = 16
S5_GROUPS = GROUP_W // S5_CH
S5_P = 64
SSD_HEADDIM = 64
SSD_HEADS = GROUP_W // SSD_HEADDIM
SSD_NGROUPS = 2
SSD_N = 64
SSD_CONV = 4
SSD_XBC = GROUP_W + 2 * SSD_NGROUPS * SSD_N
SSD_CHUNK = 64
HG_EXPAND = 64
HG_HEADS = GROUP_W // HG_EXPAND
HG_DV = GROUP_W // HG_HEADS
LB_FLOOR = 1e-30
GLA_HEADS = 4
GLA_DK = GROUP_W // 2 // GLA_HEADS
GLA_DV = GROUP_W // GLA_HEADS
GLA_RANK = 16
GLA_TAU = 16.0
VEC_CHUNK = 32
D_FF = 4 * D_MODEL
EPS = 1e-6

SPLITS = (GROUP_W,
          GROUP_W, SSD_XBC, SSD_HEADS,
          HG_HEADS * HG_EXPAND, HG_HEADS * HG_EXPAND, HG_HEADS * HG_DV, GROUP_W,
          GLA_HEADS * GLA_DK, GLA_HEADS * GLA_DK, GLA_HEADS * GLA_DV, GROUP_W, GLA_RANK)
N_IN = sum(SPLITS)
SPLIT_POINTS = tuple(int(v) for v in np.cumsum(SPLITS)[:-1])

kernel_name = 'hybrid_s5_ssd_hgrn2_gla_decode_step'


def rmsnorm(x, g):
    x32 = x.astype(jnp.float32)
    y = x32 * lax.rsqrt(jnp.mean(x32 * x32, axis=-1, keepdims=True) + EPS)
    return (y * g.astype(jnp.float32)).astype(x.dtype)


def to_chunks(t, c):
    b, l = t.shape[:2]
    return jnp.moveaxis(t.reshape((b, l // c, c) + t.shape[2:]), 1, 0)


def from_chunks(t):
    n, b, c = t.shape[:3]
    return jnp.moveaxis(t, 0, 1).reshape((b, n * c) + t.shape[3:])


def masked_decay(seg, mask):
    return jnp.where(mask, jnp.exp(jnp.where(mask, seg, 0.0)), 0.0)


def ssd_scan(q, k, v, log_a, h0):
    c = math.gcd(q.shape[1], SSD_CHUNK)
    mask = jnp.tril(jnp.ones((c, c), bool))[None, :, :, None]

    def step(h, inp):
        qc, kc, vc, lac = inp
        cum = jnp.cumsum(lac, axis=1)
        seg = cum[:, :, None, :] - cum[:, None, :, :]
        decay = masked_decay(seg, mask)
        scores = jnp.einsum('bihn,bjhn->bijh', qc, kc) * decay
        o = (jnp.einsum('bijh,bjhp->bihp', scores, vc)
             + jnp.einsum('bihn,bhnp->bihp', qc, h) * jnp.exp(cum)[..., None])
        w = jnp.exp(cum[:, -1:, :] - cum)
        h_new = (jnp.exp(cum[:, -1])[:, :, None, None] * h
                 + jnp.einsum('bjhn,bjhp->bhnp', kc * w[..., None], vc))
        return h_new, o

    h_last, o = lax.scan(step, h0, (to_chunks(q, c), to_chunks(k, c), to_chunks(v, c), to_chunks(log_a, c)))
    return from_chunks(o), h_last


def gla_scan(q, k, v, log_a, h0):
    c = math.gcd(q.shape[1], VEC_CHUNK)
    mask = jnp.tril(jnp.ones((c, c), bool))[None, :, :, None, None]

    def step(h, inp):
        qc, kc, vc, lac = inp
        cum = jnp.cumsum(lac, axis=1)
        seg = cum[:, :, None] - cum[:, None]
        decay = masked_decay(seg, mask)
        scores = jnp.einsum('bihk,bjhk,bijhk->bijh', qc, kc, decay)
        o = (jnp.einsum('bijh,bjhv->bihv', scores, vc)
             + jnp.einsum('bihk,bhkv->bihv', qc * jnp.exp(cum), h))
        last = cum[:, -1]
        h_new = (jnp.exp(last)[..., None] * h
                 + jnp.einsum('bjhk,bjhv->bhkv', kc * jnp.exp(last[:, None] - cum), vc))
        return h_new, o

    h_last, o = lax.scan(step, h0, (to_chunks(q, c), to_chunks(k, c), to_chunks(v, c), to_chunks(log_a, c)))
    return from_chunks(o), h_last


def s5_mixer(u, h0_re, h0_im, lam_re, lam_im, log_dt, b_re, b_im, c_re, c_im, d, w_glu, b_glu):
    f32 = jnp.float32
    bsz, l, _ = u.shape
    lam = lax.complex(lam_re.astype(f32), lam_im.astype(f32))
    lam_bar = jnp.exp(lam * jnp.exp(log_dt.astype(f32))[:, None])
    b_bar = ((lam_bar - 1.0) / lam)[..., None] * lax.complex(b_re.astype(f32), b_im.astype(f32))
    c_mat = lax.complex(c_re.astype(f32), c_im.astype(f32))
    ug = u.reshape(bsz, l, S5_GROUPS, S5_CH).astype(jnp.complex64)
    bu = jnp.einsum('gpc,blgc->blgp', b_bar, ug)
    h0 = lax.complex(h0_re.astype(f32), h0_im.astype(f32))
    bu = bu.at[:, 0].add(lam_bar * h0)
    a = jnp.broadcast_to(lam_bar, bu.shape)
    _, h = lax.associative_scan(lambda e1, e2: (e1[0] * e2[0], e2[0] * e1[1] + e2[1]), (a, bu), axis=1)
    y = jnp.einsum('gcp,blgp->blgc', c_mat, h).real.reshape(bsz, l, GROUP_W) + d * u
    z = jax.nn.gelu(y)
    out = z * jax.nn.sigmoid(z @ w_glu + b_glu)
    return out, h[:, -1].real, h[:, -1].imag


def ssd_mixer(z, xbc, dt_raw, conv_buf, h0, conv_w, conv_b, dt_bias, a_log, d_skip, norm_g):
    bsz, l, _ = xbc.shape
    full = jnp.concatenate([conv_buf.astype(jnp.float32), xbc], axis=1)
    conv = sum(full[:, j:j + l] * conv_w[j] for j in range(SSD_CONV)) + conv_b
    new_buf = full[:, l:]
    act = jax.nn.silu(conv)
    x, bm, cm = jnp.split(act, [GROUP_W, GROUP_W + SSD_NGROUPS * SSD_N], axis=-1)
    x = x.reshape(bsz, l, SSD_HEADS, SSD_HEADDIM)
    rep = SSD_HEADS // SSD_NGROUPS
    bm = jnp.repeat(bm.reshape(bsz, l, SSD_NGROUPS, SSD_N), rep, axis=2)
    cm = jnp.repeat(cm.reshape(bsz, l, SSD_NGROUPS, SSD_N), rep, axis=2)
    dt = jax.nn.softplus(dt_raw + dt_bias)
    a = -jnp.exp(a_log.astype(jnp.float32))
    y, h_last = ssd_scan(cm, bm, x * dt[..., None], dt * a, h0.astype(jnp.float32))
    y = y + d_skip[:, None] * x
    y = rmsnorm(y.reshape(bsz, l, GROUP_W) * jax.nn.silu(z), norm_g)
    return y, new_buf, h_last


def hgrn2_mixer(q, f, i, gate, h0, lb, norm_g):
    bsz, l, _ = q.shape
    sh = (bsz, l, HG_HEADS, HG_EXPAND)
    q = jax.nn.silu(q).reshape(sh)
    fz = f.reshape(sh)
    lb = lb.reshape(HG_HEADS, HG_EXPAND)
    log_lb = jnp.log(jnp.maximum(lb, LB_FLOOR))
    log_f = jnp.logaddexp(jax.nn.log_sigmoid(fz), log_lb + jax.nn.log_sigmoid(-fz))
    k = (1.0 - lb) * jax.nn.sigmoid(-fz)
    v = i.reshape(bsz, l, HG_HEADS, HG_DV)
    o, h_last = gla_scan(q, k, v, log_f, h0.astype(jnp.float32))
    o = rmsnorm(o, norm_g.reshape(HG_HEADS, HG_DV)).reshape(bsz, l, GROUP_W) * jax.nn.silu(gate)
    return o, h_last


def gla_mixer(q, k, v, gate, lr, h0, w_gk2, b_gk, norm_g):
    bsz, l, _ = q.shape
    q = q.reshape(bsz, l, GLA_HEADS, GLA_DK) * (GLA_DK ** -0.5)
    k = k.reshape(bsz, l, GLA_HEADS, GLA_DK)
    v = v.reshape(bsz, l, GLA_HEADS, GLA_DV)
    log_a = jax.nn.log_sigmoid((lr @ w_gk2 + b_gk).reshape(bsz, l, GLA_HEADS, GLA_DK)) / GLA_TAU
    o, h_last = gla_scan(q, k, v, log_a, h0.astype(jnp.float32))
    o = rmsnorm(o, norm_g).reshape(bsz, l, GROUP_W) * jax.nn.silu(gate)
    return o, h_last


def layer(x, states, p):
    s5_re0, s5_im0, conv0, ssd0, hg0, gla0 = states
    hn = rmsnorm(x, p['norm_mix_g'])
    proj = (hn @ p['w_in']).astype(jnp.float32)
    (u, ssd_z, ssd_xbc, ssd_dt, hg_q, hg_f, hg_i, hg_gate,
     gla_q, gla_k, gla_v, gla_gate, gla_lr) = jnp.split(proj, SPLIT_POINTS, axis=-1)
    o_s5, s5_re1, s5_im1 = s5_mixer(u, s5_re0, s5_im0, p['s5_lam_re'], p['s5_lam_im'], p['s5_log_dt'],
                                    p['s5_b_re'], p['s5_b_im'], p['s5_c_re'], p['s5_c_im'],
                                    p['s5_d'], p['s5_w_glu'], p['s5_b_glu'])
    o_ssd, conv1, ssd1 = ssd_mixer(ssd_z, ssd_xbc, ssd_dt, conv0, ssd0, p['ssd_conv_w'], p['ssd_conv_b'],
                                   p['ssd_dt_bias'], p['ssd_a_log'], p['ssd_d'], p['ssd_norm_g'])
    o_hg, hg1 = hgrn2_mixer(hg_q, hg_f, hg_i, hg_gate, hg0, p['hg_lb'], p['hg_norm_g'])
    o_gla, gla1 = gla_mixer(gla_q, gla_k, gla_v, gla_gate, gla_lr, gla0, p['gla_w_gk2'], p['gla_b_gk'], p['gla_norm_g'])
    mix = jnp.concatenate([o_s5, o_ssd, o_hg, o_gla], axis=-1).astype(x.dtype)
    x = x + mix @ p['w_out']
    hn = rmsnorm(x, p['norm_mlp_g'])
    x = x + jnp.square(jax.nn.relu(hn @ p['w_up'])) @ p['w_down']
    return x, (s5_re1, s5_im1, conv1, ssd1, hg1, gla1)


def setup_inputs(seed: int = 0) -> dict:
    key = jax.random.key(seed)
    ks = iter(jax.random.split(key, 48))
    f32 = jnp.float32

    def nrm(shape, scale):
        return jax.random.normal(next(ks), shape, f32) * scale

    def gain(shape):
        return 1.0 + nrm(shape, 0.02)

    lam_im_base = jnp.pi * jnp.arange(S5_P, dtype=f32)
    dt_ssd = jnp.exp(jax.random.uniform(next(ks), (DEPTH, SSD_HEADS), f32, math.log(1e-3), math.log(1e-1)))
    return {
        'x_prompt': nrm((BATCH, SEQ, D_MODEL), 1.0),
        'x_sample': nrm((DEC_BATCH, DEC_SEQ, D_MODEL), 1.0),
        'state_s5_re': nrm((DEPTH, DEC_BATCH, S5_GROUPS, S5_P), 0.5),
        'state_s5_im': nrm((DEPTH, DEC_BATCH, S5_GROUPS, S5_P), 0.5),
        'state_ssd_conv': nrm((DEPTH, DEC_BATCH, SSD_CONV - 1, SSD_XBC), 1.0),
        'state_ssd': nrm((DEPTH, DEC_BATCH, SSD_HEADS, SSD_N, SSD_HEADDIM), 0.1),
        'state_hgrn': nrm((DEPTH, DEC_BATCH, HG_HEADS, HG_EXPAND, HG_DV), 0.3),
        'state_gla': nrm((DEPTH, DEC_BATCH, GLA_HEADS, GLA_DK, GLA_DV), 0.3),
        'norm_mix_g': gain((DEPTH, D_MODEL)),
        'w_in': nrm((DEPTH, D_MODEL, N_IN), D_MODEL ** -0.5),
        's5_lam_re': -0.5 + nrm((DEPTH, S5_GROUPS, S5_P), 0.01),
        's5_lam_im': lam_im_base + nrm((DEPTH, S5_GROUPS, S5_P), 0.01),
        's5_log_dt': jax.random.uniform(next(ks), (DEPTH, S5_GROUPS), f32, math.log(1e-3), math.log(1e-1)),
        's5_b_re': nrm((DEPTH, S5_GROUPS, S5_P, S5_CH), (2 * S5_CH) ** -0.5),
        's5_b_im': nrm((DEPTH, S5_GROUPS, S5_P, S5_CH), (2 * S5_CH) ** -0.5),
        's5_c_re': nrm((DEPTH, S5_GROUPS, S5_CH, S5_P), (2 * S5_P) ** -0.5),
        's5_c_im': nrm((DEPTH, S5_GROUPS, S5_CH, S5_P), (2 * S5_P) ** -0.5),
        's5_d': nrm((DEPTH, GROUP_W), 1.0),
        's5_w_glu': nrm((DEPTH, GROUP_W, GROUP_W), GROUP_W ** -0.5),
        's5_b_glu': nrm((DEPTH, GROUP_W), 0.01),
        'ssd_conv_w': nrm((DEPTH, SSD_CONV, SSD_XBC), SSD_CONV ** -0.5),
        'ssd_conv_b': nrm((DEPTH, SSD_XBC), 0.01),
        'ssd_dt_bias': dt_ssd + jnp.log(-jnp.expm1(-dt_ssd)),
        'ssd_a_log': jnp.log(jax.random.uniform(next(ks), (DEPTH, SSD_HEADS), f32, 1.0, 16.0)),
        'ssd_d': gain((DEPTH, SSD_HEADS)),
        'ssd_norm_g': gain((DEPTH, GROUP_W)),
        'hg_lb_logits': nrm((DEPTH, HG_HEADS * HG_EXPAND), 0.1),
        'hg_norm_g': gain((DEPTH, HG_HEADS * HG_DV)),
        'gla_w_gk2': nrm((DEPTH, GLA_RANK, GLA_HEADS * GLA_DK), GLA_RANK ** -0.5),
        'gla_b_gk': nrm((DEPTH, GLA_HEADS * GLA_DK), 0.01),
        'gla_norm_g': gain((DEPTH, GLA_DV)),
        'w_out': nrm((DEPTH, D_MIX, D_MODEL), D_MIX ** -0.5),
        'norm_mlp_g': gain((DEPTH, D_MODEL)),
        'w_up': nrm((DEPTH, D_MODEL, D_FF), D_MODEL ** -0.5),
        'w_down': nrm((DEPTH, D_FF, D_MODEL), D_FF ** -0.5),
        'norm_final_g': gain((D_MODEL,)),
    }


def reference(x_prompt, x_sample, state_s5_re, state_s5_im, state_ssd_conv, state_ssd, state_hgrn, state_gla,
              norm_mix_g, w_in, s5_lam_re, s5_lam_im, s5_log_dt, s5_b_re, s5_b_im, s5_c_re, s5_c_im,
              s5_d, s5_w_glu, s5_b_glu, ssd_conv_w, ssd_conv_b, ssd_dt_bias, ssd_a_log, ssd_d, ssd_norm_g,
              hg_lb_logits, hg_norm_g, gla_w_gk2, gla_b_gk, gla_norm_g, w_out, norm_mlp_g, w_up, w_down,
              norm_final_g):
    f32 = jnp.float32
    sm = jax.nn.softmax(hg_lb_logits.astype(f32), axis=0)
    lb_all = jnp.cumsum(sm, axis=0) - sm[0:1]

    sample_in = (state_s5_re, state_s5_im, state_ssd_conv, state_ssd, state_hgrn, state_gla)
    prompt_zero = tuple(jnp.zeros((BATCH,) + s.shape[2:], f32) for s in sample_in)

    hp, hs = x_prompt, x_sample
    new_p = [[] for _ in sample_in]
    new_s = [[] for _ in sample_in]
    for l in range(DEPTH):
        p = {
            'norm_mix_g': norm_mix_g[l], 'w_in': w_in[l],
            's5_lam_re': s5_lam_re[l], 's5_lam_im': s5_lam_im[l], 's5_log_dt': s5_log_dt[l],
            's5_b_re': s5_b_re[l], 's5_b_im': s5_b_im[l], 's5_c_re': s5_c_re[l], 's5_c_im': s5_c_im[l],
            's5_d': s5_d[l], 's5_w_glu': s5_w_glu[l], 's5_b_glu': s5_b_glu[l],
            'ssd_conv_w': ssd_conv_w[l], 'ssd_conv_b': ssd_conv_b[l], 'ssd_dt_bias': ssd_dt_bias[l],
            'ssd_a_log': ssd_a_log[l], 'ssd_d': ssd_d[l], 'ssd_norm_g': ssd_norm_g[l],
            'hg_lb': lb_all[l], 'hg_norm_g': hg_norm_g[l],
            'gla_w_gk2': gla_w_gk2[l], 'gla_b_gk': gla_b_gk[l], 'gla_norm_g': gla_norm_g[l],
            'w_out': w_out[l], 'norm_mlp_g': norm_mlp_g[l], 'w_up': w_up[l], 'w_down': w_down[l],
        }
        hp, st_p = layer(hp, prompt_zero, p)
        hs, st_s = layer(hs, tuple(s[l] for s in sample_in), p)
        for j in range(len(sample_in)):
            new_p[j].append(st_p[j].astype(sample_in[j].dtype))
            new_s[j].append(st_s[j].astype(sample_in[j].dtype))

    y_prompt = rmsnorm(hp, norm_final_g)
    y_sample = rmsnorm(hs, norm_final_g)
    p_s5_re, p_s5_im, p_conv, p_ssd, p_hg, p_gla = [jnp.stack(v, axis=0) for v in new_p]
    s_s5_re, s_s5_im, s_conv, s_ssd, s_hg, s_gla = [jnp.stack(v, axis=0) for v in new_s]
    return (y_prompt, y_sample, p_s5_re, p_s5_im, p_conv, p_ssd, p_hg, p_gla,
            s_s5_re, s_s5_im, s_conv, s_ssd, s_hg, s_gla)
```

```python
import numpy as np
from contextlib import ExitStack
import concourse.bass as bass
import concourse.mybir as mybir
from concourse.bass_utils import run_bass_kernel_spmd

F32 = mybir.dt.float32
BF16 = mybir.dt.bfloat16
AF = mybir.ActivationFunctionType
ALU = mybir.AluOpType
AX = mybir.AxisListType

EPOCH = 60000
DEPTH = 4
D = 1024
NP_ = 2048
NS = 16
NT = NP_ + NS
NIN = 2836
EPS = 1e-6
GN = 256
GROUPS = [(GN * i, GN) for i in range(NP_ // GN)] + [(NP_, NS)]
NG = len(GROUPS)
LASTP = NG - 2
SAMP = NG - 1
RUN_LAYERS = DEPTH
RUN_MLP = True
RUN_HG = True
RUN_GLA = True
RUN_SSD = True
RUN_S5 = True
RUN_SAMPLE = True
RUN_TILES = GN // 128
RUN_GROUPS = NG
RUN_CONV_OUT = True
DBG_STAGE = 99
INTERLEAVE = True
PRO_AHEAD = True
FIN_DEFER = True
SSD_STAGE = 99


class Prog:
    def __init__(self, nc, es):
        self.nc = nc
        self.es = es
        self.engs = ['pe', 'act', 'dve', 'pool', 'sp']
        self.ops = {e: [] for e in self.engs}
        self.count = {e: 0 for e in self.engs}
        self.sems = {e: [] for e in self.engs}
        self.waited = {e: {} for e in self.engs}
        self.pending = {e: [] for e in self.engs}
        self.lw = {}
        self.rd = {}
        self.dma_sems = []
        self.dma_issued = []
        self.dma_rr = 0
        self.sw_sem = {}
        self.n_dma_sems = 10
        for i in range(self.n_dma_sems):
            self.dma_sems.append(es.enter_context(nc.semaphore("dq%d" % i)))
            self.dma_issued.append(0)

    def _sem_for(self, eng, idx):
        ep = (idx - 1) // EPOCH
        while len(self.sems[eng]) <= ep:
            self.sems[eng].append(self.es.enter_context(self.nc.semaphore("s_%s_%d" % (eng, len(self.sems[eng])))))
        return self.sems[eng][ep], (idx - 1) % EPOCH + 1, ep

    def _wait_tok(self, eng, d, waits):
        if d[0] == 'dma':
            _, si, val = d
            key = ('dma', si)
            if self.waited[eng].get(key, 0) < val:
                self.waited[eng][key] = val
                waits.append((self.dma_sems[si], val))
        else:
            e2, idx = d
            if e2 == 'pe' and eng == 'pe':
                return
            sem, val, ep = self._sem_for(e2, idx)
            key = (e2, ep)
            if self.waited[eng].get(key, 0) < val:
                self.waited[eng][key] = val
                waits.append((sem, val))

    def _deps(self, eng, reads, writes):
        ps = [b for b in reads if b.startswith('pb') or b == 'ptb']
        if ps:
            writes = list(writes) + ps
        deps = []
        for b in reads:
            if b in self.lw:
                deps.append(self.lw[b])
        for b in writes:
            if b in self.lw:
                deps.append(self.lw[b])
            deps.extend(self.rd.get(b, []))
        waits = self.pending[eng]
        self.pending[eng] = []
        best = {}
        for d in deps:
            if d[0] == 'dma':
                k = ('dma', d[1])
                v = d[2]
            else:
                k = (d[0], (d[1] - 1) // EPOCH)
                v = d[1]
            if k not in best or best[k][0] < v:
                best[k] = (v, d)
        for k in best:
            self._wait_tok(eng, best[k][1], waits)
        return waits

    def _note(self, tok, reads, writes):
        ps = [b for b in reads if b.startswith('pb') or b == 'ptb']
        if ps:
            reads = [b for b in reads if b not in ps]
            writes = list(writes) + ps
        for b in reads:
            self.rd.setdefault(b, []).append(tok)
        for b in writes:
            self.lw[b] = tok
            self.rd[b] = []

    def op(self, eng, fn, reads=(), writes=()):
        waits = self._deps(eng, reads, writes)
        self.count[eng] += 1
        idx = self.count[eng]
        sem, val, ep = self._sem_for(eng, idx)
        self.ops[eng].append((waits, fn, sem, 1))
        tok = (eng, idx)
        self._note(tok, reads, writes)
        return tok

    def dma(self, eng, fn, reads=(), writes=()):
        waits = self._deps(eng, reads, writes)
        if eng == 'pool':
            self.dma_sems.append(self.es.enter_context(self.nc.semaphore("dsw%d" % len(self.dma_sems))))
            self.dma_issued.append(0)
            si = len(self.dma_sems) - 1
        else:
            si = self.dma_rr
            self.dma_rr = (self.dma_rr + 1) % self.n_dma_sems
        prev = self.dma_issued[si] * 16
        key = ('dma', si)
        if prev > 0 and self.waited[eng].get(key, 0) < prev:
            self.waited[eng][key] = prev
            waits.append((self.dma_sems[si], prev))
        self.dma_issued[si] += 1
        val = self.dma_issued[si] * 16
        self.ops[eng].append((waits, fn, self.dma_sems[si], 16))
        tok = ('dma', si, val)
        self._note(tok, reads, writes)
        return tok

    def barrier(self):
        for e in self.engs:
            for e2 in self.engs:
                if e2 != e and self.count[e2] > 0:
                    self._wait_tok(e, (e2, self.count[e2]), self.pending[e])
            for si in range(len(self.dma_sems)):
                if self.dma_issued[si] > 0:
                    self._wait_tok(e, ('dma', si, self.dma_issued[si] * 16), self.pending[e])

    def finish(self):
        self.barrier()
        for e in self.engs:
            self.ops[e].append((self.pending[e], None, None, 0))
            self.pending[e] = []

    def emit(self):
        nc = self.nc
        P = self
        with nc.Block() as block:
            def run(e, name):
                for waits, fn, sem, inc in P.ops[name]:
                    for (s, v) in waits:
                        e.wait_ge(s, v)
                    if fn is not None:
                        ins = fn(e)
                        if sem is not None:
                            ins.then_inc(sem, inc)

            @block.tensor
            def _(e):
                run(e, 'pe')

            @block.scalar
            def _(e):
                run(e, 'act')

            @block.vector
            def _(e):
                run(e, 'dve')

            @block.gpsimd
            def _(e):
                run(e, 'pool')

            @block.sync
            def _(e):
                run(e, 'sp')


def host_consts():
    j = np.arange(128)
    c = {}
    c['ident'] = np.eye(128, dtype=np.float32)
    c['ones'] = np.ones((128, 128), np.float32)
    c['tglob'] = (j[:, None] <= j[None, :]).astype(np.float32)
    c['tloc'] = ((j[:, None] <= j[None, :]) & ((j[:, None] // 32) == (j[None, :] // 32))).astype(np.float32)
    c['ltg'] = (j[:, None] > j[None, :]).astype(np.float32)
    c['tglob'] = (j[:, None] <= j[None, :]).astype(np.float32)
    ea = np.zeros((4, 128, 128), np.float32)
    for a in range(4):
        jp = j[:, None]
        jj = j[None, :]
        plus = (jj < 32 * a) & (jp > jj) & (jp < 32 * a)
        minus = (jj >= 32 * a) & (jj < 32 * (a + 1)) & (jp >= 32 * a) & (jp <= jj)
        ea[a] = plus.astype(np.float32) - minus.astype(np.float32)
    c['ea'] = ea
    c['ea0g'] = -c['tglob']
    return c


def build_program():
    nc = bass.Bass("TRN2", target_bir_lowering=False)
    din = lambda n, s: nc.dram_tensor(n, list(s), F32, kind="ExternalInput").ap()
    dout = lambda n, s: nc.dram_tensor(n, list(s), F32, kind="ExternalOutput").ap()
    xT_in = din("xT_in", [D, NT])
    w_in = din("w_in", [DEPTH, 128, 8, NIN])
    w_out = din("w_out", [DEPTH, 128, 8, D])
    w_up = din("w_up", [DEPTH, 128, 8, 4096])
    w_down = din("w_down", [DEPTH, 128, 32, D])
    g_mix = din("g_mix", [128, DEPTH, 8])
    g_mlp = din("g_mlp", [128, DEPTH, 8])
    g_fin = din("g_fin", [128, 8])
    hg_lb_logits = din("hg_lb_logits", [128, DEPTH, 256])
    hg_norm_g = din("hg_norm_g", [128, DEPTH, 256])
    gla_w_gk2 = din("gla_w_gk2", [16, DEPTH, 128])
    gla_b_gk = din("gla_b_gk", [128, DEPTH, 128])
    gla_norm_g = din("gla_norm_g", [128, DEPTH, 64])
    ssd_cw = din("ssd_cw", [128, DEPTH, 4, 4])
    ssd_cb = din("ssd_cb", [128, DEPTH, 4])
    ssd_rows = din("ssd_rows", [128, DEPTH, 3, 4])
    ssd_ng = din("ssd_ng", [128, DEPTH, 256])
    s5_lam = din("s5_lam", [128, DEPTH, 2, 8])
    s5_ldt = din("s5_ldt", [128, DEPTH, 8])
    s5_B = din("s5_B", [128, DEPTH, 2, 8, 16])
    s5_C = din("s5_C", [128, DEPTH, 2, 8, 16])
    s5_d = din("s5_d", [128, DEPTH, 2])
    s5_bglu = din("s5_bglu", [128, DEPTH, 2])
    s5_wglu = din("s5_wglu", [DEPTH, 128, 2, 256])
    s5_h0 = din("s5_h0", [DEPTH, 2, 128, 8, 16])
    ssd_cwr = din("ssd_cwr", [DEPTH, 4, NS, 512])
    ssd_cbr = din("ssd_cbr", [DEPTH, NS, 512])
    si_conv = din("si_conv", [DEPTH, NS, 3, 512])
    si_ssd = din("si_ssd", [DEPTH, NS, 4, 64, 64])
    si_hg = din("si_hg", [DEPTH, NS, 4, 64, 64])
    si_gla = din("si_gla", [DEPTH, NS, 4, 32, 64])
    c_iota = din("c_iota", [128, 128])
    c_bd = din("c_bd", [128, 128])
    c_ident = din("c_ident", [128, 128])
    c_ones = din("c_ones", [128, 128])
    c_tglob = din("c_tglob", [128, 128])
    c_tloc = din("c_tloc", [128, 128])
    c_ltg = din("c_ltg", [128, 128])
    c_maskT = din("c_maskT", [128, 128])
    c_ea = din("c_ea", [128, 4, 128])
    c_ea0g = din("c_ea0g", [128, 128])

    yT = dout("yT", [D, NT])
    o_conv_p = dout("o_conv_p", [DEPTH, 3, 512])
    o_hg_p = dout("o_hg_p", [DEPTH, 256, 64])
    o_s5_p = dout("o_s5_p", [DEPTH, 2, 128, 8])
    o_conv_s = dout("o_conv_s", [DEPTH, NS, 3, 512])
    o_ssd_s = dout("o_ssd_s", [DEPTH, NS, 4, 64, 64])
    o_hg_s = dout("o_hg_s", [DEPTH, NS, 4, 64, 64])
    o_gla_s = dout("o_gla_s", [DEPTH, NS, 4, 32, 64])
    o_s5_s = dout("o_s5_s", [DEPTH, 2, 128, 8, 16])
    o_ssd_p = dout("o_ssd_p", [DEPTH, 4, 64, 64])
    o_gla_p = dout("o_gla_p", [DEPTH, 128, 64])

    with ExitStack() as es:
        P = Prog(nc, es)
        sb = lambda n, s, d=F32: es.enter_context(nc.sbuf_tensor(n, list(s), d))
        xT = sb("xT", [128, 8, NT])
        W = sb("W", [128, 32768], BF16)
        gmix = sb("gmix", [128, DEPTH, 8])
        gmlp = sb("gmlp", [128, DEPTH, 8])
        gfin = sb("gfin", [128, 8])
        identb = sb("identb", [128, 128], BF16)
        identf = sb("identf", [128, 128])
        onesb = sb("onesb", [128, 128], BF16)
        onesf = sb("onesf", [128, 128])
        tglob = sb("tglob", [128, 128])
        tloc = sb("tloc", [128, 128])
        ltg = sb("ltg", [128, 128])
        eaM = sb("eaM", [128, 4, 128])
        lbl = sb("lbl", [128, DEPTH, 256])
        cwS = sb("cwS", [128, DEPTH, 4, 4])
        cbS = sb("cbS", [128, DEPTH, 4])
        srow = sb("srow", [128, DEPTH, 3, 4])
        arow = sb("arow", [128, DEPTH, 4])
        rstd = sb("rstd", [128, GN])
        iotaS = sb("iotaS", [128, 128])
        bdS = sb("bdS", [128, 128])
        uT = sb("uT", [128, 2, NT], BF16)
        esA = ExitStack()
        sa = lambda n, s, d=F32: esA.enter_context(nc.sbuf_tensor(n, list(s), d))
        hnT = sa("hnT", [128, 8, GN], BF16)
        oml = sa("oml", [128, 256])
        lbt = sa("lbt", [128, 256])
        hgng = sa("hgng", [128, 256])
        glang = sa("glang", [128, 64])
        glab = sa("glab", [128, 128])
        ssdng = sa("ssdng", [128, 256])
        w2 = sa("w2", [16, 128])
        ecw = sa("ecw", [128, 12])
        dtS = sa("dtS", [128, 8])
        xtok = sa("xtok", [128, 256])
        xbcT = sa("xbcT", [128, 4, GN + 3])
        pj = sa("pj", [128, 2068])
        mixtok = sa("mixtok", [128, 1024], BF16)
        mixT = sa("mixT", [128, 8, 128], BF16)
        sdD = sa("sdD", [128, 512])
        ex = [sdD[:, 0:256], sdD[:, 256:512]]
        sdR = sa("sdR", [128, 512])
        laS = sdR[:, 0:256]
        qS = sdR[:, 256:512]
        kS = sa("kS", [128, 256])
        vB = sa("vB", [128, 256], BF16)
        sgS = sa("sgS", [128, 256])
        qlocB = sa("qlocB", [128, 256], BF16)
        qgB = sa("qgB", [128, 256], BF16)
        kaB = [sa("kaB%d" % a, [128, 256], BF16) for a in range(4)]
        khB = sa("khB", [128, 256], BF16)
        qgT = sa("qgT", [128, 2, 128], BF16)
        kaT = sa("kaT", [128, 4, 2, 128], BF16)
        sT = sa("sT", [128, 512], BF16)
        oS = sa("oS", [128, 256])
        o2S = sa("o2S", [128, 256])
        ssq = sa("ssq", [128, 4])
        elast = sa("elast", [128, 2])
        lrT = sa("lrT", [16, 128])
        esP = ExitStack()
        sp_ = lambda n, s, d=F32: esP.enter_context(nc.sbuf_tensor(n, list(s), d))
        xsT = sp_("xsT", [128, 4, GN], BF16)
        Cz = sp_("Cz", [128, 2, GN], BF16)
        BhatZ = sp_("BhatZ", [128, 4, 128], BF16)
        btok = sp_("btok", [128, 128])
        st_ssd = sp_("st_ssd", [128, 2, 64])
        stb_ssd = sp_("stb_ssd", [128, 2, 64], BF16)
        hst = {'hg': sp_("hst_hg", [128, 2, 64]), 'gla': sp_("hst_gla", [128, 2, 64])}
        stz = {'hg': sp_("stz_hg", [128, 4, 64], BF16), 'gla': sp_("stz_gla", [128, 4, 64], BF16)}
        qlz = {'hg': sp_("qlz_hg", [128, 4, 128], BF16), 'gla': sp_("qlz_gla", [128, 4, 128], BF16)}
        g_ex = sp_("g_ex", [128, 256])
        g_lqk = sp_("g_lqk", [128, 3, 128])
        g_vB = sp_("g_vB", [128, 256], BF16)
        g_b3 = sp_("g_b3", [128, 3, 128], BF16)
        g_qgT = sp_("g_qgT", [128, 1, 128], BF16)
        g_kaT = sp_("g_kaT", [128, 1, 1, 128], BF16)
        g_sT = sp_("g_sT", [128, 512], BF16)
        g_oS = sp_("g_oS", [128, 256])
        g_o2S = sp_("g_o2S", [128, 256])
        g_ssq = sp_("g_ssq", [128, 4])
        g_elast = sp_("g_elast", [128, 2])
        esP.close()
        esQ = ExitStack()
        sq_ = lambda n, s, d=F32: esQ.enter_context(nc.sbuf_tensor(n, list(s), d))
        dgT = sq_("dgT", [16, 512])
        smpS = sq_("smpS", [128, NS, 64])
        esQ.close()
        esA.close()
        esB = ExitStack()
        sbb = lambda n, s, d=F32: esB.enter_context(nc.sbuf_tensor(n, list(s), d))
        hn2T = sbb("hn2T", [128, 8, NT], BF16)
        hT = sbb("hT", [128, 8, GN], BF16)
        hT2 = sbb("hT2", [128, 8, GN], BF16)
        esB.close()
        esS = ExitStack()
        ss = lambda n, s, d=F32: esS.enter_context(nc.sbuf_tensor(n, list(s), d))
        lamS = ss("lamS", [128, 2, 8])
        t8 = ss("t8", [128, 12, 8])
        tab = ss("tab", [128, 8, 17, 8])
        Bq = ss("Bq", [128, 2, 8, 16])
        Cq = ss("Cq", [128, 2, 8, 16])
        Bbar = ss("Bbar", [128, 2, 8, 16])
        t16 = ss("t16", [128, 4, 8, 16])
        Xexp = ss("Xexp", [128, 4, 2, 8, 32], BF16)
        XZ = ss("XZ", [128, 4, 128], BF16)
        ZA = ss("ZA", [128, 2, 8, 128])
        ZB = ss("ZB", [128, 2, 8, 128])
        c2S = ss("c2S", [128, 8, 128])
        s2S = ss("s2S", [128, 8, 128])
        Sb = ss("Sb", [128, 2, 8, 129], BF16)
        zT = ss("zT", [128, 2, NT], BF16)
        YZb = ss("YZb", [128, 2, 4, 128], BF16)
        wgB = ss("wgB", [128, 2, 256], BF16)
        h0S = ss("h0S", [128, 2, 8, 16])
        h1S = ss("h1S", [128, 2, 8, 16])
        h1b = ss("h1b", [128, 2, 8, 16], BF16)
        d5 = ss("d5", [128, 2])
        bg5 = ss("bg5", [128, 2])
        esS.close()
        tmpX = zT[:, 0, 0:2048].bitcast(F32).rearrange("p (a n) -> p a n", a=8)
        ytmp = ZA[:, 0, 0:4, :].rearrange("p a n -> p (a n)")
        gtmp = ZA[:, 0, 4:8, :].rearrange("p a n -> p (a n)")
        wgS = c2S[:, 0:4, :].rearrange("p a n -> p (a n)").rearrange("p (k n) -> p k n", k=2)

        pb = [es.enter_context(nc.psum_tensor("pb%d" % i, [128, 512], F32)) for i in range(7)]
        ptb = es.enter_context(nc.psum_tensor("ptb", [128, 1024], BF16))

        rr = {'a': 0}

        def nxt(lo, n):
            rr['a'] += 1
            return lo + (rr['a'] % n)

        def mm(out, lhsT, rhs, start, stop, r, w):
            P.op('pe', lambda e: e.matmul(out, lhsT=lhsT, rhs=rhs, start=start, stop=stop), reads=r, writes=w)

        def act(out, in_, func, r, w, bias=None, scale=None, accum_out=None):
            kw = {}
            if bias is not None:
                kw['bias'] = bias
            if scale is not None:
                kw['scale'] = scale
            if accum_out is not None:
                kw['accum_out'] = accum_out
            P.op('act', lambda e: e.activation(out=out, in_=in_, func=func, **kw), reads=r, writes=w)

        def tt(out, in0, in1, op, r, w, eng='dve'):
            P.op(eng, lambda e: e.tensor_tensor(out=out, in0=in0, in1=in1, op=op), reads=r, writes=w)

        def ts(out, in0, s1, s2, op0, op1, r, w, eng='dve'):
            if op1 is None:
                P.op(eng, lambda e: e.tensor_scalar(out=out, in0=in0, scalar1=s1, scalar2=None, op0=op0), reads=r, writes=w)
            else:
                P.op(eng, lambda e: e.tensor_scalar(out=out, in0=in0, scalar1=s1, scalar2=s2, op0=op0, op1=op1), reads=r, writes=w)

        def stt(out, in0, scalar, in1, op0, op1, r, w, eng='dve'):
            P.op(eng, lambda e: e.scalar_tensor_tensor(out=out, in0=in0, scalar=scalar, in1=in1, op0=op0, op1=op1),
                 reads=r, writes=w)

        def cp(out, in_, r, w, eng='dve'):
            if eng == 'act':
                act(out, in_, AF.Copy, r, w)
            else:
                P.op(eng, lambda e: e.tensor_copy(out=out, in_=in_), reads=r, writes=w)

        def dma(eng, out, in_, r, w):
            P.dma(eng, lambda e: e.dma_start(out=out, in_=in_), reads=r, writes=w)

        def dma_nc(eng, out, in_, r, w):
            P.dma(eng, lambda e: e.dma_start(out=out, in_=in_, allow_slow_non_contiguous=True), reads=r, writes=w)

        for c in range(8):
            dma('sp', xT[:, c, :], xT_in[c * 128:(c + 1) * 128, :], [], ['x%d_%d' % (c, g) for g in range(NG)])
        dma('sp', gmix[:], g_mix, [], ['gmix'])
        dma('sp', gmlp[:], g_mlp, [], ['gmlp'])
        dma('sp', gfin[:], g_fin, [], ['gfin'])
        dma('pool', identb[:], c_ident, [], ['identb'])
        dma('pool', onesb[:], c_ones, [], ['onesb'])
        dma('sp', identf[:], c_ident, [], ['identf'])
        dma('sp', onesf[:], c_ones, [], ['onesf'])
        dma('sp', tglob[:], c_tglob, [], ['tglob'])
        dma('sp', iotaS[:], c_iota, [], ['iotaS'])
        dma('sp', bdS[:], c_bd, [], ['bdS'])
        dma('sp', tloc[:], c_tloc, [], ['tloc'])
        dma('sp', ltg[:], c_ltg, [], ['ltg'])
        dma('sp', eaM[:], c_ea, [], ['eaM'])
        dma('sp', lbl[:], hg_lb_logits, [], ['lbl'])
        dma('sp', cwS[:], ssd_cw, [], ['cwS'])
        dma('sp', cbS[:], ssd_cb, [], ['cbS'])
        dma('sp', srow[:], ssd_rows, [], ['srow'])
        act(arow[:], srow[:, :, 1, :], AF.Exp, ['srow'], ['arow'])
        ts(arow[:], arow[:], -1.0, None, ALU.mult, None, ['arow'], ['arow'])
        act(lbl[:], lbl[:], AF.Exp, ['lbl'], ['lbl'])
        tt(lbt[:], lbl[:, 0, :], lbl[:, 1, :], ALU.add, ['lbl'], ['lbt'])
        tt(lbt[:], lbt[:], lbl[:, 2, :], ALU.add, ['lbl', 'lbt'], ['lbt'])
        tt(lbt[:], lbt[:], lbl[:, 3, :], ALU.add, ['lbl', 'lbt'], ['lbt'])
        P.op('dve', lambda e: e.reciprocal(out=lbt[:], in_=lbt[:]), reads=['lbt'], writes=['lbt'])
        for l in range(DEPTH):
            tt(lbl[:, l, :], lbl[:, l, :], lbt[:], ALU.mult, ['lbl', 'lbt'], ['lbl'])
        tt(lbl[:, 3, :], lbl[:, 3, :], lbl[:, 2, :], ALU.add, ['lbl'], ['lbl'])
        tt(lbl[:, 3, :], lbl[:, 3, :], lbl[:, 1, :], ALU.add, ['lbl'], ['lbl'])
        tt(lbl[:, 2, :], lbl[:, 2, :], lbl[:, 1, :], ALU.add, ['lbl'], ['lbl'])
        P.op('dve', lambda e: e.memset(lbl[:, 0, :], 0.0), reads=['lbl'], writes=['lbl'])

        WIN_OFF = 0
        WOUT_OFF = 8 * NIN
        win_v = W[:, WIN_OFF:WIN_OFF + 8 * NIN].rearrange("p (c n) -> p c n", c=8)
        wout_v = W[:, WOUT_OFF:WOUT_OFF + 8 * D].rearrange("p (c n) -> p c n", c=8)
        IN_CHUNKS = [(0, 1024), (1024, 2048), (2048, NIN)]

        def load_phaseA_weights(l):
            for i, (a, b) in enumerate(IN_CHUNKS):
                dma('pool', win_v[:, :, a:b], w_in[l, :, :, a:b], [], ['win%d' % i])
            dma('pool', wout_v[:, :, :], w_out[l, :, :, :], [], ['wout'])

        def mlp_views(b):
            off = (b % 2) * 16384
            up = W[:, off:off + 8192].rearrange("p (c n) -> p c n", c=8)
            dn = W[:, off + 8192:off + 16384].rearrange("p (c n) -> p c n", c=8)
            return up, dn

        def load_mlp_block(l, b):
            up, dn = mlp_views(b)
            h = b % 2
            dma('pool', up[:, :, :], w_up[l, :, :, b * 1024:(b + 1) * 1024], [], ['mup%d' % h])
            dma('pool', dn[:, :, :], w_down[l, :, b * 8:(b + 1) * 8, :], [], ['mdn%d' % h])

        def rmsnorm_group(g, gain_ap_fn, gkey, out_T, okey, oc0):
            t0, n = GROUPS[g]
            for c in range(8):
                act(out_T[:, c, oc0:oc0 + n], xT[:, c, t0:t0 + n], AF.Square, ['x%d_%d' % (c, g)], [okey + '%d' % c])
            bk = nxt(0, 2)
            for c in range(8):
                mm(pb[bk][:, :n], onesb[:], out_T[:, c, oc0:oc0 + n], c == 0, c == 7, ['onesb', okey + '%d' % c], ['pb%d' % bk])
            act(rstd[:, :n], pb[bk][:, :n], AF.Sqrt, ['pb%d' % bk], ['rstd'], bias=EPS, scale=1.0 / D)
            P.op('dve', lambda e: e.reciprocal(out=rstd[:, :n], in_=rstd[:, :n]), reads=['rstd'], writes=['rstd'])
            for c in range(8):
                stt(out_T[:, c, oc0:oc0 + n], xT[:, c, t0:t0 + n], gain_ap_fn(c), rstd[:, :n], ALU.mult, ALU.mult,
                    ['x%d_%d' % (c, g), gkey, 'rstd'], [okey + '%d' % c])

        def win_key(a, b):
            ks = []
            for i, (ca, cb) in enumerate(IN_CHUNKS):
                if a < cb and b > ca:
                    ks.append('win%d' % i)
            return ks

        TOK_GROUPS = [(256, 512, 0), (1024, 1536, 256), (1536, 2048, 768), (2048, 2560, 1280), (2560, NIN, 1792)]

        class BS:
            pass

        B_HG = BS()
        B_HG.pfx = ''
        B_HG.ex = ex
        B_HG.qgB, B_HG.qlocB, B_HG.kaB, B_HG.khB = qgB, qlocB, kaB, khB
        B_HG.qgT, B_HG.kaT, B_HG.sT = qgT, kaT, sT
        B_HG.oS, B_HG.o2S, B_HG.ssq, B_HG.elast = oS, o2S, ssq, elast
        B_HG.banks = [4, 5]
        B_HG.obank = 6
        B_HG.vkey = 'vB'
        B_GL = BS()
        B_GL.pfx = 'g_'
        B_GL.ex = [g_ex[:, 0:128], g_ex[:, 128:256]]
        B_GL.qgB, B_GL.qlocB, B_GL.kaB, B_GL.khB = g_b3[:, 0, :], None, [g_b3[:, 1, :]], g_b3[:, 2, :]
        B_GL.qgT, B_GL.kaT, B_GL.sT = g_qgT, g_kaT, g_sT
        B_GL.oS, B_GL.o2S, B_GL.ssq, B_GL.elast = g_oS, g_o2S, g_ssq, g_elast
        B_GL.banks = [2]
        B_GL.obank = 3
        B_GL.vkey = 'g_vB'
        brr = {'n': 0}

        def brot(B):
            brr['n'] += 1
            return B.banks[brr['n'] % len(B.banks)]

        def gla_like(B, name, l, H, K, FT, q_ap, k_ap, v_ap, la_ap, escale, nsub, srcs):
            HK = H * K
            NF = HK // FT
            st = hst[name]
            px = B.pfx
            exk = [px + 'ex0', px + 'ex1']
            okey = 'pb%d' % B.obank

            def cum_exp(mat_ap, mkey, ei, neg=False):
                bk = brot(B)
                mm(pb[bk][:, :HK], mat_ap, la_ap, True, True, [mkey] + srcs, ['pb%d' % bk])
                act(B.ex[ei][:, :HK], pb[bk][:, :HK], AF.Exp, ['pb%d' % bk], [exk[ei]], scale=(-escale if neg else escale))

            cum_exp(tglob[:], 'tglob', 0)
            tt(B.qgB[:, :HK], q_ap, B.ex[0][:, :HK], ALU.mult, [exk[0]] + srcs, [px + 'qgB'])
            yield
            if nsub > 1:
                cum_exp(tloc[:], 'tloc', 1)
                tt(B.qlocB[:, :HK], q_ap, B.ex[1][:, :HK], ALU.mult, [exk[1]] + srcs, [px + 'qlocB'])
                yield
            for a in range(nsub):
                ei = a % 2
                if nsub > 1:
                    cum_exp(eaM[:, a, :], 'eaM', ei)
                else:
                    cum_exp(tglob[:], 'tglob', ei, neg=True)
                tt(B.kaB[a][:, :HK], k_ap, B.ex[ei][:, :HK], ALU.mult, [exk[ei]] + srcs, [px + 'kaB%d' % a])
                yield
            ei = nsub % 2
            cum_exp(ltg[:], 'ltg', ei)
            tt(B.khB[:, :HK], k_ap, B.ex[ei][:, :HK], ALU.mult, [exk[ei]] + srcs, [px + 'khB'])
            bk = brot(B)
            for f in range(NF):
                mm(pb[bk][:FT, f:f + 1], la_ap[:, f * FT:(f + 1) * FT], onesf[:, 0:1], True, True, srcs + ['onesf'], ['pb%d' % bk])
            act(B.elast[:FT, :NF], pb[bk][:FT, :NF], AF.Exp, ['pb%d' % bk], [px + 'elast'], scale=escale)
            yield
            qz = qlz[name]
            slot = 0
            tl = []
            items = [(B.qgB, px + 'qgB', 'qg', None)]
            if nsub > 1:
                items.append((B.qlocB, px + 'qlocB', 'ql', None))
            for a in range(nsub):
                items.append((B.kaB[a], px + 'kaB%d' % a, 'ka', a))

            def evac(s_, kind, f, a):
                src = ptb[:FT, s_ * 128:(s_ + 1) * 128]
                if kind == 'ka':
                    cp(B.kaT[:FT, a, f, :], src, ['ptb'], [px + 'kaT'], eng='dve')
                    return
                if kind == 'qg':
                    cp(B.qgT[:FT, f, :], src, ['ptb'], [px + 'qgT'], eng='dve')
                if kind == 'ql' or (kind == 'qg' and nsub == 1):
                    for h in range(H):
                        if (h * K) // FT != f:
                            continue
                        r0 = (h * K) % FT
                        cp(qz[r0:r0 + K, h, :], ptb[r0:r0 + K, s_ * 128:(s_ + 1) * 128], ['ptb'], ['qlz_' + name],
                           eng='dve')

            for (src, skey, kind, a) in items:
                for f in range(NF):
                    P.op('pe', lambda e, s_=slot, src=src, f=f: e.transpose(ptb[:FT, s_ * 128:(s_ + 1) * 128], src[:, f * FT:(f + 1) * FT], identb[:]),
                         reads=[skey, 'identb'], writes=['ptb'])
                    tl.append((slot, kind, f, a))
                    slot += 1
                    if slot == 8:
                        yield
                        for ii, it in enumerate(tl):
                            evac(*it)
                            if ii % 3 == 2:
                                yield
                        tl = []
                        slot = 0
                        yield
            if tl:
                yield
            for ii, it in enumerate(tl):
                evac(*it)
                if ii % 3 == 2:
                    yield
            yield
            sci = brot(B)
            sc = pb[sci]
            sck = 'pb%d' % sci
            CS = 128 // nsub
            for h in range(H):
                f = (h * K) // FT
                for a in range(nsub):
                    mm(sc[:, h * 128 + a * CS: h * 128 + (a + 1) * CS], B.kaT[:FT, a, f, :], qz[:FT, h, a * CS:(a + 1) * CS],
                       True, True, [px + 'kaT', 'qlz_' + name], [sck])
            tt(B.sT[:, :H * 128].rearrange("p (h i) -> p h i", h=H), sc[:, :H * 128].rearrange("p (h i) -> p h i", h=H),
               tglob[:].unsqueeze(1).to_broadcast([128, H, 128]), ALU.mult, [sck, 'tglob'], [px + 'sT'])
            yield
            sz = stz[name]
            for h in range(H):
                f = (h * K) // FT
                mm(pb[B.obank][:, h * 64:(h + 1) * 64], B.sT[:, h * 128:(h + 1) * 128], v_ap[:, h * 64:(h + 1) * 64], True, False,
                   [px + 'sT', B.vkey], [okey])
                mm(pb[B.obank][:, h * 64:(h + 1) * 64], B.qgT[:FT, f, :], sz[:FT, h, :], False, True,
                   [px + 'qgT', 'stz_' + name], [okey])
            yield
            ubi = brot(B)
            ub = pb[ubi]
            ubk = 'pb%d' % ubi
            for h in range(H):
                f = (h * K) // FT
                mm(ub[:FT, h * 64:(h + 1) * 64], B.khB[:, f * FT:(f + 1) * FT], v_ap[:, h * 64:(h + 1) * 64], True, True,
                   [px + 'khB', B.vkey], [ubk])
            yield
            for h in range(H):
                f = (h * K) // FT
                r0 = (h * K) % FT
                stt(st[r0:r0 + K, f, :], st[r0:r0 + K, f, :], B.elast[r0:r0 + K, f:f + 1], ub[r0:r0 + K, h * 64:(h + 1) * 64],
                    ALU.mult, ALU.add, ['hst_' + name, px + 'elast', ubk], ['hst_' + name])
            for h in range(H):
                f = (h * K) // FT
                r0 = (h * K) % FT
                cp(sz[r0:r0 + K, h, :], st[r0:r0 + K, f, :], ['hst_' + name], ['stz_' + name], eng='dve')
            yield

        def ssd_tile(l, c0):
            tsl = slice(c0, c0 + 128)
            for cc in range(3):
                P.op('pe', lambda e, cc=cc: e.transpose(ptb[:, cc * 128:(cc + 1) * 128], xsT[:, cc, tsl], identb[:]),
                     reads=['xsT', 'identb'], writes=['ptb'])
            cp(xtok[:], ptb[:, 0:256], ['ptb'], ['xtok'], eng='act')
            cp(btok[:], ptb[:, 256:384], ['ptb'], ['btok'], eng='act')
            tt(dtS[:, 0:4], pj[:, 256:260], srow[:, l, 0, :], ALU.add, ['pj', 'srow'], ['dtS'])
            act(dtS[:, 0:4], dtS[:, 0:4], AF.Exp, ['dtS'], ['dtS'])
            act(dtS[:, 0:4], dtS[:, 0:4], AF.Ln, ['dtS'], ['dtS'], bias=1.0)
            tt(dtS[:, 4:8], dtS[:, 0:4], arow[:, l, :], ALU.mult, ['dtS', 'arow'], ['dtS'])
            la = dtS[:, 4:8]
            yield
            bk = nxt(4, 2)
            mm(pb[bk][:, 0:4], tglob[:], la, True, True, ['tglob', 'dtS'], ['pb%d' % bk])
            mm(pb[bk][:, 4:8], ltg[:], la, True, True, ['ltg', 'dtS'], ['pb%d' % bk])
            mm(pb[bk][:, 8:12], onesf[:], la, True, True, ['onesf', 'dtS'], ['pb%d' % bk])
            act(ecw[:], pb[bk][:, 0:12], AF.Exp, ['pb%d' % bk], ['ecw'])
            yield
            tt(sdR[:].rearrange("p (h i) -> p h i", h=4), tglob[:].unsqueeze(1).to_broadcast([128, 4, 128]),
               la.unsqueeze(2).to_broadcast([128, 4, 128]), ALU.mult, ['tglob', 'dtS'], ['laS', 'qS'])
            bk = nxt(4, 2)
            mm(pb[bk][:, :512], ltg[:], sdR[:], True, True, ['ltg', 'laS', 'qS'], ['pb%d' % bk])
            act(sdD[:], pb[bk][:, :512], AF.Exp, ['pb%d' % bk], ['ex0', 'ex1'])
            tt(sdD[:].rearrange("p (h i) -> p h i", h=4), sdD[:].rearrange("p (h i) -> p h i", h=4),
               tglob[:].unsqueeze(1).to_broadcast([128, 4, 128]), ALU.mult, ['ex0', 'ex1', 'tglob'], ['ex0', 'ex1'])
            yield
            sci = nxt(4, 2)
            for gg in range(2):
                mm(pb[sci][:, gg * 128:(gg + 1) * 128], xsT[:, 2, tsl], Cz[:, gg, tsl], True, True, ['xsT', 'Cz'], ['pb%d' % sci])
            for gg in range(2):
                tt(sT[:, gg * 256:(gg + 1) * 256].rearrange("p (r i) -> p r i", r=2),
                   pb[sci][:, gg * 128:(gg + 1) * 128].unsqueeze(1).to_broadcast([128, 2, 128]),
                   sdD[:, gg * 256:(gg + 1) * 256].rearrange("p (r i) -> p r i", r=2), ALU.mult, ['pb%d' % sci, 'ex0', 'ex1'], ['sT'])
            yield
            tt(vB[:].rearrange("p (h v) -> p h v", h=4), xtok[:].rearrange("p (h v) -> p h v", h=4),
               dtS[:, 0:4].unsqueeze(2).to_broadcast([128, 4, 64]), ALU.mult, ['xtok', 'dtS'], ['vB'])
            for h in range(4):
                gg = h // 2
                ts(BhatZ[:, h, 64 * gg:64 * gg + 64], btok[:, 64 * gg:64 * gg + 64], ecw[:, 4 + h:5 + h], None, ALU.mult, None,
                   ['btok', 'ecw'], ['BhatZ'])
            yield
            for h in range(4):
                mm(pb[6][:, h * 64:(h + 1) * 64], sT[:, h * 128:(h + 1) * 128], vB[:, h * 64:(h + 1) * 64], True, True,
                   ['sT', 'vB'], ['pb6'])
            p2 = nxt(4, 2)
            for h in range(4):
                mm(pb[p2][:, h * 64:(h + 1) * 64], Cz[:, h // 2, tsl], stb_ssd[:, h % 2, :], True, True, ['Cz', 'stb_ssd'], ['pb%d' % p2])
            ubi = nxt(4, 2)
            for r in range(2):
                for gg in range(2):
                    h = 2 * gg + r
                    mm(pb[ubi][:, r * 64:(r + 1) * 64], BhatZ[:, h, :], vB[:, h * 64:(h + 1) * 64], gg == 0, gg == 1,
                       ['BhatZ', 'vB'], ['pb%d' % ubi])
            yield 'front_done'
            yield from ssd_tail(l, p2, ubi)

        def ssd_tail(l, p2, ubi):
            tt(oS[:].rearrange("p (h v) -> p h v", h=4), pb[p2][:, :256].rearrange("p (h v) -> p h v", h=4),
               ecw[:, 0:4].unsqueeze(2).to_broadcast([128, 4, 64]), ALU.mult, ['pb%d' % p2, 'ecw'], ['oS'])
            for r in range(2):
                for gg in range(2):
                    h = 2 * gg + r
                    ps_ = slice(64 * gg, 64 * gg + 64)
                    stt(st_ssd[ps_, r, :], st_ssd[ps_, r, :], ecw[ps_, 8 + h:9 + h], pb[ubi][ps_, r * 64:(r + 1) * 64], ALU.mult, ALU.add,
                        ['st_ssd', 'ecw', 'pb%d' % ubi], ['st_ssd'])
            yield
            tt(oS[:], oS[:], pb[6][:, :256], ALU.add, ['oS', 'pb6'], ['oS'])
            cp(stb_ssd[:], st_ssd[:], ['st_ssd'], ['stb_ssd'], eng='act')
            tt(o2S[:].rearrange("p (h v) -> p h v", h=4), xtok[:].rearrange("p (h v) -> p h v", h=4),
               srow[:, l, 2, :].unsqueeze(2).to_broadcast([128, 4, 64]), ALU.mult, ['xtok', 'srow'], ['o2S'])
            tt(oS[:], oS[:], o2S[:], ALU.add, ['oS', 'o2S'], ['oS'])
            yield
            act(o2S[:], pj[:, 0:256], AF.Silu, ['pj'], ['o2S'])
            tt(oS[:], oS[:], o2S[:], ALU.mult, ['oS', 'o2S'], ['oS'])
            yield
            tt(o2S[:], oS[:], oS[:], ALU.mult, ['oS'], ['o2S'])
            P.op('dve', lambda e: e.tensor_reduce(out=ssq[:, 0:1], in_=o2S[:], axis=AX.X, op=ALU.add), reads=['o2S'], writes=['ssq'])
            yield
            act(ssq[:, 0:1], ssq[:, 0:1], AF.Sqrt, ['ssq'], ['ssq'], bias=EPS, scale=1.0 / 256)
            P.op('dve', lambda e: e.reciprocal(out=ssq[:, 0:1], in_=ssq[:, 0:1]), reads=['ssq'], writes=['ssq'])
            yield
            stt(mixtok[:, 256:512], oS[:], ssq[:, 0:1], ssdng[:], ALU.mult, ALU.mult, ['oS', 'ssq', 'ssdng'], ['mixtok'])
            yield

        def head_norm_gate_gen(l, gvec_ap, gkey, gate_ap, dst_ap, npart=128, from_psum=True, B=None):
            B = B or B_HG
            px = B.pfx
            oS_, o2S_, ssq_ = B.oS, B.o2S, B.ssq
            ko, ko2, ks = px + 'oS', px + 'o2S', px + 'ssq'
            if from_psum:
                cp(oS_[:npart], pb[B.obank][:npart, :256], ['pb%d' % B.obank], [ko], eng='act')
            tt(o2S_[:npart], oS_[:npart], oS_[:npart], ALU.mult, [ko], [ko2])
            P.op('dve', lambda e: e.tensor_reduce(out=ssq_[:npart], in_=o2S_[:npart].rearrange("p (h v) -> p h v", h=4), axis=AX.X, op=ALU.add),
                 reads=[ko2], writes=[ks])
            yield
            act(ssq_[:npart], ssq_[:npart], AF.Sqrt, [ks], [ks], bias=EPS, scale=1.0 / 64)
            P.op('dve', lambda e: e.reciprocal(out=ssq_[:npart], in_=ssq_[:npart]), reads=[ks], writes=[ks])
            yield
            tt(oS_[:npart].rearrange("p (h v) -> p h v", h=4), oS_[:npart].rearrange("p (h v) -> p h v", h=4),
               ssq_[:npart].unsqueeze(2).to_broadcast([npart, 4, 64]), ALU.mult, [ko, ks], [ko])
            tt(oS_[:npart].rearrange("p (h v) -> p h v", h=4), oS_[:npart].rearrange("p (h v) -> p h v", h=4), gvec_ap, ALU.mult, [ko, gkey], [ko])
            yield
            act(o2S_[:npart], gate_ap, AF.Silu, ['pj'], [ko2])
            tt(dst_ap, oS_[:npart], o2S_[:npart], ALU.mult, [ko, ko2], ['mixtok'])

        def head_norm_gate(*a_, **k_):
            for _ in head_norm_gate_gen(*a_, **k_):
                pass

        Yexp = W[:, 0:8704].rearrange("p (m r q c) -> p m r q c", m=17, r=2, q=8)
        Kblk = W[:, 8704:12800].rearrange("p (m h c) -> p m h c", m=16, h=2)
        WinZ = W[:, 12800:20992].rearrange("p (m r q c) -> p m r q c", m=4, r=2, q=8)
        MAGIC = 12582912.0
        TWO_PI = 6.283185307179586

        def sin_of(ang_ap, out_ap, tmp_ap, tmp2_ap, shift, rk, wk):
            ts(tmp_ap, ang_ap, 1.0 / TWO_PI, shift, ALU.mult, ALU.add, rk, wk)
            ts(tmp2_ap, tmp_ap, MAGIC, None, ALU.add, None, wk, wk)
            ts(tmp2_ap, tmp2_ap, -MAGIC, None, ALU.add, None, wk, wk)
            tt(tmp_ap, tmp_ap, tmp2_ap, ALU.subtract, wk, wk)
            ts(tmp_ap, tmp_ap, 0.4999995, -0.4999995, ALU.min, ALU.max, wk, wk)
            act(out_ap, tmp_ap, AF.Sin, wk, wk, scale=TWO_PI)

        def s5_phase(l):
            TB = lambda k: tab[:, k, :, :]
            kk = ['s5t']
            dma('sp', lamS[:], s5_lam[:, l, :, :], [], ['lamS'])
            dma('sp', t8[:, 0, :], s5_ldt[:, l, :], [], kk)
            dma('sp', Bq[:], s5_B[:, l], [], ['Bq'])
            dma('sp', Cq[:], s5_C[:, l], [], ['Cq'])
            dma('sp', d5[:], s5_d[:, l, :], [], ['d5'])
            dma('sp', bg5[:], s5_bglu[:, l, :], [], ['bg5'])
            for ri in range(2):
                dma('sp', h0S[:, ri], s5_h0[l, ri], [], ['h0S'])
            P.op('pool', lambda e: e.memset(W[:, 0:8704], 0.0), writes=['Yexp'])
            P.op('pool', lambda e: e.memset(Xexp[:], 0.0), writes=['Xexp'])
            P.op('pool', lambda e: e.memset(XZ[:], 0.0), writes=['XZ'])
            P.op('pool', lambda e: e.memset(YZb[:], 0.0), writes=['YZb0', 'YZb1'])
            act(t8[:, 0, :], t8[:, 0, :], AF.Exp, kk, kk)
            tt(t8[:, 1, :], lamS[:, 0, :], t8[:, 0, :], ALU.mult, kk + ['lamS'], kk)
            tt(t8[:, 2, :], lamS[:, 1, :], t8[:, 0, :], ALU.mult, kk + ['lamS'], kk)
            iv = iotaS[:, 0:17].unsqueeze(2).to_broadcast([128, 17, 8])
            tt(TB(0), iv, t8[:, 1, :].unsqueeze(1).to_broadcast([128, 17, 8]), ALU.mult, kk + ['iotaS'], kk)
            act(TB(0), TB(0), AF.Exp, kk, kk)
            tt(TB(1), iv, t8[:, 2, :].unsqueeze(1).to_broadcast([128, 17, 8]), ALU.mult, kk + ['iotaS'], kk)
            sin_of(TB(1), TB(3), TB(2), TB(5), 0.0, kk, kk)
            sin_of(TB(1), TB(4), TB(2), TB(5), 0.25, kk, kk)
            tt(TB(5), TB(0), TB(4), ALU.mult, kk, kk)
            tt(TB(6), TB(0), TB(3), ALU.mult, kk, kk)
            ts(TB(7), TB(6), -1.0, None, ALU.mult, None, kk, kk)
            lr1, li1 = tab[:, 5, 1, :], tab[:, 6, 1, :]
            ts(t8[:, 3, :], lr1, -1.0, None, ALU.add, None, kk, kk)
            tt(t8[:, 4, :], lamS[:, 0, :], lamS[:, 0, :], ALU.mult, ['lamS'], kk)
            tt(t8[:, 5, :], lamS[:, 1, :], lamS[:, 1, :], ALU.mult, ['lamS'], kk)
            tt(t8[:, 4, :], t8[:, 4, :], t8[:, 5, :], ALU.add, kk, kk)
            P.op('dve', lambda e: e.reciprocal(out=t8[:, 4, :], in_=t8[:, 4, :]), reads=kk, writes=kk)
            tt(t8[:, 5, :], t8[:, 3, :], lamS[:, 0, :], ALU.mult, kk + ['lamS'], kk)
            tt(t8[:, 6, :], li1, lamS[:, 1, :], ALU.mult, kk + ['lamS'], kk)
            tt(t8[:, 5, :], t8[:, 5, :], t8[:, 6, :], ALU.add, kk, kk)
            tt(t8[:, 5, :], t8[:, 5, :], t8[:, 4, :], ALU.mult, kk, kk)
            tt(t8[:, 6, :], li1, lamS[:, 0, :], ALU.mult, kk + ['lamS'], kk)
            tt(t8[:, 7, :], t8[:, 3, :], lamS[:, 1, :], ALU.mult, kk + ['lamS'], kk)
            tt(t8[:, 6, :], t8[:, 6, :], t8[:, 7, :], ALU.subtract, kk, kk)
            tt(t8[:, 6, :], t8[:, 6, :], t8[:, 4, :], ALU.mult, kk, kk)
            kr = t8[:, 5, :].unsqueeze(2).to_broadcast([128, 8, 16])
            ki = t8[:, 6, :].unsqueeze(2).to_broadcast([128, 8, 16])
            tk = ['t16']
            tt(t16[:, 0], Bq[:, 0], kr, ALU.mult, ['Bq'] + kk, tk)
            tt(t16[:, 1], Bq[:, 1], ki, ALU.mult, ['Bq'] + kk, tk)
            tt(Bbar[:, 0], t16[:, 0], t16[:, 1], ALU.subtract, tk, ['Bbar'])
            tt(t16[:, 0], Bq[:, 1], kr, ALU.mult, ['Bq'] + kk, tk)
            tt(t16[:, 1], Bq[:, 0], ki, ALU.mult, ['Bq'] + kk, tk)
            tt(Bbar[:, 1], t16[:, 0], t16[:, 1], ALU.add, tk, ['Bbar'])
            zk = ['ZA', 'ZB']

            def cexp(dst, nm, m0, m1, A, Ak, conj_second):
                nb = m1 - m0
                pa = ZA[:].rearrange("p a b c -> p (a b c)")[:, 0:nb * 128].rearrange("p (m q c) -> p m q c", m=nb, q=8)
                pb_ = ZB[:].rearrange("p a b c -> p (a b c)")[:, 0:nb * 128].rearrange("p (m q c) -> p m q c", m=nb, q=8)
                A0 = A[:, 0].unsqueeze(1).to_broadcast([128, nb, 8, 16])
                A1 = A[:, 1].unsqueeze(1).to_broadcast([128, nb, 8, 16])
                lrm = tab[:, 5, m0:m1, :].unsqueeze(3).to_broadcast([128, nb, 8, 16])
                lim = tab[:, 6, m0:m1, :].unsqueeze(3).to_broadcast([128, nb, 8, 16])
                nlim = tab[:, 7, m0:m1, :].unsqueeze(3).to_broadcast([128, nb, 8, 16])
                tt(pa, A0, lrm, ALU.mult, [Ak] + kk, zk)
                tt(pb_, A1, lim, ALU.mult, [Ak] + kk, zk)
                for hf in range(2):
                    ps_ = slice(64 * hf, 64 * hf + 64)
                    tt(dst[ps_, m0:m1, 0, :, 16 * hf:16 * hf + 16], pa[ps_], pb_[ps_], ALU.subtract, zk, [nm])
                if conj_second:
                    tt(pa, A0, nlim, ALU.mult, [Ak] + kk, zk)
                    tt(pb_, A1, lrm, ALU.mult, [Ak] + kk, zk)
                    op2 = ALU.subtract
                else:
                    tt(pa, A1, lrm, ALU.mult, [Ak] + kk, zk)
                    tt(pb_, A0, lim, ALU.mult, [Ak] + kk, zk)
                    op2 = ALU.add
                for hf in range(2):
                    ps_ = slice(64 * hf, 64 * hf + 64)
                    tt(dst[ps_, m0:m1, 1, :, 16 * hf:16 * hf + 16], pa[ps_], pb_[ps_], op2, zk, [nm])

            cexp(Xexp, 'Xexp', 0, 4, Bbar, 'Bbar', False)
            cexp(Yexp, 'Yexp', 0, 9, Cq, 'Cq', True)
            cexp(Yexp, 'Yexp', 9, 17, Cq, 'Cq', True)
            for ph in range(2):
                for mg in range(4):
                    bk = nxt(0, 2)
                    for k4 in range(4):
                        m = mg * 4 + k4
                        for ri in range(2):
                            mm(pb[bk][:, k4 * 128:(k4 + 1) * 128], Xexp[:, 0, ri, 4 * ph:4 * ph + 4, :].rearrange("p q c -> p (q c)"),
                               Yexp[:, m, ri, 4 * ph:4 * ph + 4, :].rearrange("p q c -> p (q c)"), ri == 0, ri == 1, ['Xexp', 'Yexp'], ['pb%d' % bk])
                    tt(Kblk[:, mg * 4:mg * 4 + 4, ph, :], pb[bk][:, :512].rearrange("p (k c) -> p k c", k=4),
                       bdS[:].unsqueeze(1).to_broadcast([128, 4, 128]), ALU.mult, ['pb%d' % bk, 'bdS'], ['Kblk'])
            slot = 0
            pend = []
            for m in range(4):
                for ri in range(2):
                    for q in range(8):
                        p4 = q % 4
                        cp(XZ[:, p4, 32 * p4:32 * p4 + 32], Xexp[:, m, ri, q, :], ['Xexp'], ['XZ'], eng='act')
                        P.op('pe', lambda e, s_=slot, p4=p4: e.transpose(ptb[:, s_ * 128:(s_ + 1) * 128], XZ[:, p4, :], identb[:]),
                             reads=['XZ', 'identb'], writes=['ptb'])
                        pend.append((slot, m, ri, q))
                        slot += 1
                        if slot == 8:
                            for (s_, m_, r_, q_) in pend:
                                cp(WinZ[:, m_, r_, q_, :], ptb[:, s_ * 128:(s_ + 1) * 128], ['ptb'], ['WinZ'], eng='dve')
                            pend = []
                            slot = 0
            a2 = ZB[:, 0]
            tt(a2, tab[:, 1, 16, :].unsqueeze(2).to_broadcast([128, 8, 128]), iotaS[:].unsqueeze(1).to_broadcast([128, 8, 128]),
               ALU.mult, kk + ['iotaS'], zk)
            sin_of(a2, s2S[:], ZB[:, 1], tmpX[:], 0.0, zk, zk + ['s2S', 'zT'])
            sin_of(a2, c2S[:], ZB[:, 1], tmpX[:], 0.25, zk, zk + ['c2S', 'zT'])
            for q in range(8):
                ph = q // 4
                uv4 = uT[:, ph, 0:NP_].rearrange("p (n j) -> p j n", j=4)
                zb = 2 * (q % 2)
                kzr, kzi = 'pb%d' % zb, 'pb%d' % (zb + 1)
                for ri in range(2):
                    for j in range(4):
                        mm(pb[zb + ri][:, :512], WinZ[:, 3 - j, ri, q, :], uv4[:, j, :], j == 0, j == 3, ['WinZ', 'uT'], ['pb%d' % (zb + ri)])
                zr = pb[zb][:, :512].rearrange("p (n k) -> p k n", k=4)
                zi = pb[zb + 1][:, :512].rearrange("p (n k) -> p k n", k=4)
                for k in range(4):
                    mk = 4 * (3 - k)
                    lr_, li_, nli_ = tab[:, 5, mk, q:q + 1], tab[:, 6, mk, q:q + 1], tab[:, 7, mk, q:q + 1]
                    if k == 0:
                        ts(ZA[:, 0, q, :], zr[:, k, :], lr_, None, ALU.mult, None, [kzr] + kk, ['ZA'])
                        ts(ZA[:, 1, q, :], zi[:, k, :], lr_, None, ALU.mult, None, [kzi] + kk, ['ZA'])
                    else:
                        stt(ZA[:, 0, q, :], zr[:, k, :], lr_, ZA[:, 0, q, :], ALU.mult, ALU.add, [kzr, 'ZA'] + kk, ['ZA'])
                        stt(ZA[:, 1, q, :], zi[:, k, :], lr_, ZA[:, 1, q, :], ALU.mult, ALU.add, [kzi, 'ZA'] + kk, ['ZA'])
                    stt(ZA[:, 0, q, :], zi[:, k, :], nli_, ZA[:, 0, q, :], ALU.mult, ALU.add, [kzi, 'ZA'] + kk, ['ZA'])
                    stt(ZA[:, 1, q, :], zr[:, k, :], li_, ZA[:, 1, q, :], ALU.mult, ALU.add, [kzr, 'ZA'] + kk, ['ZA'])
            tt(ZB[:, 0], c2S[:], ZA[:, 0], ALU.mult, ['c2S', 'ZA'], ['ZB'])
            tt(tmpX[:], s2S[:], ZA[:, 1], ALU.mult, ['s2S', 'ZA'], ['zT'])
            tt(ZB[:, 0], ZB[:, 0], tmpX[:], ALU.add, ['ZB', 'zT'], ['ZB'])
            tt(ZB[:, 1], c2S[:], ZA[:, 1], ALU.mult, ['c2S', 'ZA'], ['ZB'])
            tt(tmpX[:], s2S[:], ZA[:, 0], ALU.mult, ['s2S', 'ZA'], ['zT'])
            tt(ZB[:, 1], ZB[:, 1], tmpX[:], ALU.subtract, ['ZB', 'zT'], ['ZB'])
            for ri in range(2):
                for q in range(8):
                    P.op('dve', lambda e, ri=ri, q=q: e.tensor_tensor_scan(out=ZA[:, ri, q, :], data0=tab[:, 0, 16, q:q + 1].to_broadcast([128, 128]),
                                                                          data1=ZB[:, ri, q, :], initial=0.0, op0=ALU.mult, op1=ALU.add),
                         reads=['ZB'] + kk, writes=['ZA'])
            tt(ZB[:, 0], c2S[:], ZA[:, 0], ALU.mult, ['c2S', 'ZA'], ['ZB'])
            tt(tmpX[:], s2S[:], ZA[:, 1], ALU.mult, ['s2S', 'ZA'], ['zT'])
            tt(ZB[:, 0], ZB[:, 0], tmpX[:], ALU.subtract, ['ZB', 'zT'], ['ZB'])
            tt(ZB[:, 1], c2S[:], ZA[:, 1], ALU.mult, ['c2S', 'ZA'], ['ZB'])
            tt(tmpX[:], s2S[:], ZA[:, 0], ALU.mult, ['s2S', 'ZA'], ['zT'])
            tt(ZB[:, 1], ZB[:, 1], tmpX[:], ALU.add, ['ZB', 'zT'], ['ZB'])
            P.op('dve', lambda e: e.memset(Sb[:, :, :, 0:1], 0.0), writes=['Sb'])
            cp(Sb[:, :, :, 1:129], ZB[:], ['ZB'], ['Sb'], eng='act')
            for ri in range(2):
                dma_nc('sp', o_s5_p[l, ri], ZB[:, ri, :, 127], ['ZB'], [])
            dma('sp', wgS, s5_wglu[l], ['c2S'], ['c2S'])
            cp(wgB[:], wgS, ['c2S'], ['wgB'], eng='act')
            yzc = {'n': 0}

            def yz_load(m, ri, ph):
                sl = yzc['n'] % 2
                yzc['n'] += 1
                for p4 in range(4):
                    cp(YZb[:, sl, p4, 32 * p4:32 * p4 + 32], Yexp[:, m, ri, 4 * ph + p4, :], ['Yexp'], ['YZb%d' % sl], eng='act' if p4 % 2 else 'pool')
                return sl

            for ri in range(2):
                for q in range(8):
                    mm(pb[4][:, (ri * 8 + q) * 16:(ri * 8 + q + 1) * 16], WinZ[:, 0, ri, q, :], uT[:, q // 4, NP_:NT], True, True,
                       ['WinZ', 'uT'], ['pb4'])
            lr1b = lr1.unsqueeze(2).to_broadcast([128, 8, 16])
            li1b = li1.unsqueeze(2).to_broadcast([128, 8, 16])
            bur = pb[4][:, 0:128].rearrange("p (q b) -> p q b", q=8)
            bui = pb[4][:, 128:256].rearrange("p (q b) -> p q b", q=8)
            tt(t16[:, 0], h0S[:, 0], lr1b, ALU.mult, ['h0S'] + kk, tk)
            tt(t16[:, 1], h0S[:, 1], li1b, ALU.mult, ['h0S'] + kk, tk)
            tt(h1S[:, 0], t16[:, 0], t16[:, 1], ALU.subtract, tk, ['h1S'])
            tt(h1S[:, 0], h1S[:, 0], bur, ALU.add, ['h1S', 'pb4'], ['h1S'])
            tt(t16[:, 0], h0S[:, 1], lr1b, ALU.mult, ['h0S'] + kk, tk)
            tt(t16[:, 1], h0S[:, 0], li1b, ALU.mult, ['h0S'] + kk, tk)
            tt(h1S[:, 1], t16[:, 0], t16[:, 1], ALU.add, tk, ['h1S'])
            tt(h1S[:, 1], h1S[:, 1], bui, ALU.add, ['h1S', 'pb4'], ['h1S'])
            cp(h1b[:], h1S[:], ['h1S'], ['h1b'], eng='act')
            for ri in range(2):
                dma('sp', o_s5_s[l, ri], h1S[:, ri], ['h1S'], [])

            def gelu_to(z_out, n_):
                act(gtmp[:, :n_], ytmp[:, :n_], AF.Square, zk, zk)
                ts(gtmp[:, :n_], gtmp[:, :n_], 0.044715, 1.0, ALU.mult, ALU.add, zk, zk)
                tt(gtmp[:, :n_], gtmp[:, :n_], ytmp[:, :n_], ALU.mult, zk, zk)
                act(gtmp[:, :n_], gtmp[:, :n_], AF.Sigmoid, zk, zk, scale=1.5957691216057308)
                return gtmp

            for ph in range(2):
                first = True
                for ri in range(2):
                    sl = yz_load(0, ri, ph)
                    for p4 in range(4):
                        mm(pb[5][:, ph * 16:(ph + 1) * 16], YZb[:, sl, p4, :], h1b[:, ri, 4 * ph + p4, :], first, (ri == 1 and p4 == 3),
                           ['YZb%d' % sl, 'h1b'], ['pb5'])
                        first = False
                stt(ytmp[:, :NS], uT[:, ph, NP_:NT], d5[:, ph:ph + 1], pb[5][:, ph * 16:(ph + 1) * 16], ALU.mult, ALU.add,
                    ['uT', 'd5', 'pb5'] + zk, zk)
                gelu_to(None, NS)
                tt(zT[:, ph, NP_:NT], ytmp[:, :NS], gtmp[:, :NS], ALU.mult, zk, ['zT'])
            for ph in range(2):
                uv16 = uT[:, ph, 0:NP_].rearrange("p (n j) -> p j n", j=16)
                zv16 = zT[:, ph, 0:NP_].rearrange("p (n j) -> p j n", j=16)
                for i in range(16):
                    bk = i // 4
                    oc = pb[bk][:, (i % 4) * 128:(i % 4 + 1) * 128]
                    for j in range(i + 1):
                        mm(oc, Kblk[:, i - j, ph, :], uv16[:, j, :], j == 0, False, ['Kblk', 'uT'], ['pb%d' % bk])
                    for ri in range(2):
                        sl = yz_load(i + 1, ri, ph)
                        for p4 in range(4):
                            mm(oc, YZb[:, sl, p4, :], Sb[:, ri, 4 * ph + p4, 0:128], False, (ri == 1 and p4 == 3),
                               ['YZb%d' % sl, 'Sb'], ['pb%d' % bk])
                    if i % 4 == 3:
                        yv = ytmp[:, :512].rearrange("p (i n) -> p i n", i=4)
                        stt(yv, uv16[:, 4 * bk:4 * bk + 4, :], d5[:, ph:ph + 1], pb[bk][:, :512].rearrange("p (i n) -> p i n", i=4),
                            ALU.mult, ALU.add, ['uT', 'd5', 'pb%d' % bk] + zk, zk)
                        gelu_to(None, 512)
                        tt(zv16[:, 4 * bk:4 * bk + 4, :], yv, gtmp[:, :512].rearrange("p (i n) -> p i n", i=4), ALU.mult, zk, ['zT'])
            cols = [(i * 512, 512) for i in range(4)] + [(NP_, NS)]
            for co in range(2):
                for (c0_, n_) in cols:
                    bk = nxt(0, 2)
                    for k in range(2):
                        mm(pb[bk][:, :n_], wgB[:, k, co * 128:(co + 1) * 128], zT[:, k, c0_:c0_ + n_], k == 0, k == 1, ['wgB', 'zT'], ['pb%d' % bk])
                    act(gtmp[:, :n_], pb[bk][:, :n_], AF.Sigmoid, ['pb%d' % bk, 'bg5'] + zk, zk, bias=bg5[:, co:co + 1])
                    tt(uT[:, co, c0_:c0_ + n_], zT[:, co, c0_:c0_ + n_], gtmp[:, :n_], ALU.mult, ['zT'] + zk, ['uT'])
            for g in range(NG):
                t0, n = GROUPS[g]
                for m in range(8):
                    bk = nxt(2, 2)
                    for k in range(2):
                        mm(pb[bk][:, :n], wout_v[:, k, m * 128:(m + 1) * 128], uT[:, k, t0:t0 + n], k == 0, k == 1, ['wout', 'uT'], ['pb%d' % bk])
                    tt(xT[:, m, t0:t0 + n], xT[:, m, t0:t0 + n], pb[bk][:, :n], ALU.add, ['x%d_%d' % (m, g), 'pb%d' % bk], ['x%d_%d' % (m, g)])

        I16 = identf[:NS, :NS]

        def smp_update(heads, a_tile, k_ap, v_ap, q_ap, st_in, st_out, o_dst, rk):
            tiles = sorted(set(f for (_, f, _, _) in heads))
            vblk = kaT[:NS].rearrange("p a b c -> p (a b c)")
            kz = qgB[:NS, 0:128]
            aT = xtok[:, 0:NS]
            qzT = xtok[:, NS:2 * NS]
            for f in tiles:
                for (rows, src) in st_in(f):
                    dma('sp', smpS[rows, :, :], src, [], ['smpS'])
                mm(pb[4][:, 0:NS], a_tile(f), I16, True, True, rk + ['identf'], ['pb4'])
                cp(aT, pb[4][:, 0:NS], ['pb4'], ['xtok'])
                tt(smpS[:], smpS[:], aT.unsqueeze(2).to_broadcast([128, NS, 64]), ALU.mult, ['smpS', 'xtok'], ['smpS'])
                hs = [hh for hh in heads if hh[1] == f]
                for idx, (h, _, r0, K) in enumerate(hs):
                    P.op('pool', lambda e: e.memset(kz, 0.0), reads=['qgB'], writes=['qgB'])
                    cp(kz[:, r0:r0 + K], k_ap(h), rk, ['qgB'])
                    tt(vblk.rearrange("p (b v) -> p b v", b=NS), v_ap(h).unsqueeze(1).to_broadcast([NS, NS, 64]),
                       I16.unsqueeze(2).to_broadcast([NS, NS, 64]), ALU.mult, rk + ['identf'], ['kaT'])
                    for hf in range(2):
                        mm(pb[2 + hf][:, :512], kz, vblk[:, hf * 512:(hf + 1) * 512], idx == 0, idx == len(hs) - 1,
                           ['qgB', 'kaT'], ['pb%d' % (2 + hf)])
                sflat = smpS[:].rearrange("p b v -> p (b v)")
                for hf in range(2):
                    tt(sflat[:, hf * 512:(hf + 1) * 512], sflat[:, hf * 512:(hf + 1) * 512], pb[2 + hf][:, :512], ALU.add,
                       ['smpS', 'pb%d' % (2 + hf)], ['smpS'])
                for (rows, dst) in st_out(f):
                    dma('sp', dst, smpS[rows, :, :], ['smpS'], [])
                for (h, _, r0, K) in hs:
                    mm(pb[4][:K, 0:NS], q_ap(h), I16, True, True, rk + ['identf'], ['pb4'])
                    P.op('dve', lambda e: e.memset(qzT, 0.0), reads=['xtok'], writes=['xtok'])
                    cp(qzT[r0:r0 + K, :], pb[4][:K, 0:NS], ['pb4'], ['xtok'])
                    for hf in range(2):
                        mm(pb[2 + hf][:NS, :512], qzT, sflat[:, hf * 512:(hf + 1) * 512], True, True, ['xtok', 'smpS'], ['pb%d' % (2 + hf)])
                    for hf in range(2):
                        tt(dgT[:].rearrange("p (b v) -> p b v", b=8), pb[2 + hf][:NS, :512].rearrange("p (b v) -> p b v", b=8),
                           identf[:NS, hf * 8:hf * 8 + 8].unsqueeze(2).to_broadcast([NS, 8, 64]), ALU.mult, ['pb%d' % (2 + hf), 'identf'], ['dgT'])
                        dst = o_dst[:, h * 64:(h + 1) * 64] if hf == 0 else o2S[:NS, 0:64]
                        P.op('dve', lambda e, dst=dst: e.tensor_reduce(out=dst, in_=dgT[:].rearrange("p (b v) -> p v b", b=8), axis=AX.X, op=ALU.add),
                             reads=['dgT'], writes=['oS', 'o2S'])
                    tt(o_dst[:, h * 64:(h + 1) * 64], o_dst[:, h * 64:(h + 1) * 64], o2S[:NS, 0:64], ALU.add, ['oS', 'o2S'], ['oS'])

        def sample_tile(l):
            R_ = slice(0, NS)
            for (a, b, po) in TOK_GROUPS:
                bk = nxt(0, 2)
                for c in range(8):
                    mm(pb[bk][R_, :b - a], hnT[:, c, 0:NS], win_v[:, c, a:b], c == 0, c == 7, ['hnT%d' % c] + win_key(a, b), ['pb%d' % bk])
                cp(pj[R_, po:po + (b - a)], pb[bk][R_, :b - a], ['pb%d' % bk], ['pj'], eng='act')
            bk = nxt(0, 2)
            for c in range(8):
                mm(pb[bk][R_, :512], hnT[:, c, 0:NS], win_v[:, c, 512:1024], c == 0, c == 7, ['hnT%d' % c] + win_key(512, 1024), ['pb%d' % bk])
            xq = ['laS', 'qS']
            cp(sdR[R_, :], pb[bk][R_, :512], ['pb%d' % bk], xq, eng='act')
            dma('sp', o_conv_s[l, :, 0:2, :], si_conv[l, :, 1:3, :], [], [])
            dma('sp', o_conv_s[l, :, 2, :], sdR[R_, :], xq, [])
            xbf = xbcT[:].rearrange("p c n -> p (c n)")
            wrow, brow = xbf[R_, 0:512], xbf[R_, 512:1024]
            acc = sdD[R_, :]
            ak = ['ex0', 'ex1']
            dma('sp', wrow, ssd_cwr[l, 3], [], ['xbcT'])
            dma('sp', brow, ssd_cbr[l], [], ['xbcT'])
            tt(acc, sdR[R_, :], wrow, ALU.mult, xq + ['xbcT'], ak)
            tt(acc, acc, brow, ALU.add, ak + ['xbcT'], ak)
            for j in range(3):
                dma('sp', wrow, ssd_cwr[l, j], [], ['xbcT'])
                dma('sp', brow, si_conv[l, :, j, :], [], ['xbcT'])
                tt(brow, brow, wrow, ALU.mult, ['xbcT'], ['xbcT'])
                tt(acc, acc, brow, ALU.add, ak + ['xbcT'], ak)
            act(acc, acc, AF.Silu, ak, ak)
            tt(dtS[R_, 0:4], pj[R_, 256:260], srow[R_, l, 0, :], ALU.add, ['pj', 'srow'], ['dtS'])
            act(dtS[R_, 0:4], dtS[R_, 0:4], AF.Exp, ['dtS'], ['dtS'])
            act(dtS[R_, 0:4], dtS[R_, 0:4], AF.Ln, ['dtS'], ['dtS'], bias=1.0)
            tt(dtS[R_, 4:8], dtS[R_, 0:4], arow[R_, l, :], ALU.mult, ['dtS', 'arow'], ['dtS'])
            act(dtS[R_, 4:8], dtS[R_, 4:8], AF.Exp, ['dtS'], ['dtS'])
            vtok = kS[R_, :]
            tt(vtok.rearrange("p (h v) -> p h v", h=4), sdD[R_, 0:256].rearrange("p (h v) -> p h v", h=4),
               dtS[R_, 0:4].unsqueeze(2).to_broadcast([NS, 4, 64]), ALU.mult, ak + ['dtS'], ['kS'])
            aexp = sgS[R_, 0:128]

            def ssd_a(f):
                cp(aexp.rearrange("p (g n) -> p g n", g=2), dtS[R_, 4 + f:8:2].unsqueeze(2).to_broadcast([NS, 2, 64]), ['dtS'], ['sgS'])
                return aexp
            ssd_heads = [(2 * gg + r, r, 64 * gg, 64) for r in range(2) for gg in range(2)]
            smp_update(ssd_heads, ssd_a,
                       lambda h: sdD[R_, 256 + 64 * (h // 2):256 + 64 * (h // 2) + 64],
                       lambda h: vtok[:, h * 64:(h + 1) * 64],
                       lambda h: sdD[R_, 384 + 64 * (h // 2):384 + 64 * (h // 2) + 64],
                       lambda f: [(slice(64 * gg, 64 * gg + 64), si_ssd[l, :, 2 * gg + f].rearrange("b n p -> n b p")) for gg in range(2)],
                       lambda f: [(slice(64 * gg, 64 * gg + 64), o_ssd_s[l, :, 2 * gg + f].rearrange("b n p -> n b p")) for gg in range(2)],
                       oS[R_, :], ak + ['kS', 'sgS', 'dtS'])
            tt(o2S[R_, :].rearrange("p (h v) -> p h v", h=4), sdD[R_, 0:256].rearrange("p (h v) -> p h v", h=4),
               srow[R_, l, 2, :].unsqueeze(2).to_broadcast([NS, 4, 64]), ALU.mult, ak + ['srow'], ['o2S'])
            tt(oS[R_, :], oS[R_, :], o2S[R_, :], ALU.add, ['oS', 'o2S'], ['oS'])
            act(o2S[R_, :], pj[R_, 0:256], AF.Silu, ['pj'], ['o2S'])
            tt(oS[R_, :], oS[R_, :], o2S[R_, :], ALU.mult, ['oS', 'o2S'], ['oS'])
            tt(o2S[R_, :], oS[R_, :], oS[R_, :], ALU.mult, ['oS'], ['o2S'])
            P.op('dve', lambda e: e.tensor_reduce(out=ssq[R_, 0:1], in_=o2S[R_, :], axis=AX.X, op=ALU.add), reads=['o2S'], writes=['ssq'])
            act(ssq[R_, 0:1], ssq[R_, 0:1], AF.Sqrt, ['ssq'], ['ssq'], bias=EPS, scale=1.0 / 256)
            P.op('dve', lambda e: e.reciprocal(out=ssq[R_, 0:1], in_=ssq[R_, 0:1]), reads=['ssq'], writes=['ssq'])
            stt(mixtok[R_, 256:512], oS[R_, :], ssq[R_, 0:1], ssdng[R_, :], ALU.mult, ALU.mult, ['oS', 'ssq', 'ssdng'], ['mixtok'])
            qh = sdR[R_, 256:512]
            act(qh, pj[R_, 260:516], AF.Silu, ['pj'], xq)
            act(sgS[R_, :], pj[R_, 516:772], AF.Sigmoid, ['pj'], ['sgS'])
            tt(sgS[R_, :], sgS[R_, :], oml[R_, :], ALU.mult, ['sgS', 'oml'], ['sgS'])
            tt(sgS[R_, :], sgS[R_, :], lbl[R_, l, :], ALU.add, ['sgS', 'lbl'], ['sgS'])
            ts(kS[R_, :], sgS[R_, :], -1.0, 1.0, ALU.mult, ALU.add, ['sgS'], ['kS'])
            hg_heads = [(h, h // 2, 64 * (h % 2), 64) for h in range(4)]
            smp_update(hg_heads, lambda f: sgS[R_, f * 128:(f + 1) * 128],
                       lambda h: kS[R_, h * 64:(h + 1) * 64],
                       lambda h: pj[R_, 772 + h * 64:772 + (h + 1) * 64],
                       lambda h: qh[:, h * 64:(h + 1) * 64],
                       lambda f: [(slice(64 * r, 64 * r + 64), si_hg[l, :, 2 * f + r].rearrange("b k v -> k b v")) for r in range(2)],
                       lambda f: [(slice(64 * r, 64 * r + 64), o_hg_s[l, :, 2 * f + r].rearrange("b k v -> k b v")) for r in range(2)],
                       oS[R_, :], xq + ['sgS', 'kS', 'pj'])
            head_norm_gate(l, hgng[R_, :].rearrange("p (h v) -> p h v", h=4), 'hgng', pj[R_, 1028:1284], mixtok[R_, 512:768], npart=NS, from_psum=False)
            for c in range(8):
                mm(pb[4][:16, :NS], win_v[:, c, 2820:2836], hnT[:, c, 0:NS], c == 0, c == 7, win_key(2820, 2836) + ['hnT%d' % c], ['pb4'])
            cp(lrT[:, :NS], pb[4][:16, :NS], ['pb4'], ['lrT'])
            mm(pb[5][R_, :128], lrT[:, :NS], w2[:], True, True, ['lrT', 'w2'], ['pb5'])
            ga = sgS[R_, 0:128]
            tt(ga, pb[5][R_, :128], glab[R_, :], ALU.add, ['pb5', 'glab'], ['sgS'])
            act(ga, ga, AF.Exp, ['sgS'], ['sgS'], scale=-1.0)
            act(ga, ga, AF.Ln, ['sgS'], ['sgS'], bias=1.0)
            act(ga, ga, AF.Exp, ['sgS'], ['sgS'], scale=-1.0 / 16.0)
            gq = sdR[R_, 256:384]
            ts(gq, pj[R_, 1284:1412], 32.0 ** -0.5, None, ALU.mult, None, ['pj'], xq)
            gla_heads = [(h, 0, 32 * h, 32) for h in range(4)]
            smp_update(gla_heads, lambda f: ga,
                       lambda h: pj[R_, 1412 + h * 32:1412 + (h + 1) * 32],
                       lambda h: pj[R_, 1540 + h * 64:1540 + (h + 1) * 64],
                       lambda h: gq[:, h * 32:(h + 1) * 32],
                       lambda f: [(slice(0, 128), si_gla[l].rearrange("b h k v -> (h k) b v"))],
                       lambda f: [(slice(0, 128), o_gla_s[l].rearrange("b h k v -> (h k) b v"))],
                       oS[R_, :], xq + ['sgS', 'pj'])
            head_norm_gate(l, glang[R_, :].unsqueeze(1).to_broadcast([NS, 4, 64]), 'glang', pj[R_, 1796:2052], mixtok[R_, 768:1024], npart=NS, from_psum=False)
            for f in range(2, 8):
                mm(pb[6][:, f * NS:(f + 1) * NS], mixtok[R_, f * 128:(f + 1) * 128], identb[:NS, :NS], True, True, ['mixtok', 'identb'], ['pb6'])
            cp(mixT[:, 2:8, :NS], pb[6][:, 2 * NS:8 * NS].rearrange("p (f t) -> p f t", f=6), ['pb6'], ['mixT'])

        def out_proj(g, tc0, n):
            for m in range(8):
                bk = nxt(0, 2)
                for c in range(2, 8):
                    mm(pb[bk][:, :n], wout_v[:, c, m * 128:(m + 1) * 128], mixT[:, c, :n], c == 2, c == 7,
                       ['wout', 'mixT'], ['pb%d' % bk])
                tt(xT[:, m, tc0:tc0 + n], xT[:, m, tc0:tc0 + n], pb[bk][:, :n], ALU.add, ['x%d_%d' % (m, g), 'pb%d' % bk],
                   ['x%d_%d' % (m, g)])

        load_phaseA_weights(0)
        for l in range(RUN_LAYERS):
            dma('sp', hgng[:], hg_norm_g[:, l, :], [], ['hgng'])
            dma('sp', glang[:], gla_norm_g[:, l, :], [], ['glang'])
            dma('sp', glab[:], gla_b_gk[:, l, :], [], ['glab'])
            dma('sp', ssdng[:], ssd_ng[:, l, :], [], ['ssdng'])
            dma('sp', w2[:], gla_w_gk2[:, l, :], [], ['w2'])
            ts(oml[:], lbl[:, l, :], -1.0, 1.0, ALU.mult, ALU.add, ['lbl'], ['oml'])
            for nm in ('hg', 'gla'):
                P.op('pool', lambda e, nm=nm: e.memset(qlz[nm][:], 0.0), writes=['qlz_' + nm])
            P.op('pool', lambda e: e.memset(Cz[:], 0.0), writes=['Cz'])
            P.op('pool', lambda e: e.memset(BhatZ[:], 0.0), writes=['BhatZ'])
            P.op('dve', lambda e: e.memset(st_ssd[:], 0.0), writes=['st_ssd'])
            P.op('dve', lambda e: e.memset(stb_ssd[:], 0.0), writes=['stb_ssd'])
            for nm in ('hg', 'gla'):
                P.op('dve', lambda e, nm=nm: e.memset(hst[nm][:], 0.0), writes=['hst_' + nm])
                P.op('dve', lambda e, nm=nm: e.memset(stz[nm][:], 0.0), writes=['stz_' + nm])
            pro_done = set()
            pending_fin = []
            for g in range(RUN_GROUPS):
                t0, n = GROUPS[g]

                def prologue_gen(g, l=l):
                    t0_, n_ = GROUPS[g]
                    rmsnorm_group(g, lambda c, l=l: gmix[:, l, c:c + 1], 'gmix', hnT, 'hnT', 0)
                    yield
                    if g > 0 and g < SAMP:
                        cp(xbcT[:, :, 0:3], xbcT[:, :, GN:GN + 3], ['xbcT'], ['xbcT'])
                    elif g == 0:
                        P.op('dve', lambda e: e.memset(xbcT[:, :, 0:3], 0.0), reads=['xbcT'], writes=['xbcT'])
                    for m in ((0, 1) if g == SAMP else (0, 1, 4, 5, 6, 7)):
                        bk = nxt(0, 2)
                        for c in range(8):
                            mm(pb[bk][:, :n_], win_v[:, c, m * 128:(m + 1) * 128], hnT[:, c, :n_], c == 0, c == 7,
                               win_key(m * 128, (m + 1) * 128) + ['hnT%d' % c], ['pb%d' % bk])
                        if m < 2:
                            cp(uT[:, m, t0_:t0_ + n_], pb[bk][:, :n_], ['pb%d' % bk], ['uT'], eng='act')
                        else:
                            cp(xbcT[:, m - 4, 3:3 + n_], pb[bk][:, :n_], ['pb%d' % bk], ['xbcT'], eng='act')
                        yield

                if g not in pro_done:
                    for _ in prologue_gen(g):
                        pass
                    pro_done.add(g)
                if g < SAMP and RUN_SSD:
                    for cc in range(4):
                        ts(sdR[:, :n], xbcT[:, cc, 0:n], cwS[:, l, cc, 0:1], None, ALU.mult, None, ['xbcT', 'cwS'], ['laS', 'qS'])
                        for j in range(1, 4):
                            stt(sdR[:, :n], xbcT[:, cc, j:j + n], cwS[:, l, cc, j:j + 1], sdR[:, :n], ALU.mult, ALU.add,
                                ['xbcT', 'cwS', 'laS', 'qS'], ['laS', 'qS'])
                        act(xsT[:, cc, :n], sdR[:, :n], AF.Silu, ['laS', 'qS', 'cbS'], ['xsT'], bias=cbS[:, l, cc:cc + 1])
                    for gg in range(2):
                        cp(Cz[64 * gg:64 * gg + 64, gg, :n], xsT[64 * gg:64 * gg + 64, 3, :n], ['xsT'], ['Cz'], eng='pool')
                if g == LASTP and RUN_CONV_OUT:
                    for cc in range(4):
                        dma_nc('sp', o_conv_p[l][:, cc * 128:(cc + 1) * 128].rearrange("j p -> p j"), xbcT[:, cc, GN:GN + 3], ['xbcT'], [])
                if g == SAMP:
                    while pending_fin:
                        for _ in pending_fin.pop():
                            pass
                    if RUN_SAMPLE:
                        P.barrier()
                        sample_tile(l)
                    else:
                        P.op('pool', lambda e, n=n: e.memset(mixT[:, :, :n], 0.0), reads=['mixT'], writes=['mixT'])
                else:
                    for t in range(RUN_TILES):
                        c0 = t * 128
                        for (a, b, po) in TOK_GROUPS:
                            bk = nxt(0, 2)
                            for c in range(8):
                                mm(pb[bk][:, :b - a], hnT[:, c, c0:c0 + 128], win_v[:, c, a:b], c == 0, c == 7,
                                   ['hnT%d' % c] + win_key(a, b), ['pb%d' % bk])
                            cp(pj[:, po:po + (b - a)], pb[bk][:, :b - a], ['pb%d' % bk], ['pj'], eng='act')
                        def hg_gen(l=l, c0=c0):
                            act(qS[:], pj[:, 260:516], AF.Silu, ['pj'], ['qS'])
                            act(sgS[:], pj[:, 516:772], AF.Sigmoid, ['pj'], ['sgS'])
                            yield
                            tt(sgS[:], sgS[:], oml[:], ALU.mult, ['sgS', 'oml'], ['sgS'])
                            tt(sgS[:], sgS[:], lbl[:, l, :], ALU.add, ['sgS', 'lbl'], ['sgS'])
                            act(laS[:], sgS[:], AF.Ln, ['sgS'], ['laS'])
                            ts(kS[:], sgS[:], -1.0, 1.0, ALU.mult, ALU.add, ['sgS'], ['kS'])
                            cp(vB[:], pj[:, 772:1028], ['pj'], ['vB'])
                            yield
                            yield from gla_like(B_HG, 'hg', l, 4, 64, 128, qS[:], kS[:], vB, laS[:], 1.0, 4, ['qS', 'kS', 'laS'])
                            yield from head_norm_gate_gen(l, hgng[:].rearrange("p (h v) -> p h v", h=4), 'hgng', pj[:, 1028:1284], mixtok[:, 512:768], B=B_HG)
                            yield

                        def gla_gen(l=l, c0=c0):
                            gla_, gq_, gk_ = g_lqk[:, 0, :], g_lqk[:, 1, :], g_lqk[:, 2, :]
                            gk3 = ['g_la', 'g_q', 'g_k']
                            for c in range(8):
                                mm(pb[3][:16, :128], win_v[:, c, 2820:2836], hnT[:, c, c0:c0 + 128], c == 0, c == 7,
                                   win_key(2820, 2836) + ['hnT%d' % c], ['pb3'])
                            cp(lrT[:], pb[3][:16, :128], ['pb3'], ['lrT'])
                            yield
                            mm(pb[2][:, :128], lrT[:], w2[:], True, True, ['lrT', 'w2'], ['pb2'])
                            tt(gla_, pb[2][:, :128], glab[:], ALU.add, ['pb2', 'glab'], ['g_la'])
                            yield
                            act(gla_, gla_, AF.Exp, ['g_la'], ['g_la'], scale=-1.0)
                            act(gla_, gla_, AF.Ln, ['g_la'], ['g_la'], bias=1.0)
                            ts(gq_, pj[:, 1284:1412], 32.0 ** -0.5, None, ALU.mult, None, ['pj'], ['g_q'])
                            cp(gk_, pj[:, 1412:1540], ['pj'], ['g_k'])
                            cp(g_vB[:], pj[:, 1540:1796], ['pj'], ['g_vB'])
                            yield
                            yield from gla_like(B_GL, 'gla', l, 4, 32, 128, gq_, gk_, g_vB, gla_, -1.0 / 16.0, 1, gk3)
                            yield from head_norm_gate_gen(l, glang[:].unsqueeze(1).to_broadcast([128, 4, 64]), 'glang', pj[:, 1796:2052],
                                           mixtok[:, 768:1024], B=B_GL)
                            yield

                        gens = [ssd_tile(l, c0), gla_gen()]
                        while pending_fin:
                            gens.append(pending_fin.pop())
                        hg_started = False
                        if not INTERLEAVE:
                            for gi in gens + [hg_gen()]:
                                for _ in gi:
                                    pass
                            gens = []
                        prio = set()
                        while gens:
                            for gi in list(gens):
                                try:
                                    r_ = next(gi)
                                    if id(gi) in prio:
                                        r_ = next(gi)
                                except StopIteration:
                                    gens.remove(gi)
                                    continue
                                if r_ == 'front_done' and not hg_started:
                                    hg_it = hg_gen()
                                    gens.insert(0, hg_it)
                                    prio.add(id(hg_it))
                                    hg_started = True
                                    if PRO_AHEAD and t == RUN_TILES - 1 and g + 1 < RUN_GROUPS and (g + 1) not in pro_done:
                                        gens.append(prologue_gen(g + 1))
                                        pro_done.add(g + 1)
                        def finish_gen(g=g, tc0=t0 + c0):
                            for f in range(2, 8):
                                P.op('pe', lambda e, f=f: e.transpose(ptb[:, f * 128:(f + 1) * 128], mixtok[:, f * 128:(f + 1) * 128], identb[:]),
                                     reads=['mixtok', 'identb'], writes=['ptb'])
                            cp(mixT[:, 2:8, :], ptb[:, 256:1024].rearrange("p (f t) -> p f t", f=6), ['ptb'], ['mixT'])
                            yield
                            for m0 in range(0, 8, 2):
                                for m in (m0, m0 + 1):
                                    bk = nxt(0, 2)
                                    for c in range(2, 8):
                                        mm(pb[bk][:, :128], wout_v[:, c, m * 128:(m + 1) * 128], mixT[:, c, :128], c == 2, c == 7,
                                           ['wout', 'mixT'], ['pb%d' % bk])
                                    tt(xT[:, m, tc0:tc0 + 128], xT[:, m, tc0:tc0 + 128], pb[bk][:, :128], ALU.add,
                                       ['x%d_%d' % (m, g), 'pb%d' % bk], ['x%d_%d' % (m, g)])
                                yield

                        last_prompt_tile = (g == min(LASTP, RUN_GROUPS - 1) or g == LASTP) and t == RUN_TILES - 1
                        if FIN_DEFER and not last_prompt_tile and g < LASTP + 1:
                            pending_fin.append(finish_gen())
                        else:
                            for _ in finish_gen():
                                pass
                    if g == LASTP:
                        dma('sp', o_hg_p[l].rearrange("(f p) v -> p f v", p=128), hst['hg'][:], ['hst_hg'], [])
                        for gg in range(2):
                            dma('sp', o_ssd_p[l, 2 * gg:2 * gg + 2].rearrange("r n p -> n r p"), st_ssd[64 * gg:64 * gg + 64, :, :], ['st_ssd'], [])
                        dma('sp', o_gla_p[l], hst['gla'][:, 0, :], ['hst_gla'], [])
                if g == SAMP:
                    out_proj(g, t0, n)
            if RUN_S5:
                P.barrier()
                s5_phase(l)
            P.barrier()
            load_mlp_block(l, 0)
            load_mlp_block(l, 1)
            for g in range(NG):
                rmsnorm_group(g, lambda c, l=l: gmlp[:, l, c:c + 1], 'gmlp', hn2T, 'hn2T%d_' % g, GROUPS[g][0])
            for b in range(4 if RUN_MLP else 0):
                up, dn = mlp_views(b)
                hb = b % 2

                def mlp_up(g, hbuf, hk):
                    t0, n = GROUPS[g]
                    for m in range(8):
                        bk = nxt(0, 2)
                        for c in range(8):
                            mm(pb[bk][:, :n], up[:, c, m * 128:(m + 1) * 128], hn2T[:, c, t0:t0 + n], c == 0, c == 7,
                               ['mup%d' % hb, 'hn2T%d_%d' % (g, c)], ['pb%d' % bk])
                        act(hbuf[:, m, :n], pb[bk][:, :n], AF.Relu, ['pb%d' % bk], [hk + '%d' % m])
                        tt(hbuf[:, m, :n], hbuf[:, m, :n], hbuf[:, m, :n], ALU.mult, [hk + '%d' % m], [hk + '%d' % m])

                def mlp_down(g, hbuf, hk):
                    t0, n = GROUPS[g]
                    for m in range(8):
                        bk = nxt(2, 2)
                        for c in range(8):
                            mm(pb[bk][:, :n], dn[:, c, m * 128:(m + 1) * 128], hbuf[:, c, :n], c == 0, c == 7,
                               ['mdn%d' % hb, hk + '%d' % c], ['pb%d' % bk])
                        tt(xT[:, m, t0:t0 + n], xT[:, m, t0:t0 + n], pb[bk][:, :n], ALU.add,
                           ['x%d_%d' % (m, g), 'pb%d' % bk], ['x%d_%d' % (m, g)])

                hbufs = [(hT, 'hT'), (hT2, 'hU')]
                mlp_up(0, *hbufs[0])
                for g in range(NG):
                    if g + 1 < NG:
                        mlp_up(g + 1, *hbufs[(g + 1) % 2])
                    mlp_down(g, *hbufs[g % 2])
                if b + 2 < 4:
                    load_mlp_block(l, b + 2)
            P.barrier()
            if l + 1 < RUN_LAYERS:
                load_phaseA_weights(l + 1)
        for g in range(NG):
            t0, n = GROUPS[g]
            for c in range(8):
                act(hn2T[:, c, t0:t0 + n], xT[:, c, t0:t0 + n], AF.Square, ['x%d_%d' % (c, g)], ['hn2T%d_%d' % (g, c)])
            bk = nxt(0, 2)
            for c in range(8):
                mm(pb[bk][:, :n], onesb[:], hn2T[:, c, t0:t0 + n], c == 0, c == 7, ['onesb', 'hn2T%d_%d' % (g, c)], ['pb%d' % bk])
            act(rstd[:, :n], pb[bk][:, :n], AF.Sqrt, ['pb%d' % bk], ['rstd'], bias=EPS, scale=1.0 / D)
            P.op('dve', lambda e, n=n: e.reciprocal(out=rstd[:, :n], in_=rstd[:, :n]), reads=['rstd'], writes=['rstd'])
            for c in range(8):
                stt(xT[:, c, t0:t0 + n], xT[:, c, t0:t0 + n], gfin[:, c:c + 1], rstd[:, :n], ALU.mult, ALU.mult,
                    ['x%d_%d' % (c, g), 'gfin', 'rstd'], ['x%d_%d' % (c, g)])
        for c in range(8):
            dma('sp', yT[c * 128:(c + 1) * 128, :], xT[:, c, :], ['x%d_%d' % (c, g) for g in range(NG)], [])
        P.finish()
        P.emit()
    return nc


_CACHE = {}


def kernel(**inputs):
    f32 = np.float32
    x_prompt = np.asarray(inputs['x_prompt'], f32)
    x_sample = np.asarray(inputs['x_sample'], f32)
    B = x_prompt.shape[0]
    consts = host_consts()

    def pc(w):
        L, R, N = w.shape
        return np.ascontiguousarray(w.reshape(L, R // 128, 128, N).transpose(0, 2, 1, 3))

    def bc(v):
        v = np.asarray(v, f32)
        return np.ascontiguousarray(np.broadcast_to(v[None], (128,) + v.shape))

    def ql(v):
        v = np.asarray(v, f32)
        L = v.shape[0]
        rest = v.shape[3:]
        v = v.reshape((L, 8, 2, 64) + rest)
        nd = len(rest)
        v = v.transpose((2, 3, 0, 1) + tuple(range(4, 4 + nd)))
        return np.ascontiguousarray(v.reshape((128, L, 8) + rest))

    shared = {
        'w_in': pc(np.asarray(inputs['w_in'], f32)),
        'w_out': pc(np.asarray(inputs['w_out'], f32)),
        'w_up': pc(np.asarray(inputs['w_up'], f32)),
        'w_down': pc(np.asarray(inputs['w_down'], f32)),
        'g_mix': np.ascontiguousarray(np.asarray(inputs['norm_mix_g'], f32).reshape(DEPTH, 8, 128).transpose(2, 0, 1)),
        'g_mlp': np.ascontiguousarray(np.asarray(inputs['norm_mlp_g'], f32).reshape(DEPTH, 8, 128).transpose(2, 0, 1)),
        'g_fin': np.ascontiguousarray(np.asarray(inputs['norm_final_g'], f32).reshape(8, 128).T),
        'hg_lb_logits': bc(inputs['hg_lb_logits']),
        'hg_norm_g': bc(inputs['hg_norm_g']),
        'gla_w_gk2': np.ascontiguousarray(np.asarray(inputs['gla_w_gk2'], f32).transpose(1, 0, 2)),
        'gla_b_gk': bc(inputs['gla_b_gk']),
        'gla_norm_g': bc(inputs['gla_norm_g']),
        'ssd_cw': np.ascontiguousarray(np.asarray(inputs['ssd_conv_w'], f32).reshape(DEPTH, 4, 4, 128).transpose(3, 0, 2, 1)),
        'ssd_cb': np.ascontiguousarray(np.asarray(inputs['ssd_conv_b'], f32).reshape(DEPTH, 4, 128).transpose(2, 0, 1)),
        'ssd_rows': bc(np.stack([np.asarray(inputs['ssd_dt_bias'], f32), np.asarray(inputs['ssd_a_log'], f32),
                                 np.asarray(inputs['ssd_d'], f32)], axis=1)),
        'ssd_ng': bc(inputs['ssd_norm_g']),
        's5_lam': np.ascontiguousarray(np.stack([ql(inputs['s5_lam_re']), ql(inputs['s5_lam_im'])], axis=2)),
        's5_ldt': ql(np.repeat(np.asarray(inputs['s5_log_dt'], f32)[:, :, None], 64, axis=2)),
        's5_B': np.ascontiguousarray(np.stack([ql(inputs['s5_b_re']), ql(inputs['s5_b_im'])], axis=2)),
        's5_C': np.ascontiguousarray(np.stack([ql(np.asarray(inputs['s5_c_re'], f32).transpose(0, 1, 3, 2)),
                                              ql(np.asarray(inputs['s5_c_im'], f32).transpose(0, 1, 3, 2))], axis=2)),
        's5_d': np.ascontiguousarray(np.asarray(inputs['s5_d'], f32).reshape(DEPTH, 2, 128).transpose(2, 0, 1)),
        's5_bglu': np.ascontiguousarray(np.asarray(inputs['s5_b_glu'], f32).reshape(DEPTH, 2, 128).transpose(2, 0, 1)),
        's5_wglu': np.ascontiguousarray(np.asarray(inputs['s5_w_glu'], f32).reshape(DEPTH, 2, 128, 256).transpose(0, 2, 1, 3)),
        'ssd_cwr': np.ascontiguousarray(np.broadcast_to(np.asarray(inputs['ssd_conv_w'], f32)[:, :, None, :], (DEPTH, 4, NS, 512))),
        'ssd_cbr': np.ascontiguousarray(np.broadcast_to(np.asarray(inputs['ssd_conv_b'], f32)[:, None, :], (DEPTH, NS, 512))),
        'c_iota': np.ascontiguousarray(np.broadcast_to(np.arange(128, dtype=f32)[None], (128, 128))),
        'c_bd': np.kron(np.eye(4, dtype=f32), np.ones((32, 32), f32)),
        'c_ident': consts['ident'], 'c_ones': consts['ones'], 'c_tglob': consts['tglob'], 'c_tloc': consts['tloc'],
        'c_ltg': consts['ltg'], 'c_maskT': consts['tglob'],
        'c_ea': np.ascontiguousarray(consts['ea'].transpose(1, 0, 2)), 'c_ea0g': consts['ea0g'],
    }
    in_maps = []
    for i in range(8):
        xs = x_sample[i * NS:(i + 1) * NS, 0, :]
        xt = np.concatenate([x_prompt[i], xs], axis=0).T
        m = dict(shared)
        m['xT_in'] = np.ascontiguousarray(xt)
        h0 = []
        for nm in ('state_s5_re', 'state_s5_im'):
            v = np.asarray(inputs[nm], f32)[:, i * NS:(i + 1) * NS]
            v = v.reshape(DEPTH, NS, 8, 2, 64).transpose(0, 3, 4, 2, 1)
            h0.append(v.reshape(DEPTH, 128, 8, NS))
        m['s5_h0'] = np.ascontiguousarray(np.stack(h0, axis=1))
        bs = slice(i * NS, (i + 1) * NS)
        m['si_conv'] = np.ascontiguousarray(np.asarray(inputs['state_ssd_conv'], f32)[:, bs])
        m['si_ssd'] = np.ascontiguousarray(np.asarray(inputs['state_ssd'], f32)[:, bs])
        m['si_hg'] = np.ascontiguousarray(np.asarray(inputs['state_hgrn'], f32)[:, bs])
        m['si_gla'] = np.ascontiguousarray(np.asarray(inputs['state_gla'], f32)[:, bs])
        in_maps.append(m)
    if 'nc' not in _CACHE:
        _CACHE['nc'] = build_program()
    res = run_bass_kernel_spmd(_CACHE['nc'], in_maps, core_ids=list(range(8)))
    R = res.results
    y_prompt = np.stack([R[i]['yT'][:, :NP_].T for i in range(8)], 0)
    y_sample = np.concatenate([R[i]['yT'][:, NP_:].T for i in range(8)], 0)[:, None, :]
    cat = lambda k: np.ascontiguousarray(np.concatenate([R[i][k] for i in range(8)], axis=1))
    p_conv = np.stack([R[i]['o_conv_p'] for i in range(8)], 1)
    p_ssd = np.stack([R[i]['o_ssd_p'] for i in range(8)], 1)

    def unq(v):
        sh = v.shape[:-2]
        v = v.reshape(sh + (2, 64, 8))
        nd = len(sh)
        v = v.transpose(tuple(range(nd)) + (nd + 2, nd, nd + 1))
        return v.reshape(sh + (16, 64))
    p_s5 = [np.stack([unq(R[i]['o_s5_p'][:, ri]) for i in range(8)], 1) for ri in range(2)]
    s_s5 = []
    for ri in range(2):
        per = []
        for i in range(8):
            v = R[i]['o_s5_s'][:, ri]
            v = v.reshape(DEPTH, 2, 64, 8, NS).transpose(0, 4, 3, 1, 2)
            per.append(v.reshape(DEPTH, NS, 16, 64))
        s_s5.append(np.concatenate(per, axis=1))
    p_hg = np.stack([R[i]['o_hg_p'].reshape(DEPTH, 4, 64, 64) for i in range(8)], 1)
    p_gla = np.stack([R[i]['o_gla_p'].reshape(DEPTH, 4, 32, 64) for i in range(8)], 1)
    outs = (np.ascontiguousarray(y_prompt), np.ascontiguousarray(y_sample),
            np.ascontiguousarray(p_s5[0]), np.ascontiguousarray(p_s5[1]), np.ascontiguousarray(p_conv), np.ascontiguousarray(p_ssd),
            np.ascontiguousarray(p_hg), np.ascontiguousarray(p_gla),
            np.ascontiguousarray(s_s5[0]), np.ascontiguousarray(s_s5[1]), cat('o_conv_s'), cat('o_ssd_s'),
            cat('o_hg_s'), cat('o_gla_s'))
    return outs
```

```python
import numpy as np
from contextlib import ExitStack
import concourse.bass as bass
import concourse.mybir as mybir
from concourse.bass_utils import run_bass_kernel_spmd

F32 = mybir.dt.float32
BF16 = mybir.dt.bfloat16
AF = mybir.ActivationFunctionType
ALU = mybir.AluOpType
AX = mybir.AxisListType

EPOCH = 60000
DEPTH = 4
D = 1024
NP_ = 2048
NS = 16
NT = NP_ + NS
NIN = 2836
EPS = 1e-6
GN = 256
GROUPS = [(GN * i, GN) for i in range(NP_ // GN)] + [(NP_, NS)]
NG = len(GROUPS)
LASTP = NG - 2
SAMP = NG - 1
RUN_LAYERS = DEPTH
RUN_MLP = True
RUN_HG = True
RUN_GLA = True
RUN_SSD = True
RUN_S5 = True
RUN_SAMPLE = True
RUN_TILES = GN // 128
RUN_GROUPS = NG
RUN_CONV_OUT = True
DBG_STAGE = 99
INTERLEAVE = True
PRO_AHEAD = True
FIN_DEFER = True
SSD_STAGE = 99


class Prog:
    def __init__(self, nc, es):
        self.nc = nc
        self.es = es
        self.engs = ['pe', 'act', 'dve', 'pool', 'sp']
        self.ops = {e: [] for e in self.engs}
        self.count = {e: 0 for e in self.engs}
        self.sems = {e: [] for e in self.engs}
        self.waited = {e: {} for e in self.engs}
        self.pending = {e: [] for e in self.engs}
        self.lw = {}
        self.rd = {}
        self.dma_sems = []
        self.dma_issued = []
        self.dma_rr = 0
        self.sw_sem = {}
        self.n_dma_sems = 10
        for i in range(self.n_dma_sems):
            self.dma_sems.append(es.enter_context(nc.semaphore("dq%d" % i)))
            self.dma_issued.append(0)

    def _sem_for(self, eng, idx):
        ep = (idx - 1) // EPOCH
        while len(self.sems[eng]) <= ep:
            self.sems[eng].append(self.es.enter_context(self.nc.semaphore("s_%s_%d" % (eng, len(self.sems[eng])))))
        return self.sems[eng][ep], (idx - 1) % EPOCH + 1, ep

    def _wait_tok(self, eng, d, waits):
        if d[0] == 'dma':
            _, si, val = d
            key = ('dma', si)
            if self.waited[eng].get(key, 0) < val:
                self.waited[eng][key] = val
                waits.append((self.dma_sems[si], val))
        else:
            e2, idx = d
            if e2 == 'pe' and eng == 'pe':
                return
            sem, val, ep = self._sem_for(e2, idx)
            key = (e2, ep)
            if self.waited[eng].get(key, 0) < val:
                self.waited[eng][key] = val
                waits.append((sem, val))

    def _deps(self, eng, reads, writes):
        ps = [b for b in reads if b.startswith('pb') or b == 'ptb']
        if ps:
            writes = list(writes) + ps
        deps = []
        for b in reads:
            if b in self.lw:
                deps.append(self.lw[b])
        for b in writes:
            if b in self.lw:
                deps.append(self.lw[b])
            deps.extend(self.rd.get(b, []))
        waits = self.pending[eng]
        self.pending[eng] = []
        best = {}
        for d in deps:
            if d[0] == 'dma':
                k = ('dma', d[1])
                v = d[2]
            else:
                k = (d[0], (d[1] - 1) // EPOCH)
                v = d[1]
            if k not in best or best[k][0] < v:
                best[k] = (v, d)
        for k in best:
            self._wait_tok(eng, best[k][1], waits)
        return waits

    def _note(self, tok, reads, writes):
        ps = [b for b in reads if b.startswith('pb') or b == 'ptb']
        if ps:
            reads = [b for b in reads if b not in ps]
            writes = list(writes) + ps
        for b in reads:
            self.rd.setdefault(b, []).append(tok)
        for b in writes:
            self.lw[b] = tok
            self.rd[b] = []

    def op(self, eng, fn, reads=(), writes=()):
        waits = self._deps(eng, reads, writes)
        self.count[eng] += 1
        idx = self.count[eng]
        sem, val, ep = self._sem_for(eng, idx)
        self.ops[eng].append((waits, fn, sem, 1))
        tok = (eng, idx)
        self._note(tok, reads, writes)
        return tok

    def dma(self, eng, fn, reads=(), writes=()):
        waits = self._deps(eng, reads, writes)
        if eng == 'pool':
            self.dma_sems.append(self.es.enter_context(self.nc.semaphore("dsw%d" % len(self.dma_sems))))
            self.dma_issued.append(0)
            si = len(self.dma_sems) - 1
        else:
            si = self.dma_rr
            self.dma_rr = (self.dma_rr + 1) % self.n_dma_sems
        prev = self.dma_issued[si] * 16
        key = ('dma', si)
        if prev > 0 and self.waited[eng].get(key, 0) < prev:
            self.waited[eng][key] = prev
            waits.append((self.dma_sems[si], prev))
        self.dma_issued[si] += 1
        val = self.dma_issued[si] * 16
        self.ops[eng].append((waits, fn, self.dma_sems[si], 16))
        tok = ('dma', si, val)
        self._note(tok, reads, writes)
        return tok

    def barrier(self):
        for e in self.engs:
            for e2 in self.engs:
                if e2 != e and self.count[e2] > 0:
                    self._wait_tok(e, (e2, self.count[e2]), self.pending[e])
            for si in range(len(self.dma_sems)):
                if self.dma_issued[si] > 0:
                    self._wait_tok(e, ('dma', si, self.dma_issued[si] * 16), self.pending[e])

    def finish(self):
        self.barrier()
        for e in self.engs:
            self.ops[e].append((self.pending[e], None, None, 0))
            self.pending[e] = []

    def emit(self):
        nc = self.nc
        P = self
        with nc.Block() as block:
            def run(e, name):
                for waits, fn, sem, inc in P.ops[name]:
                    for (s, v) in waits:
                        e.wait_ge(s, v)
                    if fn is not None:
                        ins = fn(e)
                        if sem is not None:
                            ins.then_inc(sem, inc)

            @block.tensor
            def _(e):
                run(e, 'pe')

            @block.scalar
            def _(e):
                run(e, 'act')

            @block.vector
            def _(e):
                run(e, 'dve')

            @block.gpsimd
            def _(e):
                run(e, 'pool')

            @block.sync
            def _(e):
                run(e, 'sp')


def host_consts():
    j = np.arange(128)
    c = {}
    c['ident'] = np.eye(128, dtype=np.float32)
    c['ones'] = np.ones((128, 128), np.float32)
    c['tglob'] = (j[:, None] <= j[None, :]).astype(np.float32)
    c['tloc'] = ((j[:, None] <= j[None, :]) & ((j[:, None] // 32) == (j[None, :] // 32))).astype(np.float32)
    c['ltg'] = (j[:, None] > j[None, :]).astype(np.float32)
    c['tglob'] = (j[:, None] <= j[None, :]).astype(np.float32)
    ea = np.zeros((4, 128, 128), np.float32)
    for a in range(4):
        jp = j[:, None]
        jj = j[None, :]
        plus = (jj < 32 * a) & (jp > jj) & (jp < 32 * a)
        minus = (jj >= 32 * a) & (jj < 32 * (a + 1)) & (jp >= 32 * a) & (jp <= jj)
        ea[a] = plus.astype(np.float32) - minus.astype(np.float32)
    c['ea'] = ea
    c['ea0g'] = -c['tglob']
    return c


def build_program():
    nc = bass.Bass("TRN2", target_bir_lowering=False)
    din = lambda n, s: nc.dram_tensor(n, list(s), F32, kind="ExternalInput").ap()
    dout = lambda n, s: nc.dram_tensor(n, list(s), F32, kind="ExternalOutput").ap()
    xT_in = din("xT_in", [D, NT])
    w_in = din("w_in", [DEPTH, 128, 8, NIN])
    w_out = din("w_out", [DEPTH, 128, 8, D])
    w_up = din("w_up", [DEPTH, 128, 8, 4096])
    w_down = din("w_down", [DEPTH, 128, 32, D])
    g_mix = din("g_mix", [128, DEPTH, 8])
    g_mlp = din("g_mlp", [128, DEPTH, 8])
    g_fin = din("g_fin", [128, 8])
    hg_lb_logits = din("hg_lb_logits", [128, DEPTH, 256])
    hg_norm_g = din("hg_norm_g", [128, DEPTH, 256])
    gla_w_gk2 = din("gla_w_gk2", [16, DEPTH, 128])
    gla_b_gk = din("gla_b_gk", [128, DEPTH, 128])
    gla_norm_g = din("gla_norm_g", [128, DEPTH, 64])
    ssd_cw = din("ssd_cw", [128, DEPTH, 4, 4])
    ssd_cb = din("ssd_cb", [128, DEPTH, 4])
    ssd_rows = din("ssd_rows", [128, DEPTH, 3, 4])
    ssd_ng = din("ssd_ng", [128, DEPTH, 256])
    s5_lam = din("s5_lam", [128, DEPTH, 2, 8])
    s5_ldt = din("s5_ldt", [128, DEPTH, 8])
    s5_B = din("s5_B", [128, DEPTH, 2, 8, 16])
    s5_C = din("s5_C", [128, DEPTH, 2, 8, 16])
    s5_d = din("s5_d", [128, DEPTH, 2])
    s5_bglu = din("s5_bglu", [128, DEPTH, 2])
    s5_wglu = din("s5_wglu", [DEPTH, 128, 2, 256])
    s5_h0 = din("s5_h0", [DEPTH, 2, 128, 8, 16])
    ssd_cwr = din("ssd_cwr", [DEPTH, 4, NS, 512])
    ssd_cbr = din("ssd_cbr", [DEPTH, NS, 512])
    si_conv = din("si_conv", [DEPTH, NS, 3, 512])
    si_ssd = din("si_ssd", [DEPTH, NS, 4, 64, 64])
    si_hg = din("si_hg", [DEPTH, NS, 4, 64, 64])
    si_gla = din("si_gla", [DEPTH, NS, 4, 32, 64])
    c_iota = din("c_iota", [128, 128])
    c_bd = din("c_bd", [128, 128])
    c_ident = din("c_ident", [128, 128])
    c_ones = din("c_ones", [128, 128])
    c_tglob = din("c_tglob", [128, 128])
    c_tloc = din("c_tloc", [128, 128])
    c_ltg = din("c_ltg", [128, 128])
    c_maskT = din("c_maskT", [128, 128])
    c_ea = din("c_ea", [128, 4, 128])
    c_ea0g = din("c_ea0g", [128, 128])

    yT = dout("yT", [D, NT])
    o_conv_p = dout("o_conv_p", [DEPTH, 3, 512])
    o_hg_p = dout("o_hg_p", [DEPTH, 256, 64])
    o_s5_p = dout("o_s5_p", [DEPTH, 2, 128, 8])
    o_conv_s = dout("o_conv_s", [DEPTH, NS, 3, 512])
    o_ssd_s = dout("o_ssd_s", [DEPTH, NS, 4, 64, 64])
    o_hg_s = dout("o_hg_s", [DEPTH, NS, 4, 64, 64])
    o_gla_s = dout("o_gla_s", [DEPTH, NS, 4, 32, 64])
    o_s5_s = dout("o_s5_s", [DEPTH, 2, 128, 8, 16])
    o_ssd_p = dout("o_ssd_p", [DEPTH, 4, 64, 64])
    o_gla_p = dout("o_gla_p", [DEPTH, 128, 64])

    with ExitStack() as es:
        P = Prog(nc, es)
        sb = lambda n, s, d=F32: es.enter_context(nc.sbuf_tensor(n, list(s), d))
        xT = sb("xT", [128, 8, NT])
        W = sb("W", [128, 32768], BF16)
        gmix = sb("gmix", [128, DEPTH, 8])
        gmlp = sb("gmlp", [128, DEPTH, 8])
        gfin = sb("gfin", [128, 8])
        identb = sb("identb", [128, 128], BF16)
        identf = sb("identf", [128, 128])
        onesb = sb("onesb", [128, 128], BF16)
        onesf = sb("onesf", [128, 128])
        tglob = sb("tglob", [128, 128])
        tloc = sb("tloc", [128, 128])
        ltg = sb("ltg", [128, 128])
        eaM = sb("eaM", [128, 4, 128])
        lbl = sb("lbl", [128, DEPTH, 256])
        cwS = sb("cwS", [128, DEPTH, 4, 4])
        cbS = sb("cbS", [128, DEPTH, 4])
        srow = sb("srow", [128, DEPTH, 3, 4])
        arow = sb("arow", [128, DEPTH, 4])
        rstd = sb("rstd", [128, GN])
        iotaS = sb("iotaS", [128, 128])
        bdS = sb("bdS", [128, 128])
        uT = sb("uT", [128, 2, NT], BF16)
        esA = ExitStack()
        sa = lambda n, s, d=F32: esA.enter_context(nc.sbuf_tensor(n, list(s), d))
        hnT = sa("hnT", [128, 8, GN], BF16)
        oml = sa("oml", [128, 256])
        lbt = sa("lbt", [128, 256])
        hgng = sa("hgng", [128, 256])
        glang = sa("glang", [128, 64])
        glab = sa("glab", [128, 128])
        ssdng = sa("ssdng", [128, 256])
        w2 = sa("w2", [16, 128])
        ecw = sa("ecw", [128, 12])
        dtS = sa("dtS", [128, 8])
        xtok = sa("xtok", [128, 256])
        xbcT = sa("xbcT", [128, 4, GN + 3])
        pj = sa("pj", [128, 2068])
        mixtok = sa("mixtok", [128, 1024], BF16)
        mixT = sa("mixT", [128, 8, 128], BF16)
        sdD = sa("sdD", [128, 512])
        ex = [sdD[:, 0:256], sdD[:, 256:512]]
        sdR = sa("sdR", [128, 512])
        laS = sdR[:, 0:256]
        qS = sdR[:, 256:512]
        kS = sa("kS", [128, 256])
        vB = sa("vB", [128, 256], BF16)
        sgS = sa("sgS", [128, 256])
        qlocB = sa("qlocB", [128, 256], BF16)
        qgB = sa("qgB", [128, 256], BF16)
        kaB = [sa("kaB%d" % a, [128, 256], BF16) for a in range(4)]
        khB = sa("khB", [128, 256], BF16)
        qgT = sa("qgT", [128, 2, 128], BF16)
        kaT = sa("kaT", [128, 4, 2, 128], BF16)
        sT = sa("sT", [128, 512], BF16)
        oS = sa("oS", [128, 256])
        o2S = sa("o2S", [128, 256])
        ssq = sa("ssq", [128, 4])
        elast = sa("elast", [128, 2])
        lrT = sa("lrT", [16, 128])
        esP = ExitStack()
        sp_ = lambda n, s, d=F32: esP.enter_context(nc.sbuf_tensor(n, list(s), d))
        xsT = sp_("xsT", [128, 4, GN], BF16)
        Cz = sp_("Cz", [128, 2, GN], BF16)
        BhatZ = sp_("BhatZ", [128, 4, 128], BF16)
        btok = sp_("btok", [128, 128])
        st_ssd = sp_("st_ssd", [128, 2, 64])
        stb_ssd = sp_("stb_ssd", [128, 2, 64], BF16)
        hst = {'hg': sp_("hst_hg", [128, 2, 64]), 'gla': sp_("hst_gla", [128, 2, 64])}
        stz = {'hg': sp_("stz_hg", [128, 4, 64], BF16), 'gla': sp_("stz_gla", [128, 4, 64], BF16)}
        qlz = {'hg': sp_("qlz_hg", [128, 4, 128], BF16), 'gla': sp_("qlz_gla", [128, 4, 128], BF16)}
        g_ex = sp_("g_ex", [128, 256])
        g_lqk = sp_("g_lqk", [128, 3, 128])
        g_vB = sp_("g_vB", [128, 256], BF16)
        g_b3 = sp_("g_b3", [128, 3, 128], BF16)
        g_qgT = sp_("g_qgT", [128, 1, 128], BF16)
        g_kaT = sp_("g_kaT", [128, 1, 1, 128], BF16)
        g_sT = sp_("g_sT", [128, 512], BF16)
        g_oS = sp_("g_oS", [128, 256])
        g_o2S = sp_("g_o2S", [128, 256])
        g_ssq = sp_("g_ssq", [128, 4])
        g_elast = sp_("g_elast", [128, 2])
        esP.close()
        esQ = ExitStack()
        sq_ = lambda n, s, d=F32: esQ.enter_context(nc.sbuf_tensor(n, list(s), d))
        dgT = sq_("dgT", [16, 512])
        smpS = sq_("smpS", [128, NS, 64])
        esQ.close()
        esA.close()
        esB = ExitStack()
        sbb = lambda n, s, d=F32: esB.enter_context(nc.sbuf_tensor(n, list(s), d))
        hn2T = sbb("hn2T", [128, 8, NT], BF16)
        hT = sbb("hT", [128, 8, GN], BF16)
        hT2 = sbb("hT2", [128, 8, GN], BF16)
        esB.close()
        esS = ExitStack()
        ss = lambda n, s, d=F32: esS.enter_context(nc.sbuf_tensor(n, list(s), d))
        lamS = ss("lamS", [128, 2, 8])
        t8 = ss("t8", [128, 12, 8])
        tab = ss("tab", [128, 8, 17, 8])
        Bq = ss("Bq", [128, 2, 8, 16])
        Cq = ss("Cq", [128, 2, 8, 16])
        Bbar = ss("Bbar", [128, 2, 8, 16])
        t16 = ss("t16", [128, 4, 8, 16])
        Xexp = ss("Xexp", [128, 4, 2, 8, 32], BF16)
        XZ = ss("XZ", [128, 4, 128], BF16)
        ZA = ss("ZA", [128, 2, 8, 128])
        ZB = ss("ZB", [128, 2, 8, 128])
        c2S = ss("c2S", [128, 8, 128])
        s2S = ss("s2S", [128, 8, 128])
        Sb = ss("Sb", [128, 2, 8, 129], BF16)
        zT = ss("zT", [128, 2, NT], BF16)
        YZb = ss("YZb", [128, 2, 4, 128], BF16)
        wgB = ss("wgB", [128, 2, 256], BF16)
        h0S = ss("h0S", [128, 2, 8, 16])
        h1S = ss("h1S", [128, 2, 8, 16])
        h1b = ss("h1b", [128, 2, 8, 16], BF16)
        d5 = ss("d5", [128, 2])
        bg5 = ss("bg5", [128, 2])
        esS.close()
        tmpX = zT[:, 0, 0:2048].bitcast(F32).rearrange("p (a n) -> p a n", a=8)
        ytmp = ZA[:, 0, 0:4, :].rearrange("p a n -> p (a n)")
        gtmp = ZA[:, 0, 4:8, :].rearrange("p a n -> p (a n)")
        wgS = c2S[:, 0:4, :].rearrange("p a n -> p (a n)").rearrange("p (k n) -> p k n", k=2)

        pb = [es.enter_context(nc.psum_tensor("pb%d" % i, [128, 512], F32)) for i in range(7)]
        ptb = es.enter_context(nc.psum_tensor("ptb", [128, 1024], BF16))

        rr = {'a': 0}

        def nxt(lo, n):
            rr['a'] += 1
            return lo + (rr['a'] % n)

        def mm(out, lhsT, rhs, start, stop, r, w):
            P.op('pe', lambda e: e.matmul(out, lhsT=lhsT, rhs=rhs, start=start, stop=stop), reads=r, writes=w)

        def act(out, in_, func, r, w, bias=None, scale=None, accum_out=None):
            kw = {}
            if bias is not None:
                kw['bias'] = bias
            if scale is not None:
                kw['scale'] = scale
            if accum_out is not None:
                kw['accum_out'] = accum_out
            P.op('act', lambda e: e.activation(out=out, in_=in_, func=func, **kw), reads=r, writes=w)

        def tt(out, in0, in1, op, r, w, eng='dve'):
            P.op(eng, lambda e: e.tensor_tensor(out=out, in0=in0, in1=in1, op=op), reads=r, writes=w)

        def ts(out, in0, s1, s2, op0, op1, r, w, eng='dve'):
            if op1 is None:
                P.op(eng, lambda e: e.tensor_scalar(out=out, in0=in0, scalar1=s1, scalar2=None, op0=op0), reads=r, writes=w)
            else:
                P.op(eng, lambda e: e.tensor_scalar(out=out, in0=in0, scalar1=s1, scalar2=s2, op0=op0, op1=op1), reads=r, writes=w)

        def stt(out, in0, scalar, in1, op0, op1, r, w, eng='dve'):
            P.op(eng, lambda e: e.scalar_tensor_tensor(out=out, in0=in0, scalar=scalar, in1=in1, op0=op0, op1=op1),
                 reads=r, writes=w)

        def cp(out, in_, r, w, eng='dve'):
            if eng == 'act':
                act(out, in_, AF.Copy, r, w)
            else:
                P.op(eng, lambda e: e.tensor_copy(out=out, in_=in_), reads=r, writes=w)

        def dma(eng, out, in_, r, w):
            P.dma(eng, lambda e: e.dma_start(out=out, in_=in_), reads=r, writes=w)

        def dma_nc(eng, out, in_, r, w):
            P.dma(eng, lambda e: e.dma_start(out=out, in_=in_, allow_slow_non_contiguous=True), reads=r, writes=w)

        for c in range(8):
            dma('sp', xT[:, c, :], xT_in[c * 128:(c + 1) * 128, :], [], ['x%d_%d' % (c, g) for g in range(NG)])
        dma('sp', gmix[:], g_mix, [], ['gmix'])
        dma('sp', gmlp[:], g_mlp, [], ['gmlp'])
        dma('sp', gfin[:], g_fin, [], ['gfin'])
        dma('pool', identb[:], c_ident, [], ['identb'])
        dma('pool', onesb[:], c_ones, [], ['onesb'])
        dma('sp', identf[:], c_ident, [], ['identf'])
        dma('sp', onesf[:], c_ones, [], ['onesf'])
        dma('sp', tglob[:], c_tglob, [], ['tglob'])
        dma('sp', iotaS[:], c_iota, [], ['iotaS'])
        dma('sp', bdS[:], c_bd, [], ['bdS'])
        dma('sp', tloc[:], c_tloc, [], ['tloc'])
        dma('sp', ltg[:], c_ltg, [], ['ltg'])
        dma('sp', eaM[:], c_ea, [], ['eaM'])
        dma('sp', lbl[:], hg_lb_logits, [], ['lbl'])
        dma('sp', cwS[:], ssd_cw, [], ['cwS'])
        dma('sp', cbS[:], ssd_cb, [], ['cbS'])
        dma('sp', srow[:], ssd_rows, [], ['srow'])
        act(arow[:], srow[:, :, 1, :], AF.Exp, ['srow'], ['arow'])
        ts(arow[:], arow[:], -1.0, None, ALU.mult, None, ['arow'], ['arow'])
        act(lbl[:], lbl[:], AF.Exp, ['lbl'], ['lbl'])
        tt(lbt[:], lbl[:, 0, :], lbl[:, 1, :], ALU.add, ['lbl'], ['lbt'])
        tt(lbt[:], lbt[:], lbl[:, 2, :], ALU.add, ['lbl', 'lbt'], ['lbt'])
        tt(lbt[:], lbt[:], lbl[:, 3, :], ALU.add, ['lbl', 'lbt'], ['lbt'])
        P.op('dve', lambda e: e.reciprocal(out=lbt[:], in_=lbt[:]), reads=['lbt'], writes=['lbt'])
        for l in range(DEPTH):
            tt(lbl[:, l, :], lbl[:, l, :], lbt[:], ALU.mult, ['lbl', 'lbt'], ['lbl'])
        tt(lbl[:, 3, :], lbl[:, 3, :], lbl[:, 2, :], ALU.add, ['lbl'], ['lbl'])
        tt(lbl[:, 3, :], lbl[:, 3, :], lbl[:, 1, :], ALU.add, ['lbl'], ['lbl'])
        tt(lbl[:, 2, :], lbl[:, 2, :], lbl[:, 1, :], ALU.add, ['lbl'], ['lbl'])
        P.op('dve', lambda e: e.memset(lbl[:, 0, :], 0.0), reads=['lbl'], writes=['lbl'])

        WIN_OFF = 0
        WOUT_OFF = 8 * NIN
        win_v = W[:, WIN_OFF:WIN_OFF + 8 * NIN].rearrange("p (c n) -> p c n", c=8)
        wout_v = W[:, WOUT_OFF:WOUT_OFF + 8 * D].rearrange("p (c n) -> p c n", c=8)
        IN_CHUNKS = [(0, 1024), (1024, 2048), (2048, NIN)]

        def load_phaseA_weights(l):
            for i, (a, b) in enumerate(IN_CHUNKS):
                dma('pool', win_v[:, :, a:b], w_in[l, :, :, a:b], [], ['win%d' % i])
            dma('pool', wout_v[:, :, :], w_out[l, :, :, :], [], ['wout'])

        def mlp_views(b):
            off = (b % 2) * 16384
            up = W[:, off:off + 8192].rearrange("p (c n) -> p c n", c=8)
            dn = W[:, off + 8192:off + 16384].rearrange("p (c n) -> p c n", c=8)
            return up, dn

        def load_mlp_block(l, b):
            up, dn = mlp_views(b)
            h = b % 2
            dma('pool', up[:, :, :], w_up[l, :, :, b * 1024:(b + 1) * 1024], [], ['mup%d' % h])
            dma('pool', dn[:, :, :], w_down[l, :, b * 8:(b + 1) * 8, :], [], ['mdn%d' % h])

        def rmsnorm_group(g, gain_ap_fn, gkey, out_T, okey, oc0):
            t0, n = GROUPS[g]
            for c in range(8):
                act(out_T[:, c, oc0:oc0 + n], xT[:, c, t0:t0 + n], AF.Square, ['x%d_%d' % (c, g)], [okey + '%d' % c])
            bk = nxt(0, 2)
            for c in range(8):
                mm(pb[bk][:, :n], onesb[:], out_T[:, c, oc0:oc0 + n], c == 0, c == 7, ['onesb', okey + '%d' % c], ['pb%d' % bk])
            act(rstd[:, :n], pb[bk][:, :n], AF.Sqrt, ['pb%d' % bk], ['rstd'], bias=EPS, scale=1.0 / D)
            P.op('dve', lambda e: e.reciprocal(out=rstd[:, :n], in_=rstd[:, :n]), reads=['rstd'], writes=['rstd'])
            for c in range(8):
                stt(out_T[:, c, oc0:oc0 + n], xT[:, c, t0:t0 + n], gain_ap_fn(c), rstd[:, :n], ALU.mult, ALU.mult,
                    ['x%d_%d' % (c, g), gkey, 'rstd'], [okey + '%d' % c])

        def win_key(a, b):
            ks = []
            for i, (ca, cb) in enumerate(IN_CHUNKS):
                if a < cb and b > ca:
                    ks.append('win%d' % i)
            return ks

        TOK_GROUPS = [(256, 512, 0), (1024, 1536, 256), (1536, 2048, 768), (2048, 2560, 1280), (2560, NIN, 1792)]

        class BS:
            pass

        B_HG = BS()
        B_HG.pfx = ''
        B_HG.ex = ex
        B_HG.qgB, B_HG.qlocB, B_HG.kaB, B_HG.khB = qgB, qlocB, kaB, khB
        B_HG.qgT, B_HG.kaT, B_HG.sT = qgT, kaT, sT
        B_HG.oS, B_HG.o2S, B_HG.ssq, B_HG.elast = oS, o2S, ssq, elast
        B_HG.banks = [4, 5]
        B_HG.obank = 6
        B_HG.vkey = 'vB'
        B_HG.evac_eng = 'dve'
        B_GL = BS()
        B_GL.pfx = 'g_'
        B_GL.ex = [g_ex[:, 0:128], g_ex[:, 128:256]]
        B_GL.qgB, B_GL.qlocB, B_GL.kaB, B_GL.khB = g_b3[:, 0, :], None, [g_b3[:, 1, :]], g_b3[:, 2, :]
        B_GL.qgT, B_GL.kaT, B_GL.sT = g_qgT, g_kaT, g_sT
        B_GL.oS, B_GL.o2S, B_GL.ssq, B_GL.elast = g_oS, g_o2S, g_ssq, g_elast
        B_GL.banks = [2]
        B_GL.obank = 3
        B_GL.vkey = 'g_vB'
        B_GL.evac_eng = 'act'
        brr = {'n': 0}

        def brot(B):
            brr['n'] += 1
            return B.banks[brr['n'] % len(B.banks)]

        def gla_like(B, name, l, H, K, FT, q_ap, k_ap, v_ap, la_ap, escale, nsub, srcs):
            HK = H * K
            NF = HK // FT
            st = hst[name]
            px = B.pfx
            exk = [px + 'ex0', px + 'ex1']
            okey = 'pb%d' % B.obank

            def cum_exp(mat_ap, mkey, ei, neg=False):
                bk = brot(B)
                mm(pb[bk][:, :HK], mat_ap, la_ap, True, True, [mkey] + srcs, ['pb%d' % bk])
                act(B.ex[ei][:, :HK], pb[bk][:, :HK], AF.Exp, ['pb%d' % bk], [exk[ei]], scale=(-escale if neg else escale))

            cum_exp(tglob[:], 'tglob', 0)
            tt(B.qgB[:, :HK], q_ap, B.ex[0][:, :HK], ALU.mult, [exk[0]] + srcs, [px + 'qgB'])
            yield
            if nsub > 1:
                cum_exp(tloc[:], 'tloc', 1)
                tt(B.qlocB[:, :HK], q_ap, B.ex[1][:, :HK], ALU.mult, [exk[1]] + srcs, [px + 'qlocB'])
                yield
            for a in range(nsub):
                ei = a % 2
                if nsub > 1:
                    cum_exp(eaM[:, a, :], 'eaM', ei)
                else:
                    cum_exp(tglob[:], 'tglob', ei, neg=True)
                tt(B.kaB[a][:, :HK], k_ap, B.ex[ei][:, :HK], ALU.mult, [exk[ei]] + srcs, [px + 'kaB%d' % a])
                yield
            ei = nsub % 2
            cum_exp(ltg[:], 'ltg', ei)
            tt(B.khB[:, :HK], k_ap, B.ex[ei][:, :HK], ALU.mult, [exk[ei]] + srcs, [px + 'khB'])
            bk = brot(B)
            for f in range(NF):
                mm(pb[bk][:FT, f:f + 1], la_ap[:, f * FT:(f + 1) * FT], onesf[:, 0:1], True, True, srcs + ['onesf'], ['pb%d' % bk])
            act(B.elast[:FT, :NF], pb[bk][:FT, :NF], AF.Exp, ['pb%d' % bk], [px + 'elast'], scale=escale)
            yield
            qz = qlz[name]
            slot = 0
            tl = []
            items = [(B.qgB, px + 'qgB', 'qg', None)]
            if nsub > 1:
                items.append((B.qlocB, px + 'qlocB', 'ql', None))
            for a in range(nsub):
                items.append((B.kaB[a], px + 'kaB%d' % a, 'ka', a))

            def evac(s_, kind, f, a):
                src = ptb[:FT, s_ * 128:(s_ + 1) * 128]
                if kind == 'ka':
                    cp(B.kaT[:FT, a, f, :], src, ['ptb'], [px + 'kaT'], eng=B.evac_eng)
                    return
                if kind == 'qg':
                    cp(B.qgT[:FT, f, :], src, ['ptb'], [px + 'qgT'], eng=B.evac_eng)
                if kind == 'ql' or (kind == 'qg' and nsub == 1):
                    for h in range(H):
                        if (h * K) // FT != f:
                            continue
                        r0 = (h * K) % FT
                        cp(qz[r0:r0 + K, h, :], ptb[r0:r0 + K, s_ * 128:(s_ + 1) * 128], ['ptb'], ['qlz_' + name],
                           eng=B.evac_eng)

            for (src, skey, kind, a) in items:
                for f in range(NF):
                    P.op('pe', lambda e, s_=slot, src=src, f=f: e.transpose(ptb[:FT, s_ * 128:(s_ + 1) * 128], src[:, f * FT:(f + 1) * FT], identb[:]),
                         reads=[skey, 'identb'], writes=['ptb'])
                    tl.append((slot, kind, f, a))
                    slot += 1
                    if slot == 8:
                        for it in tl:
                            evac(*it)
                        tl = []
                        slot = 0
                        yield
            for it in tl:
                evac(*it)
            yield
            sci = brot(B)
            sc = pb[sci]
            sck = 'pb%d' % sci
            CS = 128 // nsub
            for h in range(H):
                f = (h * K) // FT
                for a in range(nsub):
                    mm(sc[:, h * 128 + a * CS: h * 128 + (a + 1) * CS], B.kaT[:FT, a, f, :], qz[:FT, h, a * CS:(a + 1) * CS],
                       True, True, [px + 'kaT', 'qlz_' + name], [sck])
            tt(B.sT[:, :H * 128].rearrange("p (h i) -> p h i", h=H), sc[:, :H * 128].rearrange("p (h i) -> p h i", h=H),
               tglob[:].unsqueeze(1).to_broadcast([128, H, 128]), ALU.mult, [sck, 'tglob'], [px + 'sT'])
            yield
            sz = stz[name]
            for h in range(H):
                f = (h * K) // FT
                mm(pb[B.obank][:, h * 64:(h + 1) * 64], B.sT[:, h * 128:(h + 1) * 128], v_ap[:, h * 64:(h + 1) * 64], True, False,
                   [px + 'sT', B.vkey], [okey])
                mm(pb[B.obank][:, h * 64:(h + 1) * 64], B.qgT[:FT, f, :], sz[:FT, h, :], False, True,
                   [px + 'qgT', 'stz_' + name], [okey])
            yield
            ubi = brot(B)
            ub = pb[ubi]
            ubk = 'pb%d' % ubi
            for h in range(H):
                f = (h * K) // FT
                mm(ub[:FT, h * 64:(h + 1) * 64], B.khB[:, f * FT:(f + 1) * FT], v_ap[:, h * 64:(h + 1) * 64], True, True,
                   [px + 'khB', B.vkey], [ubk])
            for h in range(H):
                f = (h * K) // FT
                r0 = (h * K) % FT
                stt(st[r0:r0 + K, f, :], st[r0:r0 + K, f, :], B.elast[r0:r0 + K, f:f + 1], ub[r0:r0 + K, h * 64:(h + 1) * 64],
                    ALU.mult, ALU.add, ['hst_' + name, px + 'elast', ubk], ['hst_' + name])
            for h in range(H):
                f = (h * K) // FT
                r0 = (h * K) % FT
                cp(sz[r0:r0 + K, h, :], st[r0:r0 + K, f, :], ['hst_' + name], ['stz_' + name], eng=B.evac_eng)
            yield

        def ssd_tile(l, c0):
            tsl = slice(c0, c0 + 128)
            for cc in range(3):
                P.op('pe', lambda e, cc=cc: e.transpose(ptb[:, cc * 128:(cc + 1) * 128], xsT[:, cc, tsl], identb[:]),
                     reads=['xsT', 'identb'], writes=['ptb'])
            cp(xtok[:], ptb[:, 0:256], ['ptb'], ['xtok'], eng='act')
            cp(btok[:], ptb[:, 256:384], ['ptb'], ['btok'], eng='act')
            tt(dtS[:, 0:4], pj[:, 256:260], srow[:, l, 0, :], ALU.add, ['pj', 'srow'], ['dtS'])
            act(dtS[:, 0:4], dtS[:, 0:4], AF.Exp, ['dtS'], ['dtS'])
            act(dtS[:, 0:4], dtS[:, 0:4], AF.Ln, ['dtS'], ['dtS'], bias=1.0)
            tt(dtS[:, 4:8], dtS[:, 0:4], arow[:, l, :], ALU.mult, ['dtS', 'arow'], ['dtS'])
            la = dtS[:, 4:8]
            yield
            bk = nxt(4, 2)
            mm(pb[bk][:, 0:4], tglob[:], la, True, True, ['tglob', 'dtS'], ['pb%d' % bk])
            mm(pb[bk][:, 4:8], ltg[:], la, True, True, ['ltg', 'dtS'], ['pb%d' % bk])
            mm(pb[bk][:, 8:12], onesf[:], la, True, True, ['onesf', 'dtS'], ['pb%d' % bk])
            act(ecw[:], pb[bk][:, 0:12], AF.Exp, ['pb%d' % bk], ['ecw'])
            yield
            tt(sdR[:].rearrange("p (h i) -> p h i", h=4), tglob[:].unsqueeze(1).to_broadcast([128, 4, 128]),
               la.unsqueeze(2).to_broadcast([128, 4, 128]), ALU.mult, ['tglob', 'dtS'], ['laS', 'qS'])
            bk = nxt(4, 2)
            mm(pb[bk][:, :512], ltg[:], sdR[:], True, True, ['ltg', 'laS', 'qS'], ['pb%d' % bk])
            act(sdD[:], pb[bk][:, :512], AF.Exp, ['pb%d' % bk], ['ex0', 'ex1'])
            tt(sdD[:].rearrange("p (h i) -> p h i", h=4), sdD[:].rearrange("p (h i) -> p h i", h=4),
               tglob[:].unsqueeze(1).to_broadcast([128, 4, 128]), ALU.mult, ['ex0', 'ex1', 'tglob'], ['ex0', 'ex1'])
            yield
            sci = nxt(4, 2)
            for gg in range(2):
                mm(pb[sci][:, gg * 128:(gg + 1) * 128], xsT[:, 2, tsl], Cz[:, gg, tsl], True, True, ['xsT', 'Cz'], ['pb%d' % sci])
            for gg in range(2):
                tt(sT[:, gg * 256:(gg + 1) * 256].rearrange("p (r i) -> p r i", r=2),
                   pb[sci][:, gg * 128:(gg + 1) * 128].unsqueeze(1).to_broadcast([128, 2, 128]),
                   sdD[:, gg * 256:(gg + 1) * 256].rearrange("p (r i) -> p r i", r=2), ALU.mult, ['pb%d' % sci, 'ex0', 'ex1'], ['sT'])
            yield
            tt(vB[:].rearrange("p (h v) -> p h v", h=4), xtok[:].rearrange("p (h v) -> p h v", h=4),
               dtS[:, 0:4].unsqueeze(2).to_broadcast([128, 4, 64]), ALU.mult, ['xtok', 'dtS'], ['vB'])
            for h in range(4):
                gg = h // 2
                ts(BhatZ[:, h, 64 * gg:64 * gg + 64], btok[:, 64 * gg:64 * gg + 64], ecw[:, 4 + h:5 + h], None, ALU.mult, None,
                   ['btok', 'ecw'], ['BhatZ'])
            yield
            for h in range(4):
                mm(pb[6][:, h * 64:(h + 1) * 64], sT[:, h * 128:(h + 1) * 128], vB[:, h * 64:(h + 1) * 64], True, True,
                   ['sT', 'vB'], ['pb6'])
            p2 = nxt(4, 2)
            for h in range(4):
                mm(pb[p2][:, h * 64:(h + 1) * 64], Cz[:, h // 2, tsl], stb_ssd[:, h % 2, :], True, True, ['Cz', 'stb_ssd'], ['pb%d' % p2])
            ubi = nxt(4, 2)
            for r in range(2):
                for gg in range(2):
                    h = 2 * gg + r
                    mm(pb[ubi][:, r * 64:(r + 1) * 64], BhatZ[:, h, :], vB[:, h * 64:(h + 1) * 64], gg == 0, gg == 1,
                       ['BhatZ', 'vB'], ['pb%d' % ubi])
            yield 'front_done'
            yield from ssd_tail(l, p2, ubi)

        def ssd_tail(l, p2, ubi):
            tt(oS[:].rearrange("p (h v) -> p h v", h=4), pb[p2][:, :256].rearrange("p (h v) -> p h v", h=4),
               ecw[:, 0:4].unsqueeze(2).to_broadcast([128, 4, 64]), ALU.mult, ['pb%d' % p2, 'ecw'], ['oS'])
            for r in range(2):
                for gg in range(2):
                    h = 2 * gg + r
                    ps_ = slice(64 * gg, 64 * gg + 64)
                    stt(st_ssd[ps_, r, :], st_ssd[ps_, r, :], ecw[ps_, 8 + h:9 + h], pb[ubi][ps_, r * 64:(r + 1) * 64], ALU.mult, ALU.add,
                        ['st_ssd', 'ecw', 'pb%d' % ubi], ['st_ssd'])
            yield
            tt(oS[:], oS[:], pb[6][:, :256], ALU.add, ['oS', 'pb6'], ['oS'])
            cp(stb_ssd[:], st_ssd[:], ['st_ssd'], ['stb_ssd'], eng='act')
            tt(o2S[:].rearrange("p (h v) -> p h v", h=4), xtok[:].rearrange("p (h v) -> p h v", h=4),
               srow[:, l, 2, :].unsqueeze(2).to_broadcast([128, 4, 64]), ALU.mult, ['xtok', 'srow'], ['o2S'])
            tt(oS[:], oS[:], o2S[:], ALU.add, ['oS', 'o2S'], ['oS'])
            yield
            act(o2S[:], pj[:, 0:256], AF.Silu, ['pj'], ['o2S'])
            tt(oS[:], oS[:], o2S[:], ALU.mult, ['oS', 'o2S'], ['oS'])
            yield
            tt(o2S[:], oS[:], oS[:], ALU.mult, ['oS'], ['o2S'])
            P.op('dve', lambda e: e.tensor_reduce(out=ssq[:, 0:1], in_=o2S[:], axis=AX.X, op=ALU.add), reads=['o2S'], writes=['ssq'])
            yield
            act(ssq[:, 0:1], ssq[:, 0:1], AF.Sqrt, ['ssq'], ['ssq'], bias=EPS, scale=1.0 / 256)
            P.op('dve', lambda e: e.reciprocal(out=ssq[:, 0:1], in_=ssq[:, 0:1]), reads=['ssq'], writes=['ssq'])
            yield
            stt(mixtok[:, 256:512], oS[:], ssq[:, 0:1], ssdng[:], ALU.mult, ALU.mult, ['oS', 'ssq', 'ssdng'], ['mixtok'])
            yield

        def head_norm_gate_gen(l, gvec_ap, gkey, gate_ap, dst_ap, npart=128, from_psum=True, B=None):
            B = B or B_HG
            px = B.pfx
            oS_, o2S_, ssq_ = B.oS, B.o2S, B.ssq
            ko, ko2, ks = px + 'oS', px + 'o2S', px + 'ssq'
            if from_psum:
                cp(oS_[:npart], pb[B.obank][:npart, :256], ['pb%d' % B.obank], [ko], eng='act')
            tt(o2S_[:npart], oS_[:npart], oS_[:npart], ALU.mult, [ko], [ko2])
            P.op('dve', lambda e: e.tensor_reduce(out=ssq_[:npart], in_=o2S_[:npart].rearrange("p (h v) -> p h v", h=4), axis=AX.X, op=ALU.add),
                 reads=[ko2], writes=[ks])
            yield
            act(ssq_[:npart], ssq_[:npart], AF.Sqrt, [ks], [ks], bias=EPS, scale=1.0 / 64)
            P.op('dve', lambda e: e.reciprocal(out=ssq_[:npart], in_=ssq_[:npart]), reads=[ks], writes=[ks])
            yield
            tt(oS_[:npart].rearrange("p (h v) -> p h v", h=4), oS_[:npart].rearrange("p (h v) -> p h v", h=4),
               ssq_[:npart].unsqueeze(2).to_broadcast([npart, 4, 64]), ALU.mult, [ko, ks], [ko])
            tt(oS_[:npart].rearrange("p (h v) -> p h v", h=4), oS_[:npart].rearrange("p (h v) -> p h v", h=4), gvec_ap, ALU.mult, [ko, gkey], [ko])
            yield
            act(o2S_[:npart], gate_ap, AF.Silu, ['pj'], [ko2])
            tt(dst_ap, oS_[:npart], o2S_[:npart], ALU.mult, [ko, ko2], ['mixtok'])

        def head_norm_gate(*a_, **k_):
            for _ in head_norm_gate_gen(*a_, **k_):
                pass

        Yexp = W[:, 0:8704].rearrange("p (m r q c) -> p m r q c", m=17, r=2, q=8)
        Kblk = W[:, 8704:12800].rearrange("p (m h c) -> p m h c", m=16, h=2)
        WinZ = W[:, 12800:20992].rearrange("p (m r q c) -> p m r q c", m=4, r=2, q=8)
        MAGIC = 12582912.0
        TWO_PI = 6.283185307179586

        def sin_of(ang_ap, out_ap, tmp_ap, tmp2_ap, shift, rk, wk):
            ts(tmp_ap, ang_ap, 1.0 / TWO_PI, shift, ALU.mult, ALU.add, rk, wk)
            ts(tmp2_ap, tmp_ap, MAGIC, None, ALU.add, None, wk, wk)
            ts(tmp2_ap, tmp2_ap, -MAGIC, None, ALU.add, None, wk, wk)
            tt(tmp_ap, tmp_ap, tmp2_ap, ALU.subtract, wk, wk)
            ts(tmp_ap, tmp_ap, 0.4999995, -0.4999995, ALU.min, ALU.max, wk, wk)
            act(out_ap, tmp_ap, AF.Sin, wk, wk, scale=TWO_PI)

        def s5_phase(l):
            TB = lambda k: tab[:, k, :, :]
            kk = ['s5t']
            dma('sp', lamS[:], s5_lam[:, l, :, :], [], ['lamS'])
            dma('sp', t8[:, 0, :], s5_ldt[:, l, :], [], kk)
            dma('sp', Bq[:], s5_B[:, l], [], ['Bq'])
            dma('sp', Cq[:], s5_C[:, l], [], ['Cq'])
            dma('sp', d5[:], s5_d[:, l, :], [], ['d5'])
            dma('sp', bg5[:], s5_bglu[:, l, :], [], ['bg5'])
            for ri in range(2):
                dma('sp', h0S[:, ri], s5_h0[l, ri], [], ['h0S'])
            P.op('pool', lambda e: e.memset(W[:, 0:8704], 0.0), writes=['Yexp'])
            P.op('pool', lambda e: e.memset(Xexp[:], 0.0), writes=['Xexp'])
            P.op('pool', lambda e: e.memset(XZ[:], 0.0), writes=['XZ'])
            P.op('pool', lambda e: e.memset(YZb[:], 0.0), writes=['YZb0', 'YZb1'])
            act(t8[:, 0, :], t8[:, 0, :], AF.Exp, kk, kk)
            tt(t8[:, 1, :], lamS[:, 0, :], t8[:, 0, :], ALU.mult, kk + ['lamS'], kk)
            tt(t8[:, 2, :], lamS[:, 1, :], t8[:, 0, :], ALU.mult, kk + ['lamS'], kk)
            iv = iotaS[:, 0:17].unsqueeze(2).to_broadcast([128, 17, 8])
            tt(TB(0), iv, t8[:, 1, :].unsqueeze(1).to_broadcast([128, 17, 8]), ALU.mult, kk + ['iotaS'], kk)
            act(TB(0), TB(0), AF.Exp, kk, kk)
            tt(TB(1), iv, t8[:, 2, :].unsqueeze(1).to_broadcast([128, 17, 8]), ALU.mult, kk + ['iotaS'], kk)
            sin_of(TB(1), TB(3), TB(2), TB(5), 0.0, kk, kk)
            sin_of(TB(1), TB(4), TB(2), TB(5), 0.25, kk, kk)
            tt(TB(5), TB(0), TB(4), ALU.mult, kk, kk)
            tt(TB(6), TB(0), TB(3), ALU.mult, kk, kk)
            ts(TB(7), TB(6), -1.0, None, ALU.mult, None, kk, kk)
            lr1, li1 = tab[:, 5, 1, :], tab[:, 6, 1, :]
            ts(t8[:, 3, :], lr1, -1.0, None, ALU.add, None, kk, kk)
            tt(t8[:, 4, :], lamS[:, 0, :], lamS[:, 0, :], ALU.mult, ['lamS'], kk)
            tt(t8[:, 5, :], lamS[:, 1, :], lamS[:, 1, :], ALU.mult, ['lamS'], kk)
            tt(t8[:, 4, :], t8[:, 4, :], t8[:, 5, :], ALU.add, kk, kk)
            P.op('dve', lambda e: e.reciprocal(out=t8[:, 4, :], in_=t8[:, 4, :]), reads=kk, writes=kk)
            tt(t8[:, 5, :], t8[:, 3, :], lamS[:, 0, :], ALU.mult, kk + ['lamS'], kk)
            tt(t8[:, 6, :], li1, lamS[:, 1, :], ALU.mult, kk + ['lamS'], kk)
            tt(t8[:, 5, :], t8[:, 5, :], t8[:, 6, :], ALU.add, kk, kk)
            tt(t8[:, 5, :], t8[:, 5, :], t8[:, 4, :], ALU.mult, kk, kk)
            tt(t8[:, 6, :], li1, lamS[:, 0, :], ALU.mult, kk + ['lamS'], kk)
            tt(t8[:, 7, :], t8[:, 3, :], lamS[:, 1, :], ALU.mult, kk + ['lamS'], kk)
            tt(t8[:, 6, :], t8[:, 6, :], t8[:, 7, :], ALU.subtract, kk, kk)
            tt(t8[:, 6, :], t8[:, 6, :], t8[:, 4, :], ALU.mult, kk, kk)
            kr = t8[:, 5, :].unsqueeze(2).to_broadcast([128, 8, 16])
            ki = t8[:, 6, :].unsqueeze(2).to_broadcast([128, 8, 16])
            tk = ['t16']
            tt(t16[:, 0], Bq[:, 0], kr, ALU.mult, ['Bq'] + kk, tk)
            tt(t16[:, 1], Bq[:, 1], ki, ALU.mult, ['Bq'] + kk, tk)
            tt(Bbar[:, 0], t16[:, 0], t16[:, 1], ALU.subtract, tk, ['Bbar'])
            tt(t16[:, 0], Bq[:, 1], kr, ALU.mult, ['Bq'] + kk, tk)
            tt(t16[:, 1], Bq[:, 0], ki, ALU.mult, ['Bq'] + kk, tk)
            tt(Bbar[:, 1], t16[:, 0], t16[:, 1], ALU.add, tk, ['Bbar'])
            zk = ['ZA', 'ZB']

            def cexp(dst, nm, m0, m1, A, Ak, conj_second):
                nb = m1 - m0
                pa = ZA[:].rearrange("p a b c -> p (a b c)")[:, 0:nb * 128].rearrange("p (m q c) -> p m q c", m=nb, q=8)
                pb_ = ZB[:].rearrange("p a b c -> p (a b c)")[:, 0:nb * 128].rearrange("p (m q c) -> p m q c", m=nb, q=8)
                A0 = A[:, 0].unsqueeze(1).to_broadcast([128, nb, 8, 16])
                A1 = A[:, 1].unsqueeze(1).to_broadcast([128, nb, 8, 16])
                lrm = tab[:, 5, m0:m1, :].unsqueeze(3).to_broadcast([128, nb, 8, 16])
                lim = tab[:, 6, m0:m1, :].unsqueeze(3).to_broadcast([128, nb, 8, 16])
                nlim = tab[:, 7, m0:m1, :].unsqueeze(3).to_broadcast([128, nb, 8, 16])
                tt(pa, A0, lrm, ALU.mult, [Ak] + kk, zk)
                tt(pb_, A1, lim, ALU.mult, [Ak] + kk, zk)
                for hf in range(2):
                    ps_ = slice(64 * hf, 64 * hf + 64)
                    tt(dst[ps_, m0:m1, 0, :, 16 * hf:16 * hf + 16], pa[ps_], pb_[ps_], ALU.subtract, zk, [nm])
                if conj_second:
                    tt(pa, A0, nlim, ALU.mult, [Ak] + kk, zk)
                    tt(pb_, A1, lrm, ALU.mult, [Ak] + kk, zk)
                    op2 = ALU.subtract
                else:
                    tt(pa, A1, lrm, ALU.mult, [Ak] + kk, zk)
                    tt(pb_, A0, lim, ALU.mult, [Ak] + kk, zk)
                    op2 = ALU.add
                for hf in range(2):
                    ps_ = slice(64 * hf, 64 * hf + 64)
                    tt(dst[ps_, m0:m1, 1, :, 16 * hf:16 * hf + 16], pa[ps_], pb_[ps_], op2, zk, [nm])

            cexp(Xexp, 'Xexp', 0, 4, Bbar, 'Bbar', False)
            cexp(Yexp, 'Yexp', 0, 9, Cq, 'Cq', True)
            cexp(Yexp, 'Yexp', 9, 17, Cq, 'Cq', True)
            for ph in range(2):
                for mg in range(4):
                    bk = nxt(0, 2)
                    for k4 in range(4):
                        m = mg * 4 + k4
                        for ri in range(2):
                            mm(pb[bk][:, k4 * 128:(k4 + 1) * 128], Xexp[:, 0, ri, 4 * ph:4 * ph + 4, :].rearrange("p q c -> p (q c)"),
                               Yexp[:, m, ri, 4 * ph:4 * ph + 4, :].rearrange("p q c -> p (q c)"), ri == 0, ri == 1, ['Xexp', 'Yexp'], ['pb%d' % bk])
                    tt(Kblk[:, mg * 4:mg * 4 + 4, ph, :], pb[bk][:, :512].rearrange("p (k c) -> p k c", k=4),
                       bdS[:].unsqueeze(1).to_broadcast([128, 4, 128]), ALU.mult, ['pb%d' % bk, 'bdS'], ['Kblk'])
            slot = 0
            pend = []
            for m in range(4):
                for ri in range(2):
                    for q in range(8):
                        p4 = q % 4
                        cp(XZ[:, p4, 32 * p4:32 * p4 + 32], Xexp[:, m, ri, q, :], ['Xexp'], ['XZ'], eng='act')
                        P.op('pe', lambda e, s_=slot, p4=p4: e.transpose(ptb[:, s_ * 128:(s_ + 1) * 128], XZ[:, p4, :], identb[:]),
                             reads=['XZ', 'identb'], writes=['ptb'])
                        pend.append((slot, m, ri, q))
                        slot += 1
                        if slot == 8:
                            for (s_, m_, r_, q_) in pend:
                                cp(WinZ[:, m_, r_, q_, :], ptb[:, s_ * 128:(s_ + 1) * 128], ['ptb'], ['WinZ'], eng='dve')
                            pend = []
                            slot = 0
            a2 = ZB[:, 0]
            tt(a2, tab[:, 1, 16, :].unsqueeze(2).to_broadcast([128, 8, 128]), iotaS[:].unsqueeze(1).to_broadcast([128, 8, 128]),
               ALU.mult, kk + ['iotaS'], zk)
            sin_of(a2, s2S[:], ZB[:, 1], tmpX[:], 0.0, zk, zk + ['s2S', 'zT'])
            sin_of(a2, c2S[:], ZB[:, 1], tmpX[:], 0.25, zk, zk + ['c2S', 'zT'])
            for q in range(8):
                ph = q // 4
                uv4 = uT[:, ph, 0:NP_].rearrange("p (n j) -> p j n", j=4)
                zb = 2 * (q % 2)
                kzr, kzi = 'pb%d' % zb, 'pb%d' % (zb + 1)
                for ri in range(2):
                    for j in range(4):
                        mm(pb[zb + ri][:, :512], WinZ[:, 3 - j, ri, q, :], uv4[:, j, :], j == 0, j == 3, ['WinZ', 'uT'], ['pb%d' % (zb + ri)])
                zr = pb[zb][:, :512].rearrange("p (n k) -> p k n", k=4)
                zi = pb[zb + 1][:, :512].rearrange("p (n k) -> p k n", k=4)
                for k in range(4):
                    mk = 4 * (3 - k)
                    lr_, li_, nli_ = tab[:, 5, mk, q:q + 1], tab[:, 6, mk, q:q + 1], tab[:, 7, mk, q:q + 1]
                    if k == 0:
                        ts(ZA[:, 0, q, :], zr[:, k, :], lr_, None, ALU.mult, None, [kzr] + kk, ['ZA'])
                        ts(ZA[:, 1, q, :], zi[:, k, :], lr_, None, ALU.mult, None, [kzi] + kk, ['ZA'])
                    else:
                        stt(ZA[:, 0, q, :], zr[:, k, :], lr_, ZA[:, 0, q, :], ALU.mult, ALU.add, [kzr, 'ZA'] + kk, ['ZA'])
                        stt(ZA[:, 1, q, :], zi[:, k, :], lr_, ZA[:, 1, q, :], ALU.mult, ALU.add, [kzi, 'ZA'] + kk, ['ZA'])
                    stt(ZA[:, 0, q, :], zi[:, k, :], nli_, ZA[:, 0, q, :], ALU.mult, ALU.add, [kzi, 'ZA'] + kk, ['ZA'])
                    stt(ZA[:, 1, q, :], zr[:, k, :], li_, ZA[:, 1, q, :], ALU.mult, ALU.add, [kzr, 'ZA'] + kk, ['ZA'])
            tt(ZB[:, 0], c2S[:], ZA[:, 0], ALU.mult, ['c2S', 'ZA'], ['ZB'])
            tt(tmpX[:], s2S[:], ZA[:, 1], ALU.mult, ['s2S', 'ZA'], ['zT'])
            tt(ZB[:, 0], ZB[:, 0], tmpX[:], ALU.add, ['ZB', 'zT'], ['ZB'])
            tt(ZB[:, 1], c2S[:], ZA[:, 1], ALU.mult, ['c2S', 'ZA'], ['ZB'])
            tt(tmpX[:], s2S[:], ZA[:, 0], ALU.mult, ['s2S', 'ZA'], ['zT'])
            tt(ZB[:, 1], ZB[:, 1], tmpX[:], ALU.subtract, ['ZB', 'zT'], ['ZB'])
            for ri in range(2):
                for q in range(8):
                    P.op('dve', lambda e, ri=ri, q=q: e.tensor_tensor_scan(out=ZA[:, ri, q, :], data0=tab[:, 0, 16, q:q + 1].to_broadcast([128, 128]),
                                                                          data1=ZB[:, ri, q, :], initial=0.0, op0=ALU.mult, op1=ALU.add),
                         reads=['ZB'] + kk, writes=['ZA'])
            tt(ZB[:, 0], c2S[:], ZA[:, 0], ALU.mult, ['c2S', 'ZA'], ['ZB'])
            tt(tmpX[:], s2S[:], ZA[:, 1], ALU.mult, ['s2S', 'ZA'], ['zT'])
            tt(ZB[:, 0], ZB[:, 0], tmpX[:], ALU.subtract, ['ZB', 'zT'], ['ZB'])
            tt(ZB[:, 1], c2S[:], ZA[:, 1], ALU.mult, ['c2S', 'ZA'], ['ZB'])
            tt(tmpX[:], s2S[:], ZA[:, 0], ALU.mult, ['s2S', 'ZA'], ['zT'])
            tt(ZB[:, 1], ZB[:, 1], tmpX[:], ALU.add, ['ZB', 'zT'], ['ZB'])
            P.op('dve', lambda e: e.memset(Sb[:, :, :, 0:1], 0.0), writes=['Sb'])
            cp(Sb[:, :, :, 1:129], ZB[:], ['ZB'], ['Sb'], eng='act')
            for ri in range(2):
                dma_nc('sp', o_s5_p[l, ri], ZB[:, ri, :, 127], ['ZB'], [])
            dma('sp', wgS, s5_wglu[l], ['c2S'], ['c2S'])
            cp(wgB[:], wgS, ['c2S'], ['wgB'], eng='act')
            yzc = {'n': 0}

            def yz_load(m, ri, ph):
                sl = yzc['n'] % 2
                yzc['n'] += 1
                for p4 in range(4):
                    cp(YZb[:, sl, p4, 32 * p4:32 * p4 + 32], Yexp[:, m, ri, 4 * ph + p4, :], ['Yexp'], ['YZb%d' % sl], eng='act' if p4 % 2 else 'pool')
                return sl

            for ri in range(2):
                for q in range(8):
                    mm(pb[4][:, (ri * 8 + q) * 16:(ri * 8 + q + 1) * 16], WinZ[:, 0, ri, q, :], uT[:, q // 4, NP_:NT], True, True,
                       ['WinZ', 'uT'], ['pb4'])
            lr1b = lr1.unsqueeze(2).to_broadcast([128, 8, 16])
            li1b = li1.unsqueeze(2).to_broadcast([128, 8, 16])
            bur = pb[4][:, 0:128].rearrange("p (q b) -> p q b", q=8)
            bui = pb[4][:, 128:256].rearrange("p (q b) -> p q b", q=8)
            tt(t16[:, 0], h0S[:, 0], lr1b, ALU.mult, ['h0S'] + kk, tk)
            tt(t16[:, 1], h0S[:, 1], li1b, ALU.mult, ['h0S'] + kk, tk)
            tt(h1S[:, 0], t16[:, 0], t16[:, 1], ALU.subtract, tk, ['h1S'])
            tt(h1S[:, 0], h1S[:, 0], bur, ALU.add, ['h1S', 'pb4'], ['h1S'])
            tt(t16[:, 0], h0S[:, 1], lr1b, ALU.mult, ['h0S'] + kk, tk)
            tt(t16[:, 1], h0S[:, 0], li1b, ALU.mult, ['h0S'] + kk, tk)
            tt(h1S[:, 1], t16[:, 0], t16[:, 1], ALU.add, tk, ['h1S'])
            tt(h1S[:, 1], h1S[:, 1], bui, ALU.add, ['h1S', 'pb4'], ['h1S'])
            cp(h1b[:], h1S[:], ['h1S'], ['h1b'], eng='act')
            for ri in range(2):
                dma('sp', o_s5_s[l, ri], h1S[:, ri], ['h1S'], [])

            def gelu_to(z_out, n_):
                act(gtmp[:, :n_], ytmp[:, :n_], AF.Square, zk, zk)
                ts(gtmp[:, :n_], gtmp[:, :n_], 0.044715, 1.0, ALU.mult, ALU.add, zk, zk)
                tt(gtmp[:, :n_], gtmp[:, :n_], ytmp[:, :n_], ALU.mult, zk, zk)
                act(gtmp[:, :n_], gtmp[:, :n_], AF.Sigmoid, zk, zk, scale=1.5957691216057308)
                return gtmp

            for ph in range(2):
                first = True
                for ri in range(2):
                    sl = yz_load(0, ri, ph)
                    for p4 in range(4):
                        mm(pb[5][:, ph * 16:(ph + 1) * 16], YZb[:, sl, p4, :], h1b[:, ri, 4 * ph + p4, :], first, (ri == 1 and p4 == 3),
                           ['YZb%d' % sl, 'h1b'], ['pb5'])
                        first = False
                stt(ytmp[:, :NS], uT[:, ph, NP_:NT], d5[:, ph:ph + 1], pb[5][:, ph * 16:(ph + 1) * 16], ALU.mult, ALU.add,
                    ['uT', 'd5', 'pb5'] + zk, zk)
                gelu_to(None, NS)
                tt(zT[:, ph, NP_:NT], ytmp[:, :NS], gtmp[:, :NS], ALU.mult, zk, ['zT'])
            for ph in range(2):
                uv16 = uT[:, ph, 0:NP_].rearrange("p (n j) -> p j n", j=16)
                zv16 = zT[:, ph, 0:NP_].rearrange("p (n j) -> p j n", j=16)
                for i in range(16):
                    bk = i // 4
                    oc = pb[bk][:, (i % 4) * 128:(i % 4 + 1) * 128]
                    for j in range(i + 1):
                        mm(oc, Kblk[:, i - j, ph, :], uv16[:, j, :], j == 0, False, ['Kblk', 'uT'], ['pb%d' % bk])
                    for ri in range(2):
                        sl = yz_load(i + 1, ri, ph)
                        for p4 in range(4):
                            mm(oc, YZb[:, sl, p4, :], Sb[:, ri, 4 * ph + p4, 0:128], False, (ri == 1 and p4 == 3),
                               ['YZb%d' % sl, 'Sb'], ['pb%d' % bk])
                    if i % 4 == 3:
                        yv = ytmp[:, :512].rearrange("p (i n) -> p i n", i=4)
                        stt(yv, uv16[:, 4 * bk:4 * bk + 4, :], d5[:, ph:ph + 1], pb[bk][:, :512].rearrange("p (i n) -> p i n", i=4),
                            ALU.mult, ALU.add, ['uT', 'd5', 'pb%d' % bk] + zk, zk)
                        gelu_to(None, 512)
                        tt(zv16[:, 4 * bk:4 * bk + 4, :], yv, gtmp[:, :512].rearrange("p (i n) -> p i n", i=4), ALU.mult, zk, ['zT'])
            cols = [(i * 512, 512) for i in range(4)] + [(NP_, NS)]
            for co in range(2):
                for (c0_, n_) in cols:
                    bk = nxt(0, 2)
                    for k in range(2):
                        mm(pb[bk][:, :n_], wgB[:, k, co * 128:(co + 1) * 128], zT[:, k, c0_:c0_ + n_], k == 0, k == 1, ['wgB', 'zT'], ['pb%d' % bk])
                    act(gtmp[:, :n_], pb[bk][:, :n_], AF.Sigmoid, ['pb%d' % bk, 'bg5'] + zk, zk, bias=bg5[:, co:co + 1])
                    tt(uT[:, co, c0_:c0_ + n_], zT[:, co, c0_:c0_ + n_], gtmp[:, :n_], ALU.mult, ['zT'] + zk, ['uT'])
            for g in range(NG):
                t0, n = GROUPS[g]
                for m in range(8):
                    bk = nxt(2, 2)
                    for k in range(2):
                        mm(pb[bk][:, :n], wout_v[:, k, m * 128:(m + 1) * 128], uT[:, k, t0:t0 + n], k == 0, k == 1, ['wout', 'uT'], ['pb%d' % bk])
                    tt(xT[:, m, t0:t0 + n], xT[:, m, t0:t0 + n], pb[bk][:, :n], ALU.add, ['x%d_%d' % (m, g), 'pb%d' % bk], ['x%d_%d' % (m, g)])

        I16 = identf[:NS, :NS]

        def smp_update(heads, a_tile, k_ap, v_ap, q_ap, st_in, st_out, o_dst, rk):
            tiles = sorted(set(f for (_, f, _, _) in heads))
            vblk = kaT[:NS].rearrange("p a b c -> p (a b c)")
            kz = qgB[:NS, 0:128]
            aT = xtok[:, 0:NS]
            qzT = xtok[:, NS:2 * NS]
            for f in tiles:
                for (rows, src) in st_in(f):
                    dma('sp', smpS[rows, :, :], src, [], ['smpS'])
                mm(pb[4][:, 0:NS], a_tile(f), I16, True, True, rk + ['identf'], ['pb4'])
                cp(aT, pb[4][:, 0:NS], ['pb4'], ['xtok'])
                tt(smpS[:], smpS[:], aT.unsqueeze(2).to_broadcast([128, NS, 64]), ALU.mult, ['smpS', 'xtok'], ['smpS'])
                hs = [hh for hh in heads if hh[1] == f]
                for idx, (h, _, r0, K) in enumerate(hs):
                    P.op('pool', lambda e: e.memset(kz, 0.0), reads=['qgB'], writes=['qgB'])
                    cp(kz[:, r0:r0 + K], k_ap(h), rk, ['qgB'])
                    tt(vblk.rearrange("p (b v) -> p b v", b=NS), v_ap(h).unsqueeze(1).to_broadcast([NS, NS, 64]),
                       I16.unsqueeze(2).to_broadcast([NS, NS, 64]), ALU.mult, rk + ['identf'], ['kaT'])
                    for hf in range(2):
                        mm(pb[2 + hf][:, :512], kz, vblk[:, hf * 512:(hf + 1) * 512], idx == 0, idx == len(hs) - 1,
                           ['qgB', 'kaT'], ['pb%d' % (2 + hf)])
                sflat = smpS[:].rearrange("p b v -> p (b v)")
                for hf in range(2):
                    tt(sflat[:, hf * 512:(hf + 1) * 512], sflat[:, hf * 512:(hf + 1) * 512], pb[2 + hf][:, :512], ALU.add,
                       ['smpS', 'pb%d' % (2 + hf)], ['smpS'])
                for (rows, dst) in st_out(f):
                    dma('sp', dst, smpS[rows, :, :], ['smpS'], [])
                for (h, _, r0, K) in hs:
                    mm(pb[4][:K, 0:NS], q_ap(h), I16, True, True, rk + ['identf'], ['pb4'])
                    P.op('dve', lambda e: e.memset(qzT, 0.0), reads=['xtok'], writes=['xtok'])
                    cp(qzT[r0:r0 + K, :], pb[4][:K, 0:NS], ['pb4'], ['xtok'])
                    for hf in range(2):
                        mm(pb[2 + hf][:NS, :512], qzT, sflat[:, hf * 512:(hf + 1) * 512], True, True, ['xtok', 'smpS'], ['pb%d' % (2 + hf)])
                    for hf in range(2):
                        tt(dgT[:].rearrange("p (b v) -> p b v", b=8), pb[2 + hf][:NS, :512].rearrange("p (b v) -> p b v", b=8),
                           identf[:NS, hf * 8:hf * 8 + 8].unsqueeze(2).to_broadcast([NS, 8, 64]), ALU.mult, ['pb%d' % (2 + hf), 'identf'], ['dgT'])
                        dst = o_dst[:, h * 64:(h + 1) * 64] if hf == 0 else o2S[:NS, 0:64]
                        P.op('dve', lambda e, dst=dst: e.tensor_reduce(out=dst, in_=dgT[:].rearrange("p (b v) -> p v b", b=8), axis=AX.X, op=ALU.add),
                             reads=['dgT'], writes=['oS', 'o2S'])
                    tt(o_dst[:, h * 64:(h + 1) * 64], o_dst[:, h * 64:(h + 1) * 64], o2S[:NS, 0:64], ALU.add, ['oS', 'o2S'], ['oS'])

        def sample_tile(l):
            R_ = slice(0, NS)
            for (a, b, po) in TOK_GROUPS:
                bk = nxt(0, 2)
                for c in range(8):
                    mm(pb[bk][R_, :b - a], hnT[:, c, 0:NS], win_v[:, c, a:b], c == 0, c == 7, ['hnT%d' % c] + win_key(a, b), ['pb%d' % bk])
                cp(pj[R_, po:po + (b - a)], pb[bk][R_, :b - a], ['pb%d' % bk], ['pj'], eng='act')
            bk = nxt(0, 2)
            for c in range(8):
                mm(pb[bk][R_, :512], hnT[:, c, 0:NS], win_v[:, c, 512:1024], c == 0, c == 7, ['hnT%d' % c] + win_key(512, 1024), ['pb%d' % bk])
            xq = ['laS', 'qS']
            cp(sdR[R_, :], pb[bk][R_, :512], ['pb%d' % bk], xq, eng='act')
            dma('sp', o_conv_s[l, :, 0:2, :], si_conv[l, :, 1:3, :], [], [])
            dma('sp', o_conv_s[l, :, 2, :], sdR[R_, :], xq, [])
            xbf = xbcT[:].rearrange("p c n -> p (c n)")
            wrow, brow = xbf[R_, 0:512], xbf[R_, 512:1024]
            acc = sdD[R_, :]
            ak = ['ex0', 'ex1']
            dma('sp', wrow, ssd_cwr[l, 3], [], ['xbcT'])
            dma('sp', brow, ssd_cbr[l], [], ['xbcT'])
            tt(acc, sdR[R_, :], wrow, ALU.mult, xq + ['xbcT'], ak)
            tt(acc, acc, brow, ALU.add, ak + ['xbcT'], ak)
            for j in range(3):
                dma('sp', wrow, ssd_cwr[l, j], [], ['xbcT'])
                dma('sp', brow, si_conv[l, :, j, :], [], ['xbcT'])
                tt(brow, brow, wrow, ALU.mult, ['xbcT'], ['xbcT'])
                tt(acc, acc, brow, ALU.add, ak + ['xbcT'], ak)
            act(acc, acc, AF.Silu, ak, ak)
            tt(dtS[R_, 0:4], pj[R_, 256:260], srow[R_, l, 0, :], ALU.add, ['pj', 'srow'], ['dtS'])
            act(dtS[R_, 0:4], dtS[R_, 0:4], AF.Exp, ['dtS'], ['dtS'])
            act(dtS[R_, 0:4], dtS[R_, 0:4], AF.Ln, ['dtS'], ['dtS'], bias=1.0)
            tt(dtS[R_, 4:8], dtS[R_, 0:4], arow[R_, l, :], ALU.mult, ['dtS', 'arow'], ['dtS'])
            act(dtS[R_, 4:8], dtS[R_, 4:8], AF.Exp, ['dtS'], ['dtS'])
            vtok = kS[R_, :]
            tt(vtok.rearrange("p (h v) -> p h v", h=4), sdD[R_, 0:256].rearrange("p (h v) -> p h v", h=4),
               dtS[R_, 0:4].unsqueeze(2).to_broadcast([NS, 4, 64]), ALU.mult, ak + ['dtS'], ['kS'])
            aexp = sgS[R_, 0:128]

            def ssd_a(f):
                cp(aexp.rearrange("p (g n) -> p g n", g=2), dtS[R_, 4 + f:8:2].unsqueeze(2).to_broadcast([NS, 2, 64]), ['dtS'], ['sgS'])
                return aexp
            ssd_heads = [(2 * gg + r, r, 64 * gg, 64) for r in range(2) for gg in range(2)]
            smp_update(ssd_heads, ssd_a,
                       lambda h: sdD[R_, 256 + 64 * (h // 2):256 + 64 * (h // 2) + 64],
                       lambda h: vtok[:, h * 64:(h + 1) * 64],
                       lambda h: sdD[R_, 384 + 64 * (h // 2):384 + 64 * (h // 2) + 64],
                       lambda f: [(slice(64 * gg, 64 * gg + 64), si_ssd[l, :, 2 * gg + f].rearrange("b n p -> n b p")) for gg in range(2)],
                       lambda f: [(slice(64 * gg, 64 * gg + 64), o_ssd_s[l, :, 2 * gg + f].rearrange("b n p -> n b p")) for gg in range(2)],
                       oS[R_, :], ak + ['kS', 'sgS', 'dtS'])
            tt(o2S[R_, :].rearrange("p (h v) -> p h v", h=4), sdD[R_, 0:256].rearrange("p (h v) -> p h v", h=4),
               srow[R_, l, 2, :].unsqueeze(2).to_broadcast([NS, 4, 64]), ALU.mult, ak + ['srow'], ['o2S'])
            tt(oS[R_, :], oS[R_, :], o2S[R_, :], ALU.add, ['oS', 'o2S'], ['oS'])
            act(o2S[R_, :], pj[R_, 0:256], AF.Silu, ['pj'], ['o2S'])
            tt(oS[R_, :], oS[R_, :], o2S[R_, :], ALU.mult, ['oS', 'o2S'], ['oS'])
            tt(o2S[R_, :], oS[R_, :], oS[R_, :], ALU.mult, ['oS'], ['o2S'])
            P.op('dve', lambda e: e.tensor_reduce(out=ssq[R_, 0:1], in_=o2S[R_, :], axis=AX.X, op=ALU.add), reads=['o2S'], writes=['ssq'])
            act(ssq[R_, 0:1], ssq[R_, 0:1], AF.Sqrt, ['ssq'], ['ssq'], bias=EPS, scale=1.0 / 256)
            P.op('dve', lambda e: e.reciprocal(out=ssq[R_, 0:1], in_=ssq[R_, 0:1]), reads=['ssq'], writes=['ssq'])
            stt(mixtok[R_, 256:512], oS[R_, :], ssq[R_, 0:1], ssdng[R_, :], ALU.mult, ALU.mult, ['oS', 'ssq', 'ssdng'], ['mixtok'])
            qh = sdR[R_, 256:512]
            act(qh, pj[R_, 260:516], AF.Silu, ['pj'], xq)
            act(sgS[R_, :], pj[R_, 516:772], AF.Sigmoid, ['pj'], ['sgS'])
            tt(sgS[R_, :], sgS[R_, :], oml[R_, :], ALU.mult, ['sgS', 'oml'], ['sgS'])
            tt(sgS[R_, :], sgS[R_, :], lbl[R_, l, :], ALU.add, ['sgS', 'lbl'], ['sgS'])
            ts(kS[R_, :], sgS[R_, :], -1.0, 1.0, ALU.mult, ALU.add, ['sgS'], ['kS'])
            hg_heads = [(h, h // 2, 64 * (h % 2), 64) for h in range(4)]
            smp_update(hg_heads, lambda f: sgS[R_, f * 128:(f + 1) * 128],
                       lambda h: kS[R_, h * 64:(h + 1) * 64],
                       lambda h: pj[R_, 772 + h * 64:772 + (h + 1) * 64],
                       lambda h: qh[:, h * 64:(h + 1) * 64],
                       lambda f: [(slice(64 * r, 64 * r + 64), si_hg[l, :, 2 * f + r].rearrange("b k v -> k b v")) for r in range(2)],
                       lambda f: [(slice(64 * r, 64 * r + 64), o_hg_s[l, :, 2 * f + r].rearrange("b k v -> k b v")) for r in range(2)],
                       oS[R_, :], xq + ['sgS', 'kS', 'pj'])
            head_norm_gate(l, hgng[R_, :].rearrange("p (h v) -> p h v", h=4), 'hgng', pj[R_, 1028:1284], mixtok[R_, 512:768], npart=NS, from_psum=False)
            for c in range(8):
                mm(pb[4][:16, :NS], win_v[:, c, 2820:2836], hnT[:, c, 0:NS], c == 0, c == 7, win_key(2820, 2836) + ['hnT%d' % c], ['pb4'])
            cp(lrT[:, :NS], pb[4][:16, :NS], ['pb4'], ['lrT'])
            mm(pb[5][R_, :128], lrT[:, :NS], w2[:], True, True, ['lrT', 'w2'], ['pb5'])
            ga = sgS[R_, 0:128]
            tt(ga, pb[5][R_, :128], glab[R_, :], ALU.add, ['pb5', 'glab'], ['sgS'])
            act(ga, ga, AF.Exp, ['sgS'], ['sgS'], scale=-1.0)
            act(ga, ga, AF.Ln, ['sgS'], ['sgS'], bias=1.0)
            act(ga, ga, AF.Exp, ['sgS'], ['sgS'], scale=-1.0 / 16.0)
            gq = sdR[R_, 256:384]
            ts(gq, pj[R_, 1284:1412], 32.0 ** -0.5, None, ALU.mult, None, ['pj'], xq)
            gla_heads = [(h, 0, 32 * h, 32) for h in range(4)]
            smp_update(gla_heads, lambda f: ga,
                       lambda h: pj[R_, 1412 + h * 32:1412 + (h + 1) * 32],
                       lambda h: pj[R_, 1540 + h * 64:1540 + (h + 1) * 64],
                       lambda h: gq[:, h * 32:(h + 1) * 32],
                       lambda f: [(slice(0, 128), si_gla[l].rearrange("b h k v -> (h k) b v"))],
                       lambda f: [(slice(0, 128), o_gla_s[l].rearrange("b h k v -> (h k) b v"))],
                       oS[R_, :], xq + ['sgS', 'pj'])
            head_norm_gate(l, glang[R_, :].unsqueeze(1).to_broadcast([NS, 4, 64]), 'glang', pj[R_, 1796:2052], mixtok[R_, 768:1024], npart=NS, from_psum=False)
            for f in range(2, 8):
                mm(pb[6][:, f * NS:(f + 1) * NS], mixtok[R_, f * 128:(f + 1) * 128], identb[:NS, :NS], True, True, ['mixtok', 'identb'], ['pb6'])
            cp(mixT[:, 2:8, :NS], pb[6][:, 2 * NS:8 * NS].rearrange("p (f t) -> p f t", f=6), ['pb6'], ['mixT'])

        def out_proj(g, tc0, n):
            for m in range(8):
                bk = nxt(0, 2)
                for c in range(2, 8):
                    mm(pb[bk][:, :n], wout_v[:, c, m * 128:(m + 1) * 128], mixT[:, c, :n], c == 2, c == 7,
                       ['wout', 'mixT'], ['pb%d' % bk])
                tt(xT[:, m, tc0:tc0 + n], xT[:, m, tc0:tc0 + n], pb[bk][:, :n], ALU.add, ['x%d_%d' % (m, g), 'pb%d' % bk],
                   ['x%d_%d' % (m, g)])

        load_phaseA_weights(0)
        for l in range(RUN_LAYERS):
            dma('sp', hgng[:], hg_norm_g[:, l, :], [], ['hgng'])
            dma('sp', glang[:], gla_norm_g[:, l, :], [], ['glang'])
            dma('sp', glab[:], gla_b_gk[:, l, :], [], ['glab'])
            dma('sp', ssdng[:], ssd_ng[:, l, :], [], ['ssdng'])
            dma('sp', w2[:], gla_w_gk2[:, l, :], [], ['w2'])
            ts(oml[:], lbl[:, l, :], -1.0, 1.0, ALU.mult, ALU.add, ['lbl'], ['oml'])
            for nm in ('hg', 'gla'):
                P.op('pool', lambda e, nm=nm: e.memset(qlz[nm][:], 0.0), writes=['qlz_' + nm])
            P.op('pool', lambda e: e.memset(Cz[:], 0.0), writes=['Cz'])
            P.op('pool', lambda e: e.memset(BhatZ[:], 0.0), writes=['BhatZ'])
            P.op('dve', lambda e: e.memset(st_ssd[:], 0.0), writes=['st_ssd'])
            P.op('dve', lambda e: e.memset(stb_ssd[:], 0.0), writes=['stb_ssd'])
            for nm in ('hg', 'gla'):
                P.op('dve', lambda e, nm=nm: e.memset(hst[nm][:], 0.0), writes=['hst_' + nm])
                P.op('dve', lambda e, nm=nm: e.memset(stz[nm][:], 0.0), writes=['stz_' + nm])
            pro_done = set()
            pending_fin = []
            for g in range(RUN_GROUPS):
                t0, n = GROUPS[g]

                def prologue_gen(g, l=l):
                    t0_, n_ = GROUPS[g]
                    rmsnorm_group(g, lambda c, l=l: gmix[:, l, c:c + 1], 'gmix', hnT, 'hnT', 0)
                    yield
                    if g > 0 and g < SAMP:
                        cp(xbcT[:, :, 0:3], xbcT[:, :, GN:GN + 3], ['xbcT'], ['xbcT'])
                    elif g == 0:
                        P.op('dve', lambda e: e.memset(xbcT[:, :, 0:3], 0.0), reads=['xbcT'], writes=['xbcT'])
                    for m in ((0, 1) if g == SAMP else (0, 1, 4, 5, 6, 7)):
                        bk = nxt(0, 2)
                        for c in range(8):
                            mm(pb[bk][:, :n_], win_v[:, c, m * 128:(m + 1) * 128], hnT[:, c, :n_], c == 0, c == 7,
                               win_key(m * 128, (m + 1) * 128) + ['hnT%d' % c], ['pb%d' % bk])
                        if m < 2:
                            cp(uT[:, m, t0_:t0_ + n_], pb[bk][:, :n_], ['pb%d' % bk], ['uT'], eng='act')
                        else:
                            cp(xbcT[:, m - 4, 3:3 + n_], pb[bk][:, :n_], ['pb%d' % bk], ['xbcT'], eng='act')
                        yield

                if g not in pro_done:
                    for _ in prologue_gen(g):
                        pass
                    pro_done.add(g)
                if g < SAMP and RUN_SSD:
                    for cc in range(4):
                        ts(sdR[:, :n], xbcT[:, cc, 0:n], cwS[:, l, cc, 0:1], None, ALU.mult, None, ['xbcT', 'cwS'], ['laS', 'qS'])
                        for j in range(1, 4):
                            stt(sdR[:, :n], xbcT[:, cc, j:j + n], cwS[:, l, cc, j:j + 1], sdR[:, :n], ALU.mult, ALU.add,
                                ['xbcT', 'cwS', 'laS', 'qS'], ['laS', 'qS'])
                        act(xsT[:, cc, :n], sdR[:, :n], AF.Silu, ['laS', 'qS', 'cbS'], ['xsT'], bias=cbS[:, l, cc:cc + 1])
                    for gg in range(2):
                        cp(Cz[64 * gg:64 * gg + 64, gg, :n], xsT[64 * gg:64 * gg + 64, 3, :n], ['xsT'], ['Cz'], eng='pool')
                if g == LASTP and RUN_CONV_OUT:
                    for cc in range(4):
                        dma_nc('sp', o_conv_p[l][:, cc * 128:(cc + 1) * 128].rearrange("j p -> p j"), xbcT[:, cc, GN:GN + 3], ['xbcT'], [])
                if g == SAMP:
                    while pending_fin:
                        for _ in pending_fin.pop():
                            pass
                    if RUN_SAMPLE:
                        P.barrier()
                        sample_tile(l)
                    else:
                        P.op('pool', lambda e, n=n: e.memset(mixT[:, :, :n], 0.0), reads=['mixT'], writes=['mixT'])
                else:
                    for t in range(RUN_TILES):
                        c0 = t * 128
                        for (a, b, po) in TOK_GROUPS:
                            bk = nxt(0, 2)
                            for c in range(8):
                                mm(pb[bk][:, :b - a], hnT[:, c, c0:c0 + 128], win_v[:, c, a:b], c == 0, c == 7,
                                   ['hnT%d' % c] + win_key(a, b), ['pb%d' % bk])
                            cp(pj[:, po:po + (b - a)], pb[bk][:, :b - a], ['pb%d' % bk], ['pj'], eng='act')
                        def hg_gen(l=l, c0=c0):
                            act(qS[:], pj[:, 260:516], AF.Silu, ['pj'], ['qS'])
                            act(sgS[:], pj[:, 516:772], AF.Sigmoid, ['pj'], ['sgS'])
                            yield
                            tt(sgS[:], sgS[:], oml[:], ALU.mult, ['sgS', 'oml'], ['sgS'])
                            tt(sgS[:], sgS[:], lbl[:, l, :], ALU.add, ['sgS', 'lbl'], ['sgS'])
                            act(laS[:], sgS[:], AF.Ln, ['sgS'], ['laS'])
                            ts(kS[:], sgS[:], -1.0, 1.0, ALU.mult, ALU.add, ['sgS'], ['kS'])
                            cp(vB[:], pj[:, 772:1028], ['pj'], ['vB'])
                            yield
                            yield from gla_like(B_HG, 'hg', l, 4, 64, 128, qS[:], kS[:], vB, laS[:], 1.0, 4, ['qS', 'kS', 'laS'])
                            yield from head_norm_gate_gen(l, hgng[:].rearrange("p (h v) -> p h v", h=4), 'hgng', pj[:, 1028:1284], mixtok[:, 512:768], B=B_HG)
                            yield

                        def gla_gen(l=l, c0=c0):
                            gla_, gq_, gk_ = g_lqk[:, 0, :], g_lqk[:, 1, :], g_lqk[:, 2, :]
                            gk3 = ['g_la', 'g_q', 'g_k']
                            for c in range(8):
                                mm(pb[3][:16, :128], win_v[:, c, 2820:2836], hnT[:, c, c0:c0 + 128], c == 0, c == 7,
                                   win_key(2820, 2836) + ['hnT%d' % c], ['pb3'])
                            cp(lrT[:], pb[3][:16, :128], ['pb3'], ['lrT'])
                            yield
                            mm(pb[2][:, :128], lrT[:], w2[:], True, True, ['lrT', 'w2'], ['pb2'])
                            tt(gla_, pb[2][:, :128], glab[:], ALU.add, ['pb2', 'glab'], ['g_la'])
                            yield
                            act(gla_, gla_, AF.Exp, ['g_la'], ['g_la'], scale=-1.0)
                            act(gla_, gla_, AF.Ln, ['g_la'], ['g_la'], bias=1.0)
                            ts(gq_, pj[:, 1284:1412], 32.0 ** -0.5, None, ALU.mult, None, ['pj'], ['g_q'])
                            cp(gk_, pj[:, 1412:1540], ['pj'], ['g_k'])
                            cp(g_vB[:], pj[:, 1540:1796], ['pj'], ['g_vB'])
                            yield
                            yield from gla_like(B_GL, 'gla', l, 4, 32, 128, gq_, gk_, g_vB, gla_, -1.0 / 16.0, 1, gk3)
                            yield from head_norm_gate_gen(l, glang[:].unsqueeze(1).to_broadcast([128, 4, 64]), 'glang', pj[:, 1796:2052],
                                           mixtok[:, 768:1024], B=B_GL)
                            yield

                        gens = [ssd_tile(l, c0), gla_gen()]
                        while pending_fin:
                            gens.append(pending_fin.pop())
                        hg_started = False
                        if not INTERLEAVE:
                            for gi in gens + [hg_gen()]:
                                for _ in gi:
                                    pass
                            gens = []
                        prio = set()
                        while gens:
                            for gi in list(gens):
                                try:
                                    r_ = next(gi)
                                    if id(gi) in prio:
                                        r_ = next(gi)
                                except StopIteration:
                                    gens.remove(gi)
                                    continue
                                if r_ == 'front_done' and not hg_started:
                                    hg_it = hg_gen()
                                    gens.insert(0, hg_it)
                                    prio.add(id(hg_it))
                                    hg_started = True
                                    if PRO_AHEAD and t == RUN_TILES - 1 and g + 1 < RUN_GROUPS and (g + 1) not in pro_done:
                                        gens.append(prologue_gen(g + 1))
                                        pro_done.add(g + 1)
                        def finish_gen(g=g, tc0=t0 + c0):
                            for f in range(2, 8):
                                P.op('pe', lambda e, f=f: e.transpose(ptb[:, f * 128:(f + 1) * 128], mixtok[:, f * 128:(f + 1) * 128], identb[:]),
                                     reads=['mixtok', 'identb'], writes=['ptb'])
                            cp(mixT[:, 2:8, :], ptb[:, 256:1024].rearrange("p (f t) -> p f t", f=6), ['ptb'], ['mixT'])
                            yield
                            for m0 in range(0, 8, 2):
                                for m in (m0, m0 + 1):
                                    bk = nxt(0, 2)
                                    for c in range(2, 8):
                                        mm(pb[bk][:, :128], wout_v[:, c, m * 128:(m + 1) * 128], mixT[:, c, :128], c == 2, c == 7,
                                           ['wout', 'mixT'], ['pb%d' % bk])
                                    tt(xT[:, m, tc0:tc0 + 128], xT[:, m, tc0:tc0 + 128], pb[bk][:, :128], ALU.add,
                                       ['x%d_%d' % (m, g), 'pb%d' % bk], ['x%d_%d' % (m, g)])
                                yield

                        last_prompt_tile = (g == min(LASTP, RUN_GROUPS - 1) or g == LASTP) and t == RUN_TILES - 1
                        if FIN_DEFER and not last_prompt_tile and g < LASTP + 1:
                            pending_fin.append(finish_gen())
                        else:
                            for _ in finish_gen():
                                pass
                    if g == LASTP:
                        dma('sp', o_hg_p[l].rearrange("(f p) v -> p f v", p=128), hst['hg'][:], ['hst_hg'], [])
                        for gg in range(2):
                            dma('sp', o_ssd_p[l, 2 * gg:2 * gg + 2].rearrange("r n p -> n r p"), st_ssd[64 * gg:64 * gg + 64, :, :], ['st_ssd'], [])
                        dma('sp', o_gla_p[l], hst['gla'][:, 0, :], ['hst_gla'], [])
                if g == SAMP:
                    out_proj(g, t0, n)
            if RUN_S5:
                P.barrier()
                s5_phase(l)
            P.barrier()
            load_mlp_block(l, 0)
            load_mlp_block(l, 1)
            for g in range(NG):
                rmsnorm_group(g, lambda c, l=l: gmlp[:, l, c:c + 1], 'gmlp', hn2T, 'hn2T%d_' % g, GROUPS[g][0])
            for b in range(4 if RUN_MLP else 0):
                up, dn = mlp_views(b)
                hb = b % 2

                def mlp_up(g, hbuf, hk):
                    t0, n = GROUPS[g]
                    for m in range(8):
                        bk = nxt(0, 2)
                        for c in range(8):
                            mm(pb[bk][:, :n], up[:, c, m * 128:(m + 1) * 128], hn2T[:, c, t0:t0 + n], c == 0, c == 7,
                               ['mup%d' % hb, 'hn2T%d_%d' % (g, c)], ['pb%d' % bk])
                        act(hbuf[:, m, :n], pb[bk][:, :n], AF.Relu, ['pb%d' % bk], [hk + '%d' % m])
                        tt(hbuf[:, m, :n], hbuf[:, m, :n], hbuf[:, m, :n], ALU.mult, [hk + '%d' % m], [hk + '%d' % m])

                def mlp_down(g, hbuf, hk):
                    t0, n = GROUPS[g]
                    for m in range(8):
                        bk = nxt(2, 2)
                        for c in range(8):
                            mm(pb[bk][:, :n], dn[:, c, m * 128:(m + 1) * 128], hbuf[:, c, :n], c == 0, c == 7,
                               ['mdn%d' % hb, hk + '%d' % c], ['pb%d' % bk])
                        tt(xT[:, m, t0:t0 + n], xT[:, m, t0:t0 + n], pb[bk][:, :n], ALU.add,
                           ['x%d_%d' % (m, g), 'pb%d' % bk], ['x%d_%d' % (m, g)])

                hbufs = [(hT, 'hT'), (hT2, 'hU')]
                mlp_up(0, *hbufs[0])
                for g in range(NG):
                    if g + 1 < NG:
                        mlp_up(g + 1, *hbufs[(g + 1) % 2])
                    mlp_down(g, *hbufs[g % 2])
                if b + 2 < 4:
                    load_mlp_block(l, b + 2)
            P.barrier()
            if l + 1 < RUN_LAYERS:
                load_phaseA_weights(l + 1)
        for g in range(NG):
            t0, n = GROUPS[g]
            for c in range(8):
                act(hn2T[:, c, t0:t0 + n], xT[:, c, t0:t0 + n], AF.Square, ['x%d_%d' % (c, g)], ['hn2T%d_%d' % (g, c)])
            bk = nxt(0, 2)
            for c in range(8):
                mm(pb[bk][:, :n], onesb[:], hn2T[:, c, t0:t0 + n], c == 0, c == 7, ['onesb', 'hn2T%d_%d' % (g, c)], ['pb%d' % bk])
            act(rstd[:, :n], pb[bk][:, :n], AF.Sqrt, ['pb%d' % bk], ['rstd'], bias=EPS, scale=1.0 / D)
            P.op('dve', lambda e, n=n: e.reciprocal(out=rstd[:, :n], in_=rstd[:, :n]), reads=['rstd'], writes=['rstd'])
            for c in range(8):
                stt(xT[:, c, t0:t0 + n], xT[:, c, t0:t0 + n], gfin[:, c:c + 1], rstd[:, :n], ALU.mult, ALU.mult,
                    ['x%d_%d' % (c, g), 'gfin', 'rstd'], ['x%d_%d' % (c, g)])
        for c in range(8):
            dma('sp', yT[c * 128:(c + 1) * 128, :], xT[:, c, :], ['x%d_%d' % (c, g) for g in range(NG)], [])
        P.finish()
        P.emit()
    return nc


_CACHE = {}


def kernel(**inputs):
    f32 = np.float32
    x_prompt = np.asarray(inputs['x_prompt'], f32)
    x_sample = np.asarray(inputs['x_sample'], f32)
    B = x_prompt.shape[0]
    consts = host_consts()

    def pc(w):
        L, R, N = w.shape
        return np.ascontiguousarray(w.reshape(L, R // 128, 128, N).transpose(0, 2, 1, 3))

    def bc(v):
        v = np.asarray(v, f32)
        return np.ascontiguousarray(np.broadcast_to(v[None], (128,) + v.shape))

    def ql(v):
        v = np.asarray(v, f32)
        L = v.shape[0]
        rest = v.shape[3:]
        v = v.reshape((L, 8, 2, 64) + rest)
        nd = len(rest)
        v = v.transpose((2, 3, 0, 1) + tuple(range(4, 4 + nd)))
        return np.ascontiguousarray(v.reshape((128, L, 8) + rest))

    shared = {
        'w_in': pc(np.asarray(inputs['w_in'], f32)),
        'w_out': pc(np.asarray(inputs['w_out'], f32)),
        'w_up': pc(np.asarray(inputs['w_up'], f32)),
        'w_down': pc(np.asarray(inputs['w_down'], f32)),
        'g_mix': np.ascontiguousarray(np.asarray(inputs['norm_mix_g'], f32).reshape(DEPTH, 8, 128).transpose(2, 0, 1)),
        'g_mlp': np.ascontiguousarray(np.asarray(inputs['norm_mlp_g'], f32).reshape(DEPTH, 8, 128).transpose(2, 0, 1)),
        'g_fin': np.ascontiguousarray(np.asarray(inputs['norm_final_g'], f32).reshape(8, 128).T),
        'hg_lb_logits': bc(inputs['hg_lb_logits']),
        'hg_norm_g': bc(inputs['hg_norm_g']),
        'gla_w_gk2': np.ascontiguousarray(np.asarray(inputs['gla_w_gk2'], f32).transpose(1, 0, 2)),
        'gla_b_gk': bc(inputs['gla_b_gk']),
        'gla_norm_g': bc(inputs['gla_norm_g']),
        'ssd_cw': np.ascontiguousarray(np.asarray(inputs['ssd_conv_w'], f32).reshape(DEPTH, 4, 4, 128).transpose(3, 0, 2, 1)),
        'ssd_cb': np.ascontiguousarray(np.asarray(inputs['ssd_conv_b'], f32).reshape(DEPTH, 4, 128).transpose(2, 0, 1)),
        'ssd_rows': bc(np.stack([np.asarray(inputs['ssd_dt_bias'], f32), np.asarray(inputs['ssd_a_log'], f32),
                                 np.asarray(inputs['ssd_d'], f32)], axis=1)),
        'ssd_ng': bc(inputs['ssd_norm_g']),
        's5_lam': np.ascontiguousarray(np.stack([ql(inputs['s5_lam_re']), ql(inputs['s5_lam_im'])], axis=2)),
        's5_ldt': ql(np.repeat(np.asarray(inputs['s5_log_dt'], f32)[:, :, None], 64, axis=2)),
        's5_B': np.ascontiguousarray(np.stack([ql(inputs['s5_b_re']), ql(inputs['s5_b_im'])], axis=2)),
        's5_C': np.ascontiguousarray(np.stack([ql(np.asarray(inputs['s5_c_re'], f32).transpose(0, 1, 3, 2)),
                                              ql(np.asarray(inputs['s5_c_im'], f32).transpose(0, 1, 3, 2))], axis=2)),
        's5_d': np.ascontiguousarray(np.asarray(inputs['s5_d'], f32).reshape(DEPTH, 2, 128).transpose(2, 0, 1)),
        's5_bglu': np.ascontiguousarray(np.asarray(inputs['s5_b_glu'], f32).reshape(DEPTH, 2, 128).transpose(2, 0, 1)),
        's5_wglu': np.ascontiguousarray(np.asarray(inputs['s5_w_glu'], f32).reshape(DEPTH, 2, 128, 256).transpose(0, 2, 1, 3)),
        'ssd_cwr': np.ascontiguousarray(np.broadcast_to(np.asarray(inputs['ssd_conv_w'], f32)[:, :, None, :], (DEPTH, 4, NS, 512))),
        'ssd_cbr': np.ascontiguousarray(np.broadcast_to(np.asarray(inputs['ssd_conv_b'], f32)[:, None, :], (DEPTH, NS, 512))),
        'c_iota': np.ascontiguousarray(np.broadcast_to(np.arange(128, dtype=f32)[None], (128, 128))),
        'c_bd': np.kron(np.eye(4, dtype=f32), np.ones((32, 32), f32)),
        'c_ident': consts['ident'], 'c_ones': consts['ones'], 'c_tglob': consts['tglob'], 'c_tloc': consts['tloc'],
        'c_ltg': consts['ltg'], 'c_maskT': consts['tglob'],
        'c_ea': np.ascontiguousarray(consts['ea'].transpose(1, 0, 2)), 'c_ea0g': consts['ea0g'],
    }
    in_maps = []
    for i in range(8):
        xs = x_sample[i * NS:(i + 1) * NS, 0, :]
        xt = np.concatenate([x_prompt[i], xs], axis=0).T
        m = dict(shared)
        m['xT_in'] = np.ascontiguousarray(xt)
        h0 = []
        for nm in ('state_s5_re', 'state_s5_im'):
            v = np.asarray(inputs[nm], f32)[:, i * NS:(i + 1) * NS]
            v = v.reshape(DEPTH, NS, 8, 2, 64).transpose(0, 3, 4, 2, 1)
            h0.append(v.reshape(DEPTH, 128, 8, NS))
        m['s5_h0'] = np.ascontiguousarray(np.stack(h0, axis=1))
        bs = slice(i * NS, (i + 1) * NS)
        m['si_conv'] = np.ascontiguousarray(np.asarray(inputs['state_ssd_conv'], f32)[:, bs])
        m['si_ssd'] = np.ascontiguousarray(np.asarray(inputs['state_ssd'], f32)[:, bs])
        m['si_hg'] = np.ascontiguousarray(np.asarray(inputs['state_hgrn'], f32)[:, bs])
        m['si_gla'] = np.ascontiguousarray(np.asarray(inputs['state_gla'], f32)[:, bs])
        in_maps.append(m)
    if 'nc' not in _CACHE:
        _CACHE['nc'] = build_program()
    res = run_bass_kernel_spmd(_CACHE['nc'], in_maps, core_ids=list(range(8)))
    R = res.results
    y_prompt = np.stack([R[i]['yT'][:, :NP_].T for i in range(8)], 0)
    y_sample = np.concatenate([R[i]['yT'][:, NP_:].T for i in range(8)], 0)[:, None, :]
    cat = lambda k: np.ascontiguousarray(np.concatenate([R[i][k] for i in range(8)], axis=1))
    p_conv = np.stack([R[i]['o_conv_p'] for i in range(8)], 1)
    p_ssd = np.stack([R[i]['o_ssd_p'] for i in range(8)], 1)

    def unq(v):
        sh = v.shape[:-2]
        v = v.reshape(sh + (2, 64, 8))
        nd = len(sh)
        v = v.transpose(tuple(range(nd)) + (nd + 2, nd, nd + 1))
        return v.reshape(sh + (16, 64))
    p_s5 = [np.stack([unq(R[i]['o_s5_p'][:, ri]) for i in range(8)], 1) for ri in range(2)]
    s_s5 = []
    for ri in range(2):
        per = []
        for i in range(8):
            v = R[i]['o_s5_s'][:, ri]
            v = v.reshape(DEPTH, 2, 64, 8, NS).transpose(0, 4, 3, 1, 2)
            per.append(v.reshape(DEPTH, NS, 16, 64))
        s_s5.append(np.concatenate(per, axis=1))
    p_hg = np.stack([R[i]['o_hg_p'].reshape(DEPTH, 4, 64, 64) for i in range(8)], 1)
    p_gla = np.stack([R[i]['o_gla_p'].reshape(DEPTH, 4, 32, 64) for i in range(8)], 1)
    outs = (np.ascontiguousarray(y_prompt), np.ascontiguousarray(y_sample),
            np.ascontiguousarray(p_s5[0]), np.ascontiguousarray(p_s5[1]), np.ascontiguousarray(p_conv), np.ascontiguousarray(p_ssd),
            np.ascontiguousarray(p_hg), np.ascontiguousarray(p_gla),
            np.ascontiguousarray(s_s5[0]), np.ascontiguousarray(s_s5[1]), cat('o_conv_s'), cat('o_ssd_s'),
            cat('o_hg_s'), cat('o_gla_s'))
    return outs
```

```python
import numpy as np
from contextlib import ExitStack
import concourse.bass as bass
import concourse.mybir as mybir
from concourse.bass_utils import run_bass_kernel_spmd

F32 = mybir.dt.float32
BF16 = mybir.dt.bfloat16
AF = mybir.ActivationFunctionType
ALU = mybir.AluOpType
AX = mybir.AxisListType

EPOCH = 60000
DEPTH = 4
D = 1024
NP_ = 2048
NS = 16
NT = NP_ + NS
NIN = 2836
EPS = 1e-6
GN = 256
GROUPS = [(GN * i, GN) for i in range(NP_ // GN)] + [(NP_, NS)]
NG = len(GROUPS)
LASTP = NG - 2
SAMP = NG - 1
RUN_LAYERS = DEPTH
RUN_MLP = True
RUN_HG = True
RUN_GLA = True
RUN_SSD = True
RUN_S5 = True
RUN_SAMPLE = True
RUN_TILES = GN // 128
RUN_GROUPS = NG
RUN_CONV_OUT = True
DBG_STAGE = 99
INTERLEAVE = True
PRO_AHEAD = True
FIN_DEFER = True
SSD_STAGE = 99


class Prog:
    def __init__(self, nc, es):
        self.nc = nc
        self.es = es
        self.engs = ['pe', 'act', 'dve', 'pool', 'sp']
        self.ops = {e: [] for e in self.engs}
        self.count = {e: 0 for e in self.engs}
        self.sems = {e: [] for e in self.engs}
        self.waited = {e: {} for e in self.engs}
        self.pending = {e: [] for e in self.engs}
        self.lw = {}
        self.rd = {}
        self.dma_sems = []
        self.dma_issued = []
        self.dma_rr = 0
        self.sw_sem = {}
        self.n_dma_sems = 10
        for i in range(self.n_dma_sems):
            self.dma_sems.append(es.enter_context(nc.semaphore("dq%d" % i)))
            self.dma_issued.append(0)

    def _sem_for(self, eng, idx):
        ep = (idx - 1) // EPOCH
        while len(self.sems[eng]) <= ep:
            self.sems[eng].append(self.es.enter_context(self.nc.semaphore("s_%s_%d" % (eng, len(self.sems[eng])))))
        return self.sems[eng][ep], (idx - 1) % EPOCH + 1, ep

    def _wait_tok(self, eng, d, waits):
        if d[0] == 'dma':
            _, si, val = d
            key = ('dma', si)
            if self.waited[eng].get(key, 0) < val:
                self.waited[eng][key] = val
                waits.append((self.dma_sems[si], val))
        else:
            e2, idx = d
            if e2 == 'pe' and eng == 'pe':
                return
            sem, val, ep = self._sem_for(e2, idx)
            key = (e2, ep)
            if self.waited[eng].get(key, 0) < val:
                self.waited[eng][key] = val
                waits.append((sem, val))

    def _deps(self, eng, reads, writes):
        ps = [b for b in reads if b.startswith('pb') or b == 'ptb']
        if ps:
            writes = list(writes) + ps
        deps = []
        for b in reads:
            if b in self.lw:
                deps.append(self.lw[b])
        for b in writes:
            if b in self.lw:
                deps.append(self.lw[b])
            deps.extend(self.rd.get(b, []))
        waits = self.pending[eng]
        self.pending[eng] = []
        best = {}
        for d in deps:
            if d[0] == 'dma':
                k = ('dma', d[1])
                v = d[2]
            else:
                k = (d[0], (d[1] - 1) // EPOCH)
                v = d[1]
            if k not in best or best[k][0] < v:
                best[k] = (v, d)
        for k in best:
            self._wait_tok(eng, best[k][1], waits)
        return waits

    def _note(self, tok, reads, writes):
        ps = [b for b in reads if b.startswith('pb') or b == 'ptb']
        if ps:
            reads = [b for b in reads if b not in ps]
            writes = list(writes) + ps
        for b in reads:
            self.rd.setdefault(b, []).append(tok)
        for b in writes:
            self.lw[b] = tok
            self.rd[b] = []

    def op(self, eng, fn, reads=(), writes=()):
        waits = self._deps(eng, reads, writes)
        self.count[eng] += 1
        idx = self.count[eng]
        sem, val, ep = self._sem_for(eng, idx)
        self.ops[eng].append((waits, fn, sem, 1))
        tok = (eng, idx)
        self._note(tok, reads, writes)
        return tok

    def dma(self, eng, fn, reads=(), writes=()):
        waits = self._deps(eng, reads, writes)
        if eng == 'pool':
            self.dma_sems.append(self.es.enter_context(self.nc.semaphore("dsw%d" % len(self.dma_sems))))
            self.dma_issued.append(0)
            si = len(self.dma_sems) - 1
        else:
            si = self.dma_rr
            self.dma_rr = (self.dma_rr + 1) % self.n_dma_sems
        prev = self.dma_issued[si] * 16
        key = ('dma', si)
        if prev > 0 and self.waited[eng].get(key, 0) < prev:
            self.waited[eng][key] = prev
            waits.append((self.dma_sems[si], prev))
        self.dma_issued[si] += 1
        val = self.dma_issued[si] * 16
        self.ops[eng].append((waits, fn, self.dma_sems[si], 16))
        tok = ('dma', si, val)
        self._note(tok, reads, writes)
        return tok

    def barrier(self):
        for e in self.engs:
            for e2 in self.engs:
                if e2 != e and self.count[e2] > 0:
                    self._wait_tok(e, (e2, self.count[e2]), self.pending[e])
            for si in range(len(self.dma_sems)):
                if self.dma_issued[si] > 0:
                    self._wait_tok(e, ('dma', si, self.dma_issued[si] * 16), self.pending[e])

    def finish(self):
        self.barrier()
        for e in self.engs:
            self.ops[e].append((self.pending[e], None, None, 0))
            self.pending[e] = []

    def emit(self):
        nc = self.nc
        P = self
        with nc.Block() as block:
            def run(e, name):
                for waits, fn, sem, inc in P.ops[name]:
                    for (s, v) in waits:
                        e.wait_ge(s, v)
                    if fn is not None:
                        ins = fn(e)
                        if sem is not None:
                            ins.then_inc(sem, inc)

            @block.tensor
            def _(e):
                run(e, 'pe')

            @block.scalar
            def _(e):
                run(e, 'act')

            @block.vector
            def _(e):
                run(e, 'dve')

            @block.gpsimd
            def _(e):
                run(e, 'pool')

            @block.sync
            def _(e):
                run(e, 'sp')


def host_consts():
    j = np.arange(128)
    c = {}
    c['ident'] = np.eye(128, dtype=np.float32)
    c['ones'] = np.ones((128, 128), np.float32)
    c['tglob'] = (j[:, None] <= j[None, :]).astype(np.float32)
    c['tloc'] = ((j[:, None] <= j[None, :]) & ((j[:, None] // 32) == (j[None, :] // 32))).astype(np.float32)
    c['ltg'] = (j[:, None] > j[None, :]).astype(np.float32)
    c['tglob'] = (j[:, None] <= j[None, :]).astype(np.float32)
    ea = np.zeros((4, 128, 128), np.float32)
    for a in range(4):
        jp = j[:, None]
        jj = j[None, :]
        plus = (jj < 32 * a) & (jp > jj) & (jp < 32 * a)
        minus = (jj >= 32 * a) & (jj < 32 * (a + 1)) & (jp >= 32 * a) & (jp <= jj)
        ea[a] = plus.astype(np.float32) - minus.astype(np.float32)
    c['ea'] = ea
    c['ea0g'] = -c['tglob']
    return c


def build_program():
    nc = bass.Bass("TRN2", target_bir_lowering=False)
    din = lambda n, s: nc.dram_tensor(n, list(s), F32, kind="ExternalInput").ap()
    dout = lambda n, s: nc.dram_tensor(n, list(s), F32, kind="ExternalOutput").ap()
    xT_in = din("xT_in", [D, NT])
    w_in = din("w_in", [DEPTH, 128, 8, NIN])
    w_out = din("w_out", [DEPTH, 128, 8, D])
    w_up = din("w_up", [DEPTH, 128, 8, 4096])
    w_down = din("w_down", [DEPTH, 128, 32, D])
    g_mix = din("g_mix", [128, DEPTH, 8])
    g_mlp = din("g_mlp", [128, DEPTH, 8])
    g_fin = din("g_fin", [128, 8])
    hg_lb_logits = din("hg_lb_logits", [128, DEPTH, 256])
    hg_norm_g = din("hg_norm_g", [128, DEPTH, 256])
    gla_w_gk2 = din("gla_w_gk2", [16, DEPTH, 128])
    gla_b_gk = din("gla_b_gk", [128, DEPTH, 128])
    gla_norm_g = din("gla_norm_g", [128, DEPTH, 64])
    ssd_cw = din("ssd_cw", [128, DEPTH, 4, 4])
    ssd_cb = din("ssd_cb", [128, DEPTH, 4])
    ssd_rows = din("ssd_rows", [128, DEPTH, 3, 4])
    ssd_ng = din("ssd_ng", [128, DEPTH, 256])
    s5_lam = din("s5_lam", [128, DEPTH, 2, 8])
    s5_ldt = din("s5_ldt", [128, DEPTH, 8])
    s5_B = din("s5_B", [128, DEPTH, 2, 8, 16])
    s5_C = din("s5_C", [128, DEPTH, 2, 8, 16])
    s5_d = din("s5_d", [128, DEPTH, 2])
    s5_bglu = din("s5_bglu", [128, DEPTH, 2])
    s5_wglu = din("s5_wglu", [DEPTH, 128, 2, 256])
    s5_h0 = din("s5_h0", [DEPTH, 2, 128, 8, 16])
    ssd_cwr = din("ssd_cwr", [DEPTH, 4, NS, 512])
    ssd_cbr = din("ssd_cbr", [DEPTH, NS, 512])
    si_conv = din("si_conv", [DEPTH, NS, 3, 512])
    si_ssd = din("si_ssd", [DEPTH, NS, 4, 64, 64])
    si_hg = din("si_hg", [DEPTH, NS, 4, 64, 64])
    si_gla = din("si_gla", [DEPTH, NS, 4, 32, 64])
    c_iota = din("c_iota", [128, 128])
    c_bd = din("c_bd", [128, 128])
    c_ident = din("c_ident", [128, 128])
    c_ones = din("c_ones", [128, 128])
    c_tglob = din("c_tglob", [128, 128])
    c_tloc = din("c_tloc", [128, 128])
    c_ltg = din("c_ltg", [128, 128])
    c_maskT = din("c_maskT", [128, 128])
    c_ea = din("c_ea", [128, 4, 128])
    c_ea0g = din("c_ea0g", [128, 128])

    yT = dout("yT", [D, NT])
    o_conv_p = dout("o_conv_p", [DEPTH, 3, 512])
    o_hg_p = dout("o_hg_p", [DEPTH, 256, 64])
    o_s5_p = dout("o_s5_p", [DEPTH, 2, 128, 8])
    o_conv_s = dout("o_conv_s", [DEPTH, NS, 3, 512])
    o_ssd_s = dout("o_ssd_s", [DEPTH, NS, 4, 64, 64])
    o_hg_s = dout("o_hg_s", [DEPTH, NS, 4, 64, 64])
    o_gla_s = dout("o_gla_s", [DEPTH, NS, 4, 32, 64])
    o_s5_s = dout("o_s5_s", [DEPTH, 2, 128, 8, 16])
    o_ssd_p = dout("o_ssd_p", [DEPTH, 4, 64, 64])
    o_gla_p = dout("o_gla_p", [DEPTH, 128, 64])

    with ExitStack() as es:
        P = Prog(nc, es)
        sb = lambda n, s, d=F32: es.enter_context(nc.sbuf_tensor(n, list(s), d))
        xT = sb("xT", [128, 8, NT])
        W = sb("W", [128, 32768], BF16)
        gmix = sb("gmix", [128, DEPTH, 8])
        gmlp = sb("gmlp", [128, DEPTH, 8])
        gfin = sb("gfin", [128, 8])
        identb = sb("identb", [128, 128], BF16)
        identf = sb("identf", [128, 128])
        onesb = sb("onesb", [128, 128], BF16)
        onesf = sb("onesf", [128, 128])
        tglob = sb("tglob", [128, 128])
        tloc = sb("tloc", [128, 128])
        ltg = sb("ltg", [128, 128])
        eaM = sb("eaM", [128, 4, 128])
        lbl = sb("lbl", [128, DEPTH, 256])
        cwS = sb("cwS", [128, DEPTH, 4, 4])
        cbS = sb("cbS", [128, DEPTH, 4])
        srow = sb("srow", [128, DEPTH, 3, 4])
        arow = sb("arow", [128, DEPTH, 4])
        rstd = sb("rstd", [128, GN])
        iotaS = sb("iotaS", [128, 128])
        bdS = sb("bdS", [128, 128])
        uT = sb("uT", [128, 2, NT], BF16)
        esA = ExitStack()
        sa = lambda n, s, d=F32: esA.enter_context(nc.sbuf_tensor(n, list(s), d))
        hnT = sa("hnT", [128, 8, GN], BF16)
        oml = sa("oml", [128, 256])
        lbt = sa("lbt", [128, 256])
        hgng = sa("hgng", [128, 256])
        glang = sa("glang", [128, 64])
        glab = sa("glab", [128, 128])
        ssdng = sa("ssdng", [128, 256])
        w2 = sa("w2", [16, 128])
        ecw = sa("ecw", [128, 12])
        dtS = sa("dtS", [128, 8])
        xtok = sa("xtok", [128, 256])
        xbcT = sa("xbcT", [128, 4, GN + 3])
        pj = sa("pj", [128, 2068])
        mixtok = sa("mixtok", [128, 1024], BF16)
        mixT = sa("mixT", [128, 8, 128], BF16)
        sdD = sa("sdD", [128, 512])
        ex = [sdD[:, 0:256], sdD[:, 256:512]]
        sdR = sa("sdR", [128, 512])
        laS = sdR[:, 0:256]
        qS = sdR[:, 256:512]
        kS = sa("kS", [128, 256])
        vB = sa("vB", [128, 256], BF16)
        sgS = sa("sgS", [128, 256])
        qlocB = sa("qlocB", [128, 256], BF16)
        qgB = sa("qgB", [128, 256], BF16)
        kaB = [sa("kaB%d" % a, [128, 256], BF16) for a in range(4)]
        khB = sa("khB", [128, 256], BF16)
        qgT = sa("qgT", [128, 2, 128], BF16)
        kaT = sa("kaT", [128, 4, 2, 128], BF16)
        sT = sa("sT", [128, 512], BF16)
        oS = sa("oS", [128, 256])
        o2S = sa("o2S", [128, 256])
        ssq = sa("ssq", [128, 4])
        elast = sa("elast", [128, 2])
        lrT = sa("lrT", [16, 128])
        esP = ExitStack()
        sp_ = lambda n, s, d=F32: esP.enter_context(nc.sbuf_tensor(n, list(s), d))
        xsT = sp_("xsT", [128, 4, GN], BF16)
        Cz = sp_("Cz", [128, 2, GN], BF16)
        BhatZ = sp_("BhatZ", [128, 4, 128], BF16)
        btok = sp_("btok", [128, 128])
        st_ssd = sp_("st_ssd", [128, 2, 64])
        stb_ssd = sp_("stb_ssd", [128, 2, 64], BF16)
        hst = {'hg': sp_("hst_hg", [128, 2, 64]), 'gla': sp_("hst_gla", [128, 2, 64])}
        stz = {'hg': sp_("stz_hg", [128, 4, 64], BF16), 'gla': sp_("stz_gla", [128, 4, 64], BF16)}
        qlz = {'hg': sp_("qlz_hg", [128, 4, 128], BF16), 'gla': sp_("qlz_gla", [128, 4, 128], BF16)}
        g_ex = sp_("g_ex", [128, 256])
        g_lqk = sp_("g_lqk", [128, 3, 128])
        g_vB = sp_("g_vB", [128, 256], BF16)
        g_b3 = sp_("g_b3", [128, 3, 128], BF16)
        g_qgT = sp_("g_qgT", [128, 1, 128], BF16)
        g_kaT = sp_("g_kaT", [128, 1, 1, 128], BF16)
        g_sT = sp_("g_sT", [128, 512], BF16)
        g_oS = sp_("g_oS", [128, 256])
        g_o2S = sp_("g_o2S", [128, 256])
        g_ssq = sp_("g_ssq", [128, 4])
        g_elast = sp_("g_elast", [128, 2])
        esP.close()
        esQ = ExitStack()
        sq_ = lambda n, s, d=F32: esQ.enter_context(nc.sbuf_tensor(n, list(s), d))
        dgT = sq_("dgT", [16, 512])
        smpS = sq_("smpS", [128, NS, 64])
        esQ.close()
        esA.close()
        esB = ExitStack()
        sbb = lambda n, s, d=F32: esB.enter_context(nc.sbuf_tensor(n, list(s), d))
        hn2T = sbb("hn2T", [128, 8, NT], BF16)
        hT = sbb("hT", [128, 8, GN], BF16)
        hT2 = sbb("hT2", [128, 8, GN], BF16)
        esB.close()
        esS = ExitStack()
        ss = lambda n, s, d=F32: esS.enter_context(nc.sbuf_tensor(n, list(s), d))
        lamS = ss("lamS", [128, 2, 8])
        t8 = ss("t8", [128, 12, 8])
        tab = ss("tab", [128, 8, 17, 8])
        Bq = ss("Bq", [128, 2, 8, 16])
        Cq = ss("Cq", [128, 2, 8, 16])
        Bbar = ss("Bbar", [128, 2, 8, 16])
        t16 = ss("t16", [128, 4, 8, 16])
        Xexp = ss("Xexp", [128, 4, 2, 8, 32], BF16)
        XZ = ss("XZ", [128, 4, 128], BF16)
        ZA = ss("ZA", [128, 2, 8, 128])
        ZB = ss("ZB", [128, 2, 8, 128])
        c2S = ss("c2S", [128, 8, 128])
        s2S = ss("s2S", [128, 8, 128])
        Sb = ss("Sb", [128, 2, 8, 129], BF16)
        zT = ss("zT", [128, 2, NT], BF16)
        YZb = ss("YZb", [128, 2, 4, 128], BF16)
        wgB = ss("wgB", [128, 2, 256], BF16)
        h0S = ss("h0S", [128, 2, 8, 16])
        h1S = ss("h1S", [128, 2, 8, 16])
        h1b = ss("h1b", [128, 2, 8, 16], BF16)
        d5 = ss("d5", [128, 2])
        bg5 = ss("bg5", [128, 2])
        esS.close()
        tmpX = zT[:, 0, 0:2048].bitcast(F32).rearrange("p (a n) -> p a n", a=8)
        ytmp = ZA[:, 0, 0:4, :].rearrange("p a n -> p (a n)")
        gtmp = ZA[:, 0, 4:8, :].rearrange("p a n -> p (a n)")
        wgS = c2S[:, 0:4, :].rearrange("p a n -> p (a n)").rearrange("p (k n) -> p k n", k=2)

        pb = [es.enter_context(nc.psum_tensor("pb%d" % i, [128, 512], F32)) for i in range(7)]
        ptb = es.enter_context(nc.psum_tensor("ptb", [128, 1024], BF16))

        rr = {'a': 0}

        def nxt(lo, n):
            rr['a'] += 1
            return lo + (rr['a'] % n)

        def mm(out, lhsT, rhs, start, stop, r, w):
            P.op('pe', lambda e: e.matmul(out, lhsT=lhsT, rhs=rhs, start=start, stop=stop), reads=r, writes=w)

        def act(out, in_, func, r, w, bias=None, scale=None, accum_out=None):
            kw = {}
            if bias is not None:
                kw['bias'] = bias
            if scale is not None:
                kw['scale'] = scale
            if accum_out is not None:
                kw['accum_out'] = accum_out
            P.op('act', lambda e: e.activation(out=out, in_=in_, func=func, **kw), reads=r, writes=w)

        def tt(out, in0, in1, op, r, w, eng='dve'):
            P.op(eng, lambda e: e.tensor_tensor(out=out, in0=in0, in1=in1, op=op), reads=r, writes=w)

        def ts(out, in0, s1, s2, op0, op1, r, w, eng='dve'):
            if op1 is None:
                P.op(eng, lambda e: e.tensor_scalar(out=out, in0=in0, scalar1=s1, scalar2=None, op0=op0), reads=r, writes=w)
            else:
                P.op(eng, lambda e: e.tensor_scalar(out=out, in0=in0, scalar1=s1, scalar2=s2, op0=op0, op1=op1), reads=r, writes=w)

        def stt(out, in0, scalar, in1, op0, op1, r, w, eng='dve'):
            P.op(eng, lambda e: e.scalar_tensor_tensor(out=out, in0=in0, scalar=scalar, in1=in1, op0=op0, op1=op1),
                 reads=r, writes=w)

        def cp(out, in_, r, w, eng='dve'):
            if eng == 'act':
                act(out, in_, AF.Copy, r, w)
            else:
                P.op(eng, lambda e: e.tensor_copy(out=out, in_=in_), reads=r, writes=w)

        def dma(eng, out, in_, r, w):
            P.dma(eng, lambda e: e.dma_start(out=out, in_=in_), reads=r, writes=w)

        def dma_nc(eng, out, in_, r, w):
            P.dma(eng, lambda e: e.dma_start(out=out, in_=in_, allow_slow_non_contiguous=True), reads=r, writes=w)

        for c in range(8):
            dma('sp', xT[:, c, :], xT_in[c * 128:(c + 1) * 128, :], [], ['x%d_%d' % (c, g) for g in range(NG)])
        dma('sp', gmix[:], g_mix, [], ['gmix'])
        dma('sp', gmlp[:], g_mlp, [], ['gmlp'])
        dma('sp', gfin[:], g_fin, [], ['gfin'])
        dma('pool', identb[:], c_ident, [], ['identb'])
        dma('pool', onesb[:], c_ones, [], ['onesb'])
        dma('sp', identf[:], c_ident, [], ['identf'])
        dma('sp', onesf[:], c_ones, [], ['onesf'])
        dma('sp', tglob[:], c_tglob, [], ['tglob'])
        dma('sp', iotaS[:], c_iota, [], ['iotaS'])
        dma('sp', bdS[:], c_bd, [], ['bdS'])
        dma('sp', tloc[:], c_tloc, [], ['tloc'])
        dma('sp', ltg[:], c_ltg, [], ['ltg'])
        dma('sp', eaM[:], c_ea, [], ['eaM'])
        dma('sp', lbl[:], hg_lb_logits, [], ['lbl'])
        dma('sp', cwS[:], ssd_cw, [], ['cwS'])
        dma('sp', cbS[:], ssd_cb, [], ['cbS'])
        dma('sp', srow[:], ssd_rows, [], ['srow'])
        act(arow[:], srow[:, :, 1, :], AF.Exp, ['srow'], ['arow'])
        ts(arow[:], arow[:], -1.0, None, ALU.mult, None, ['arow'], ['arow'])
        act(lbl[:], lbl[:], AF.Exp, ['lbl'], ['lbl'])
        tt(lbt[:], lbl[:, 0, :], lbl[:, 1, :], ALU.add, ['lbl'], ['lbt'])
        tt(lbt[:], lbt[:], lbl[:, 2, :], ALU.add, ['lbl', 'lbt'], ['lbt'])
        tt(lbt[:], lbt[:], lbl[:, 3, :], ALU.add, ['lbl', 'lbt'], ['lbt'])
        P.op('dve', lambda e: e.reciprocal(out=lbt[:], in_=lbt[:]), reads=['lbt'], writes=['lbt'])
        for l in range(DEPTH):
            tt(lbl[:, l, :], lbl[:, l, :], lbt[:], ALU.mult, ['lbl', 'lbt'], ['lbl'])
        tt(lbl[:, 3, :], lbl[:, 3, :], lbl[:, 2, :], ALU.add, ['lbl'], ['lbl'])
        tt(lbl[:, 3, :], lbl[:, 3, :], lbl[:, 1, :], ALU.add, ['lbl'], ['lbl'])
        tt(lbl[:, 2, :], lbl[:, 2, :], lbl[:, 1, :], ALU.add, ['lbl'], ['lbl'])
        P.op('dve', lambda e: e.memset(lbl[:, 0, :], 0.0), reads=['lbl'], writes=['lbl'])

        WIN_OFF = 0
        WOUT_OFF = 8 * NIN
        win_v = W[:, WIN_OFF:WIN_OFF + 8 * NIN].rearrange("p (c n) -> p c n", c=8)
        wout_v = W[:, WOUT_OFF:WOUT_OFF + 8 * D].rearrange("p (c n) -> p c n", c=8)
        IN_CHUNKS = [(0, 1024), (1024, 2048), (2048, NIN)]

        def load_phaseA_weights(l):
            for i, (a, b) in enumerate(IN_CHUNKS):
                dma('pool', win_v[:, :, a:b], w_in[l, :, :, a:b], [], ['win%d' % i])
            dma('pool', wout_v[:, :, :], w_out[l, :, :, :], [], ['wout'])

        def mlp_views(b):
            off = (b % 2) * 16384
            up = W[:, off:off + 8192].rearrange("p (c n) -> p c n", c=8)
            dn = W[:, off + 8192:off + 16384].rearrange("p (c n) -> p c n", c=8)
            return up, dn

        def load_mlp_block(l, b):
            up, dn = mlp_views(b)
            h = b % 2
            dma('pool', up[:, :, :], w_up[l, :, :, b * 1024:(b + 1) * 1024], [], ['mup%d' % h])
            dma('pool', dn[:, :, :], w_down[l, :, b * 8:(b + 1) * 8, :], [], ['mdn%d' % h])

        def rmsnorm_group(g, gain_ap_fn, gkey, out_T, okey, oc0):
            t0, n = GROUPS[g]
            for c in range(8):
                act(out_T[:, c, oc0:oc0 + n], xT[:, c, t0:t0 + n], AF.Square, ['x%d_%d' % (c, g)], [okey + '%d' % c])
            bk = nxt(0, 2)
            for c in range(8):
                mm(pb[bk][:, :n], onesb[:], out_T[:, c, oc0:oc0 + n], c == 0, c == 7, ['onesb', okey + '%d' % c], ['pb%d' % bk])
            act(rstd[:, :n], pb[bk][:, :n], AF.Sqrt, ['pb%d' % bk], ['rstd'], bias=EPS, scale=1.0 / D)
            P.op('dve', lambda e: e.reciprocal(out=rstd[:, :n], in_=rstd[:, :n]), reads=['rstd'], writes=['rstd'])
            for c in range(8):
                stt(out_T[:, c, oc0:oc0 + n], xT[:, c, t0:t0 + n], gain_ap_fn(c), rstd[:, :n], ALU.mult, ALU.mult,
                    ['x%d_%d' % (c, g), gkey, 'rstd'], [okey + '%d' % c])

        def win_key(a, b):
            ks = []
            for i, (ca, cb) in enumerate(IN_CHUNKS):
                if a < cb and b > ca:
                    ks.append('win%d' % i)
            return ks

        TOK_GROUPS = [(256, 512, 0), (1024, 1536, 256), (1536, 2048, 768), (2048, 2560, 1280), (2560, NIN, 1792)]

        class BS:
            pass

        B_HG = BS()
        B_HG.pfx = ''
        B_HG.ex = ex
        B_HG.qgB, B_HG.qlocB, B_HG.kaB, B_HG.khB = qgB, qlocB, kaB, khB
        B_HG.qgT, B_HG.kaT, B_HG.sT = qgT, kaT, sT
        B_HG.oS, B_HG.o2S, B_HG.ssq, B_HG.elast = oS, o2S, ssq, elast
        B_HG.banks = [4, 5]
        B_HG.obank = 6
        B_HG.vkey = 'vB'
        B_GL = BS()
        B_GL.pfx = 'g_'
        B_GL.ex = [g_ex[:, 0:128], g_ex[:, 128:256]]
        B_GL.qgB, B_GL.qlocB, B_GL.kaB, B_GL.khB = g_b3[:, 0, :], None, [g_b3[:, 1, :]], g_b3[:, 2, :]
        B_GL.qgT, B_GL.kaT, B_GL.sT = g_qgT, g_kaT, g_sT
        B_GL.oS, B_GL.o2S, B_GL.ssq, B_GL.elast = g_oS, g_o2S, g_ssq, g_elast
        B_GL.banks = [2]
        B_GL.obank = 3
        B_GL.vkey = 'g_vB'
        brr = {'n': 0}

        def brot(B):
            brr['n'] += 1
            return B.banks[brr['n'] % len(B.banks)]

        def gla_like(B, name, l, H, K, FT, q_ap, k_ap, v_ap, la_ap, escale, nsub, srcs):
            HK = H * K
            NF = HK // FT
            st = hst[name]
            px = B.pfx
            exk = [px + 'ex0', px + 'ex1']
            okey = 'pb%d' % B.obank

            def cum_exp(mat_ap, mkey, ei, neg=False):
                bk = brot(B)
                mm(pb[bk][:, :HK], mat_ap, la_ap, True, True, [mkey] + srcs, ['pb%d' % bk])
                act(B.ex[ei][:, :HK], pb[bk][:, :HK], AF.Exp, ['pb%d' % bk], [exk[ei]], scale=(-escale if neg else escale))

            cum_exp(tglob[:], 'tglob', 0)
            tt(B.qgB[:, :HK], q_ap, B.ex[0][:, :HK], ALU.mult, [exk[0]] + srcs, [px + 'qgB'])
            yield
            if nsub > 1:
                cum_exp(tloc[:], 'tloc', 1)
                tt(B.qlocB[:, :HK], q_ap, B.ex[1][:, :HK], ALU.mult, [exk[1]] + srcs, [px + 'qlocB'])
                yield
            for a in range(nsub):
                ei = a % 2
                if nsub > 1:
                    cum_exp(eaM[:, a, :], 'eaM', ei)
                else:
                    cum_exp(tglob[:], 'tglob', ei, neg=True)
                tt(B.kaB[a][:, :HK], k_ap, B.ex[ei][:, :HK], ALU.mult, [exk[ei]] + srcs, [px + 'kaB%d' % a])
                yield
            ei = nsub % 2
            cum_exp(ltg[:], 'ltg', ei)
            tt(B.khB[:, :HK], k_ap, B.ex[ei][:, :HK], ALU.mult, [exk[ei]] + srcs, [px + 'khB'])
            bk = brot(B)
            for f in range(NF):
                mm(pb[bk][:FT, f:f + 1], la_ap[:, f * FT:(f + 1) * FT], onesf[:, 0:1], True, True, srcs + ['onesf'], ['pb%d' % bk])
            act(B.elast[:FT, :NF], pb[bk][:FT, :NF], AF.Exp, ['pb%d' % bk], [px + 'elast'], scale=escale)
            yield
            qz = qlz[name]
            slot = 0
            tl = []
            items = [(B.qgB, px + 'qgB', 'qg', None)]
            if nsub > 1:
                items.append((B.qlocB, px + 'qlocB', 'ql', None))
            for a in range(nsub):
                items.append((B.kaB[a], px + 'kaB%d' % a, 'ka', a))

            def evac(s_, kind, f, a):
                src = ptb[:FT, s_ * 128:(s_ + 1) * 128]
                if kind == 'ka':
                    cp(B.kaT[:FT, a, f, :], src, ['ptb'], [px + 'kaT'], eng='dve')
                    return
                if kind == 'qg':
                    cp(B.qgT[:FT, f, :], src, ['ptb'], [px + 'qgT'], eng='dve')
                if kind == 'ql' or (kind == 'qg' and nsub == 1):
                    for h in range(H):
                        if (h * K) // FT != f:
                            continue
                        r0 = (h * K) % FT
                        cp(qz[r0:r0 + K, h, :], ptb[r0:r0 + K, s_ * 128:(s_ + 1) * 128], ['ptb'], ['qlz_' + name],
                           eng='dve')

            for (src, skey, kind, a) in items:
                for f in range(NF):
                    P.op('pe', lambda e, s_=slot, src=src, f=f: e.transpose(ptb[:FT, s_ * 128:(s_ + 1) * 128], src[:, f * FT:(f + 1) * FT], identb[:]),
                         reads=[skey, 'identb'], writes=['ptb'])
                    tl.append((slot, kind, f, a))
                    slot += 1
                    if slot == 8:
                        for it in tl:
                            evac(*it)
                        tl = []
                        slot = 0
                        yield
            for it in tl:
                evac(*it)
            yield
            sci = brot(B)
            sc = pb[sci]
            sck = 'pb%d' % sci
            CS = 128 // nsub
            for h in range(H):
                f = (h * K) // FT
                for a in range(nsub):
                    mm(sc[:, h * 128 + a * CS: h * 128 + (a + 1) * CS], B.kaT[:FT, a, f, :], qz[:FT, h, a * CS:(a + 1) * CS],
                       True, True, [px + 'kaT', 'qlz_' + name], [sck])
            tt(B.sT[:, :H * 128].rearrange("p (h i) -> p h i", h=H), sc[:, :H * 128].rearrange("p (h i) -> p h i", h=H),
               tglob[:].unsqueeze(1).to_broadcast([128, H, 128]), ALU.mult, [sck, 'tglob'], [px + 'sT'])
            yield
            sz = stz[name]
            for h in range(H):
                f = (h * K) // FT
                mm(pb[B.obank][:, h * 64:(h + 1) * 64], B.sT[:, h * 128:(h + 1) * 128], v_ap[:, h * 64:(h + 1) * 64], True, False,
                   [px + 'sT', B.vkey], [okey])
                mm(pb[B.obank][:, h * 64:(h + 1) * 64], B.qgT[:FT, f, :], sz[:FT, h, :], False, True,
                   [px + 'qgT', 'stz_' + name], [okey])
            yield
            ubi = brot(B)
            ub = pb[ubi]
            ubk = 'pb%d' % ubi
            for h in range(H):
                f = (h * K) // FT
                mm(ub[:FT, h * 64:(h + 1) * 64], B.khB[:, f * FT:(f + 1) * FT], v_ap[:, h * 64:(h + 1) * 64], True, True,
                   [px + 'khB', B.vkey], [ubk])
            yield
            for h in range(H):
                f = (h * K) // FT
                r0 = (h * K) % FT
                stt(st[r0:r0 + K, f, :], st[r0:r0 + K, f, :], B.elast[r0:r0 + K, f:f + 1], ub[r0:r0 + K, h * 64:(h + 1) * 64],
                    ALU.mult, ALU.add, ['hst_' + name, px + 'elast', ubk], ['hst_' + name])
            for h in range(H):
                f = (h * K) // FT
                r0 = (h * K) % FT
                cp(sz[r0:r0 + K, h, :], st[r0:r0 + K, f, :], ['hst_' + name], ['stz_' + name], eng='dve')
            yield

        def ssd_tile(l, c0):
            tsl = slice(c0, c0 + 128)
            for cc in range(3):
                P.op('pe', lambda e, cc=cc: e.transpose(ptb[:, cc * 128:(cc + 1) * 128], xsT[:, cc, tsl], identb[:]),
                     reads=['xsT', 'identb'], writes=['ptb'])
            cp(xtok[:], ptb[:, 0:256], ['ptb'], ['xtok'], eng='act')
            cp(btok[:], ptb[:, 256:384], ['ptb'], ['btok'], eng='act')
            tt(dtS[:, 0:4], pj[:, 256:260], srow[:, l, 0, :], ALU.add, ['pj', 'srow'], ['dtS'])
            act(dtS[:, 0:4], dtS[:, 0:4], AF.Exp, ['dtS'], ['dtS'])
            act(dtS[:, 0:4], dtS[:, 0:4], AF.Ln, ['dtS'], ['dtS'], bias=1.0)
            tt(dtS[:, 4:8], dtS[:, 0:4], arow[:, l, :], ALU.mult, ['dtS', 'arow'], ['dtS'])
            la = dtS[:, 4:8]
            yield
            bk = nxt(4, 2)
            mm(pb[bk][:, 0:4], tglob[:], la, True, True, ['tglob', 'dtS'], ['pb%d' % bk])
            mm(pb[bk][:, 4:8], ltg[:], la, True, True, ['ltg', 'dtS'], ['pb%d' % bk])
            mm(pb[bk][:, 8:12], onesf[:], la, True, True, ['onesf', 'dtS'], ['pb%d' % bk])
            act(ecw[:], pb[bk][:, 0:12], AF.Exp, ['pb%d' % bk], ['ecw'])
            yield
            tt(sdR[:].rearrange("p (h i) -> p h i", h=4), tglob[:].unsqueeze(1).to_broadcast([128, 4, 128]),
               la.unsqueeze(2).to_broadcast([128, 4, 128]), ALU.mult, ['tglob', 'dtS'], ['laS', 'qS'])
            bk = nxt(4, 2)
            mm(pb[bk][:, :512], ltg[:], sdR[:], True, True, ['ltg', 'laS', 'qS'], ['pb%d' % bk])
            act(sdD[:], pb[bk][:, :512], AF.Exp, ['pb%d' % bk], ['ex0', 'ex1'])
            tt(sdD[:].rearrange("p (h i) -> p h i", h=4), sdD[:].rearrange("p (h i) -> p h i", h=4),
               tglob[:].unsqueeze(1).to_broadcast([128, 4, 128]), ALU.mult, ['ex0', 'ex1', 'tglob'], ['ex0', 'ex1'])
            yield
            sci = nxt(4, 2)
            for gg in range(2):
                mm(pb[sci][:, gg * 128:(gg + 1) * 128], xsT[:, 2, tsl], Cz[:, gg, tsl], True, True, ['xsT', 'Cz'], ['pb%d' % sci])
            for gg in range(2):
                tt(sT[:, gg * 256:(gg + 1) * 256].rearrange("p (r i) -> p r i", r=2),
                   pb[sci][:, gg * 128:(gg + 1) * 128].unsqueeze(1).to_broadcast([128, 2, 128]),
                   sdD[:, gg * 256:(gg + 1) * 256].rearrange("p (r i) -> p r i", r=2), ALU.mult, ['pb%d' % sci, 'ex0', 'ex1'], ['sT'])
            yield
            tt(vB[:].rearrange("p (h v) -> p h v", h=4), xtok[:].rearrange("p (h v) -> p h v", h=4),
               dtS[:, 0:4].unsqueeze(2).to_broadcast([128, 4, 64]), ALU.mult, ['xtok', 'dtS'], ['vB'])
            for h in range(4):
                gg = h // 2
                ts(BhatZ[:, h, 64 * gg:64 * gg + 64], btok[:, 64 * gg:64 * gg + 64], ecw[:, 4 + h:5 + h], None, ALU.mult, None,
                   ['btok', 'ecw'], ['BhatZ'])
            yield
            for h in range(4):
                mm(pb[6][:, h * 64:(h + 1) * 64], sT[:, h * 128:(h + 1) * 128], vB[:, h * 64:(h + 1) * 64], True, True,
                   ['sT', 'vB'], ['pb6'])
            p2 = nxt(4, 2)
            for h in range(4):
                mm(pb[p2][:, h * 64:(h + 1) * 64], Cz[:, h // 2, tsl], stb_ssd[:, h % 2, :], True, True, ['Cz', 'stb_ssd'], ['pb%d' % p2])
            ubi = nxt(4, 2)
            for r in range(2):
                for gg in range(2):
                    h = 2 * gg + r
                    mm(pb[ubi][:, r * 64:(r + 1) * 64], BhatZ[:, h, :], vB[:, h * 64:(h + 1) * 64], gg == 0, gg == 1,
                       ['BhatZ', 'vB'], ['pb%d' % ubi])
            yield 'front_done'
            yield from ssd_tail(l, p2, ubi)

        def ssd_tail(l, p2, ubi):
            tt(oS[:].rearrange("p (h v) -> p h v", h=4), pb[p2][:, :256].rearrange("p (h v) -> p h v", h=4),
               ecw[:, 0:4].unsqueeze(2).to_broadcast([128, 4, 64]), ALU.mult, ['pb%d' % p2, 'ecw'], ['oS'])
            for r in range(2):
                for gg in range(2):
                    h = 2 * gg + r
                    ps_ = slice(64 * gg, 64 * gg + 64)
                    stt(st_ssd[ps_, r, :], st_ssd[ps_, r, :], ecw[ps_, 8 + h:9 + h], pb[ubi][ps_, r * 64:(r + 1) * 64], ALU.mult, ALU.add,
                        ['st_ssd', 'ecw', 'pb%d' % ubi], ['st_ssd'])
            yield
            tt(oS[:], oS[:], pb[6][:, :256], ALU.add, ['oS', 'pb6'], ['oS'])
            cp(stb_ssd[:], st_ssd[:], ['st_ssd'], ['stb_ssd'], eng='act')
            tt(o2S[:].rearrange("p (h v) -> p h v", h=4), xtok[:].rearrange("p (h v) -> p h v", h=4),
               srow[:, l, 2, :].unsqueeze(2).to_broadcast([128, 4, 64]), ALU.mult, ['xtok', 'srow'], ['o2S'])
            tt(oS[:], oS[:], o2S[:], ALU.add, ['oS', 'o2S'], ['oS'])
            yield
            act(o2S[:], pj[:, 0:256], AF.Silu, ['pj'], ['o2S'])
            tt(oS[:], oS[:], o2S[:], ALU.mult, ['oS', 'o2S'], ['oS'])
            yield
            tt(o2S[:], oS[:], oS[:], ALU.mult, ['oS'], ['o2S'])
            P.op('dve', lambda e: e.tensor_reduce(out=ssq[:, 0:1], in_=o2S[:], axis=AX.X, op=ALU.add), reads=['o2S'], writes=['ssq'])
            yield
            act(ssq[:, 0:1], ssq[:, 0:1], AF.Sqrt, ['ssq'], ['ssq'], bias=EPS, scale=1.0 / 256)
            P.op('dve', lambda e: e.reciprocal(out=ssq[:, 0:1], in_=ssq[:, 0:1]), reads=['ssq'], writes=['ssq'])
            yield
            stt(mixtok[:, 256:512], oS[:], ssq[:, 0:1], ssdng[:], ALU.mult, ALU.mult, ['oS', 'ssq', 'ssdng'], ['mixtok'])
            yield

        def head_norm_gate_gen(l, gvec_ap, gkey, gate_ap, dst_ap, npart=128, from_psum=True, B=None):
            B = B or B_HG
            px = B.pfx
            oS_, o2S_, ssq_ = B.oS, B.o2S, B.ssq
            ko, ko2, ks = px + 'oS', px + 'o2S', px + 'ssq'
            if from_psum:
                cp(oS_[:npart], pb[B.obank][:npart, :256], ['pb%d' % B.obank], [ko], eng='act')
            tt(o2S_[:npart], oS_[:npart], oS_[:npart], ALU.mult, [ko], [ko2])
            P.op('dve', lambda e: e.tensor_reduce(out=ssq_[:npart], in_=o2S_[:npart].rearrange("p (h v) -> p h v", h=4), axis=AX.X, op=ALU.add),
                 reads=[ko2], writes=[ks])
            yield
            act(ssq_[:npart], ssq_[:npart], AF.Sqrt, [ks], [ks], bias=EPS, scale=1.0 / 64)
            P.op('dve', lambda e: e.reciprocal(out=ssq_[:npart], in_=ssq_[:npart]), reads=[ks], writes=[ks])
            yield
            tt(oS_[:npart].rearrange("p (h v) -> p h v", h=4), oS_[:npart].rearrange("p (h v) -> p h v", h=4),
               ssq_[:npart].unsqueeze(2).to_broadcast([npart, 4, 64]), ALU.mult, [ko, ks], [ko])
            tt(oS_[:npart].rearrange("p (h v) -> p h v", h=4), oS_[:npart].rearrange("p (h v) -> p h v", h=4), gvec_ap, ALU.mult, [ko, gkey], [ko])
            yield
            act(o2S_[:npart], gate_ap, AF.Silu, ['pj'], [ko2])
            tt(dst_ap, oS_[:npart], o2S_[:npart], ALU.mult, [ko, ko2], ['mixtok'])

        def head_norm_gate(*a_, **k_):
            for _ in head_norm_gate_gen(*a_, **k_):
                pass

        Yexp = W[:, 0:8704].rearrange("p (m r q c) -> p m r q c", m=17, r=2, q=8)
        Kblk = W[:, 8704:12800].rearrange("p (m h c) -> p m h c", m=16, h=2)
        WinZ = W[:, 12800:20992].rearrange("p (m r q c) -> p m r q c", m=4, r=2, q=8)
        MAGIC = 12582912.0
        TWO_PI = 6.283185307179586

        def sin_of(ang_ap, out_ap, tmp_ap, tmp2_ap, shift, rk, wk):
            ts(tmp_ap, ang_ap, 1.0 / TWO_PI, shift, ALU.mult, ALU.add, rk, wk)
            ts(tmp2_ap, tmp_ap, MAGIC, None, ALU.add, None, wk, wk)
            ts(tmp2_ap, tmp2_ap, -MAGIC, None, ALU.add, None, wk, wk)
            tt(tmp_ap, tmp_ap, tmp2_ap, ALU.subtract, wk, wk)
            ts(tmp_ap, tmp_ap, 0.4999995, -0.4999995, ALU.min, ALU.max, wk, wk)
            act(out_ap, tmp_ap, AF.Sin, wk, wk, scale=TWO_PI)

        def s5_phase(l):
            TB = lambda k: tab[:, k, :, :]
            kk = ['s5t']
            dma('sp', lamS[:], s5_lam[:, l, :, :], [], ['lamS'])
            dma('sp', t8[:, 0, :], s5_ldt[:, l, :], [], kk)
            dma('sp', Bq[:], s5_B[:, l], [], ['Bq'])
            dma('sp', Cq[:], s5_C[:, l], [], ['Cq'])
            dma('sp', d5[:], s5_d[:, l, :], [], ['d5'])
            dma('sp', bg5[:], s5_bglu[:, l, :], [], ['bg5'])
            for ri in range(2):
                dma('sp', h0S[:, ri], s5_h0[l, ri], [], ['h0S'])
            P.op('pool', lambda e: e.memset(W[:, 0:8704], 0.0), writes=['Yexp'])
            P.op('pool', lambda e: e.memset(Xexp[:], 0.0), writes=['Xexp'])
            P.op('pool', lambda e: e.memset(XZ[:], 0.0), writes=['XZ'])
            P.op('pool', lambda e: e.memset(YZb[:], 0.0), writes=['YZb0', 'YZb1'])
            act(t8[:, 0, :], t8[:, 0, :], AF.Exp, kk, kk)
            tt(t8[:, 1, :], lamS[:, 0, :], t8[:, 0, :], ALU.mult, kk + ['lamS'], kk)
            tt(t8[:, 2, :], lamS[:, 1, :], t8[:, 0, :], ALU.mult, kk + ['lamS'], kk)
            iv = iotaS[:, 0:17].unsqueeze(2).to_broadcast([128, 17, 8])
            tt(TB(0), iv, t8[:, 1, :].unsqueeze(1).to_broadcast([128, 17, 8]), ALU.mult, kk + ['iotaS'], kk)
            act(TB(0), TB(0), AF.Exp, kk, kk)
            tt(TB(1), iv, t8[:, 2, :].unsqueeze(1).to_broadcast([128, 17, 8]), ALU.mult, kk + ['iotaS'], kk)
            sin_of(TB(1), TB(3), TB(2), TB(5), 0.0, kk, kk)
            sin_of(TB(1), TB(4), TB(2), TB(5), 0.25, kk, kk)
            tt(TB(5), TB(0), TB(4), ALU.mult, kk, kk)
            tt(TB(6), TB(0), TB(3), ALU.mult, kk, kk)
            ts(TB(7), TB(6), -1.0, None, ALU.mult, None, kk, kk)
            lr1, li1 = tab[:, 5, 1, :], tab[:, 6, 1, :]
            ts(t8[:, 3, :], lr1, -1.0, None, ALU.add, None, kk, kk)
            tt(t8[:, 4, :], lamS[:, 0, :], lamS[:, 0, :], ALU.mult, ['lamS'], kk)
            tt(t8[:, 5, :], lamS[:, 1, :], lamS[:, 1, :], ALU.mult, ['lamS'], kk)
            tt(t8[:, 4, :], t8[:, 4, :], t8[:, 5, :], ALU.add, kk, kk)
            P.op('dve', lambda e: e.reciprocal(out=t8[:, 4, :], in_=t8[:, 4, :]), reads=kk, writes=kk)
            tt(t8[:, 5, :], t8[:, 3, :], lamS[:, 0, :], ALU.mult, kk + ['lamS'], kk)
            tt(t8[:, 6, :], li1, lamS[:, 1, :], ALU.mult, kk + ['lamS'], kk)
            tt(t8[:, 5, :], t8[:, 5, :], t8[:, 6, :], ALU.add, kk, kk)
            tt(t8[:, 5, :], t8[:, 5, :], t8[:, 4, :], ALU.mult, kk, kk)
            tt(t8[:, 6, :], li1, lamS[:, 0, :], ALU.mult, kk + ['lamS'], kk)
            tt(t8[:, 7, :], t8[:, 3, :], lamS[:, 1, :], ALU.mult, kk + ['lamS'], kk)
            tt(t8[:, 6, :], t8[:, 6, :], t8[:, 7, :], ALU.subtract, kk, kk)
            tt(t8[:, 6, :], t8[:, 6, :], t8[:, 4, :], ALU.mult, kk, kk)
            kr = t8[:, 5, :].unsqueeze(2).to_broadcast([128, 8, 16])
            ki = t8[:, 6, :].unsqueeze(2).to_broadcast([128, 8, 16])
            tk = ['t16']
            tt(t16[:, 0], Bq[:, 0], kr, ALU.mult, ['Bq'] + kk, tk)
            tt(t16[:, 1], Bq[:, 1], ki, ALU.mult, ['Bq'] + kk, tk)
            tt(Bbar[:, 0], t16[:, 0], t16[:, 1], ALU.subtract, tk, ['Bbar'])
            tt(t16[:, 0], Bq[:, 1], kr, ALU.mult, ['Bq'] + kk, tk)
            tt(t16[:, 1], Bq[:, 0], ki, ALU.mult, ['Bq'] + kk, tk)
            tt(Bbar[:, 1], t16[:, 0], t16[:, 1], ALU.add, tk, ['Bbar'])
            zk = ['ZA', 'ZB']

            def cexp(dst, nm, m0, m1, A, Ak, conj_second):
                nb = m1 - m0
                pa = ZA[:].rearrange("p a b c -> p (a b c)")[:, 0:nb * 128].rearrange("p (m q c) -> p m q c", m=nb, q=8)
                pb_ = ZB[:].rearrange("p a b c -> p (a b c)")[:, 0:nb * 128].rearrange("p (m q c) -> p m q c", m=nb, q=8)
                A0 = A[:, 0].unsqueeze(1).to_broadcast([128, nb, 8, 16])
                A1 = A[:, 1].unsqueeze(1).to_broadcast([128, nb, 8, 16])
                lrm = tab[:, 5, m0:m1, :].unsqueeze(3).to_broadcast([128, nb, 8, 16])
                lim = tab[:, 6, m0:m1, :].unsqueeze(3).to_broadcast([128, nb, 8, 16])
                nlim = tab[:, 7, m0:m1, :].unsqueeze(3).to_broadcast([128, nb, 8, 16])
                tt(pa, A0, lrm, ALU.mult, [Ak] + kk, zk)
                tt(pb_, A1, lim, ALU.mult, [Ak] + kk, zk)
                for hf in range(2):
                    ps_ = slice(64 * hf, 64 * hf + 64)
                    tt(dst[ps_, m0:m1, 0, :, 16 * hf:16 * hf + 16], pa[ps_], pb_[ps_], ALU.subtract, zk, [nm])
                if conj_second:
                    tt(pa, A0, nlim, ALU.mult, [Ak] + kk, zk)
                    tt(pb_, A1, lrm, ALU.mult, [Ak] + kk, zk)
                    op2 = ALU.subtract
                else:
                    tt(pa, A1, lrm, ALU.mult, [Ak] + kk, zk)
                    tt(pb_, A0, lim, ALU.mult, [Ak] + kk, zk)
                    op2 = ALU.add
                for hf in range(2):
                    ps_ = slice(64 * hf, 64 * hf + 64)
                    tt(dst[ps_, m0:m1, 1, :, 16 * hf:16 * hf + 16], pa[ps_], pb_[ps_], op2, zk, [nm])

            cexp(Xexp, 'Xexp', 0, 4, Bbar, 'Bbar', False)
            cexp(Yexp, 'Yexp', 0, 9, Cq, 'Cq', True)
            cexp(Yexp, 'Yexp', 9, 17, Cq, 'Cq', True)
            for ph in range(2):
                for mg in range(4):
                    bk = nxt(0, 2)
                    for k4 in range(4):
                        m = mg * 4 + k4
                        for ri in range(2):
                            mm(pb[bk][:, k4 * 128:(k4 + 1) * 128], Xexp[:, 0, ri, 4 * ph:4 * ph + 4, :].rearrange("p q c -> p (q c)"),
                               Yexp[:, m, ri, 4 * ph:4 * ph + 4, :].rearrange("p q c -> p (q c)"), ri == 0, ri == 1, ['Xexp', 'Yexp'], ['pb%d' % bk])
                    tt(Kblk[:, mg * 4:mg * 4 + 4, ph, :], pb[bk][:, :512].rearrange("p (k c) -> p k c", k=4),
                       bdS[:].unsqueeze(1).to_broadcast([128, 4, 128]), ALU.mult, ['pb%d' % bk, 'bdS'], ['Kblk'])
            slot = 0
            pend = []
            for m in range(4):
                for ri in range(2):
                    for q in range(8):
                        p4 = q % 4
                        cp(XZ[:, p4, 32 * p4:32 * p4 + 32], Xexp[:, m, ri, q, :], ['Xexp'], ['XZ'], eng='act')
                        P.op('pe', lambda e, s_=slot, p4=p4: e.transpose(ptb[:, s_ * 128:(s_ + 1) * 128], XZ[:, p4, :], identb[:]),
                             reads=['XZ', 'identb'], writes=['ptb'])
                        pend.append((slot, m, ri, q))
                        slot += 1
                        if slot == 8:
                            for (s_, m_, r_, q_) in pend:
                                cp(WinZ[:, m_, r_, q_, :], ptb[:, s_ * 128:(s_ + 1) * 128], ['ptb'], ['WinZ'], eng='dve')
                            pend = []
                            slot = 0
            a2 = ZB[:, 0]
            tt(a2, tab[:, 1, 16, :].unsqueeze(2).to_broadcast([128, 8, 128]), iotaS[:].unsqueeze(1).to_broadcast([128, 8, 128]),
               ALU.mult, kk + ['iotaS'], zk)
            sin_of(a2, s2S[:], ZB[:, 1], tmpX[:], 0.0, zk, zk + ['s2S', 'zT'])
            sin_of(a2, c2S[:], ZB[:, 1], tmpX[:], 0.25, zk, zk + ['c2S', 'zT'])
            for q in range(8):
                ph = q // 4
                uv4 = uT[:, ph, 0:NP_].rearrange("p (n j) -> p j n", j=4)
                zb = 2 * (q % 2)
                kzr, kzi = 'pb%d' % zb, 'pb%d' % (zb + 1)
                for ri in range(2):
                    for j in range(4):
                        mm(pb[zb + ri][:, :512], WinZ[:, 3 - j, ri, q, :], uv4[:, j, :], j == 0, j == 3, ['WinZ', 'uT'], ['pb%d' % (zb + ri)])
                zr = pb[zb][:, :512].rearrange("p (n k) -> p k n", k=4)
                zi = pb[zb + 1][:, :512].rearrange("p (n k) -> p k n", k=4)
                for k in range(4):
                    mk = 4 * (3 - k)
                    lr_, li_, nli_ = tab[:, 5, mk, q:q + 1], tab[:, 6, mk, q:q + 1], tab[:, 7, mk, q:q + 1]
                    if k == 0:
                        ts(ZA[:, 0, q, :], zr[:, k, :], lr_, None, ALU.mult, None, [kzr] + kk, ['ZA'])
                        ts(ZA[:, 1, q, :], zi[:, k, :], lr_, None, ALU.mult, None, [kzi] + kk, ['ZA'])
                    else:
                        stt(ZA[:, 0, q, :], zr[:, k, :], lr_, ZA[:, 0, q, :], ALU.mult, ALU.add, [kzr, 'ZA'] + kk, ['ZA'])
                        stt(ZA[:, 1, q, :], zi[:, k, :], lr_, ZA[:, 1, q, :], ALU.mult, ALU.add, [kzi, 'ZA'] + kk, ['ZA'])
                    stt(ZA[:, 0, q, :], zi[:, k, :], nli_, ZA[:, 0, q, :], ALU.mult, ALU.add, [kzi, 'ZA'] + kk, ['ZA'])
                    stt(ZA[:, 1, q, :], zr[:, k, :], li_, ZA[:, 1, q, :], ALU.mult, ALU.add, [kzr, 'ZA'] + kk, ['ZA'])
            tt(ZB[:, 0], c2S[:], ZA[:, 0], ALU.mult, ['c2S', 'ZA'], ['ZB'])
            tt(tmpX[:], s2S[:], ZA[:, 1], ALU.mult, ['s2S', 'ZA'], ['zT'])
            tt(ZB[:, 0], ZB[:, 0], tmpX[:], ALU.add, ['ZB', 'zT'], ['ZB'])
            tt(ZB[:, 1], c2S[:], ZA[:, 1], ALU.mult, ['c2S', 'ZA'], ['ZB'])
            tt(tmpX[:], s2S[:], ZA[:, 0], ALU.mult, ['s2S', 'ZA'], ['zT'])
            tt(ZB[:, 1], ZB[:, 1], tmpX[:], ALU.subtract, ['ZB', 'zT'], ['ZB'])
            for ri in range(2):
                for q in range(8):
                    P.op('dve', lambda e, ri=ri, q=q: e.tensor_tensor_scan(out=ZA[:, ri, q, :], data0=tab[:, 0, 16, q:q + 1].to_broadcast([128, 128]),
                                                                          data1=ZB[:, ri, q, :], initial=0.0, op0=ALU.mult, op1=ALU.add),
                         reads=['ZB'] + kk, writes=['ZA'])
            tt(ZB[:, 0], c2S[:], ZA[:, 0], ALU.mult, ['c2S', 'ZA'], ['ZB'])
            tt(tmpX[:], s2S[:], ZA[:, 1], ALU.mult, ['s2S', 'ZA'], ['zT'])
            tt(ZB[:, 0], ZB[:, 0], tmpX[:], ALU.subtract, ['ZB', 'zT'], ['ZB'])
            tt(ZB[:, 1], c2S[:], ZA[:, 1], ALU.mult, ['c2S', 'ZA'], ['ZB'])
            tt(tmpX[:], s2S[:], ZA[:, 0], ALU.mult, ['s2S', 'ZA'], ['zT'])
            tt(ZB[:, 1], ZB[:, 1], tmpX[:], ALU.add, ['ZB', 'zT'], ['ZB'])
            P.op('dve', lambda e: e.memset(Sb[:, :, :, 0:1], 0.0), writes=['Sb'])
            cp(Sb[:, :, :, 1:129], ZB[:], ['ZB'], ['Sb'], eng='act')
            for ri in range(2):
                dma_nc('sp', o_s5_p[l, ri], ZB[:, ri, :, 127], ['ZB'], [])
            dma('sp', wgS, s5_wglu[l], ['c2S'], ['c2S'])
            cp(wgB[:], wgS, ['c2S'], ['wgB'], eng='act')
            yzc = {'n': 0}

            def yz_load(m, ri, ph):
                sl = yzc['n'] % 2
                yzc['n'] += 1
                for p4 in range(4):
                    cp(YZb[:, sl, p4, 32 * p4:32 * p4 + 32], Yexp[:, m, ri, 4 * ph + p4, :], ['Yexp'], ['YZb%d' % sl], eng='act' if p4 % 2 else 'pool')
                return sl

            for ri in range(2):
                for q in range(8):
                    mm(pb[4][:, (ri * 8 + q) * 16:(ri * 8 + q + 1) * 16], WinZ[:, 0, ri, q, :], uT[:, q // 4, NP_:NT], True, True,
                       ['WinZ', 'uT'], ['pb4'])
            lr1b = lr1.unsqueeze(2).to_broadcast([128, 8, 16])
            li1b = li1.unsqueeze(2).to_broadcast([128, 8, 16])
            bur = pb[4][:, 0:128].rearrange("p (q b) -> p q b", q=8)
            bui = pb[4][:, 128:256].rearrange("p (q b) -> p q b", q=8)
            tt(t16[:, 0], h0S[:, 0], lr1b, ALU.mult, ['h0S'] + kk, tk)
            tt(t16[:, 1], h0S[:, 1], li1b, ALU.mult, ['h0S'] + kk, tk)
            tt(h1S[:, 0], t16[:, 0], t16[:, 1], ALU.subtract, tk, ['h1S'])
            tt(h1S[:, 0], h1S[:, 0], bur, ALU.add, ['h1S', 'pb4'], ['h1S'])
            tt(t16[:, 0], h0S[:, 1], lr1b, ALU.mult, ['h0S'] + kk, tk)
            tt(t16[:, 1], h0S[:, 0], li1b, ALU.mult, ['h0S'] + kk, tk)
            tt(h1S[:, 1], t16[:, 0], t16[:, 1], ALU.add, tk, ['h1S'])
            tt(h1S[:, 1], h1S[:, 1], bui, ALU.add, ['h1S', 'pb4'], ['h1S'])
            cp(h1b[:], h1S[:], ['h1S'], ['h1b'], eng='act')
            for ri in range(2):
                dma('sp', o_s5_s[l, ri], h1S[:, ri], ['h1S'], [])

            def gelu_to(z_out, n_):
                act(gtmp[:, :n_], ytmp[:, :n_], AF.Square, zk, zk)
                ts(gtmp[:, :n_], gtmp[:, :n_], 0.044715, 1.0, ALU.mult, ALU.add, zk, zk)
                tt(gtmp[:, :n_], gtmp[:, :n_], ytmp[:, :n_], ALU.mult, zk, zk)
                act(gtmp[:, :n_], gtmp[:, :n_], AF.Sigmoid, zk, zk, scale=1.5957691216057308)
                return gtmp

            for ph in range(2):
                first = True
                for ri in range(2):
                    sl = yz_load(0, ri, ph)
                    for p4 in range(4):
                        mm(pb[5][:, ph * 16:(ph + 1) * 16], YZb[:, sl, p4, :], h1b[:, ri, 4 * ph + p4, :], first, (ri == 1 and p4 == 3),
                           ['YZb%d' % sl, 'h1b'], ['pb5'])
                        first = False
                stt(ytmp[:, :NS], uT[:, ph, NP_:NT], d5[:, ph:ph + 1], pb[5][:, ph * 16:(ph + 1) * 16], ALU.mult, ALU.add,
                    ['uT', 'd5', 'pb5'] + zk, zk)
                gelu_to(None, NS)
                tt(zT[:, ph, NP_:NT], ytmp[:, :NS], gtmp[:, :NS], ALU.mult, zk, ['zT'])
            for ph in range(2):
                uv16 = uT[:, ph, 0:NP_].rearrange("p (n j) -> p j n", j=16)
                zv16 = zT[:, ph, 0:NP_].rearrange("p (n j) -> p j n", j=16)
                for i in range(16):
                    bk = i // 4
                    oc = pb[bk][:, (i % 4) * 128:(i % 4 + 1) * 128]
                    for j in range(i + 1):
                        mm(oc, Kblk[:, i - j, ph, :], uv16[:, j, :], j == 0, False, ['Kblk', 'uT'], ['pb%d' % bk])
                    for ri in range(2):
                        sl = yz_load(i + 1, ri, ph)
                        for p4 in range(4):
                            mm(oc, YZb[:, sl, p4, :], Sb[:, ri, 4 * ph + p4, 0:128], False, (ri == 1 and p4 == 3),
                               ['YZb%d' % sl, 'Sb'], ['pb%d' % bk])
                    if i % 4 == 3:
                        yv = ytmp[:, :512].rearrange("p (i n) -> p i n", i=4)
                        stt(yv, uv16[:, 4 * bk:4 * bk + 4, :], d5[:, ph:ph + 1], pb[bk][:, :512].rearrange("p (i n) -> p i n", i=4),
                            ALU.mult, ALU.add, ['uT', 'd5', 'pb%d' % bk] + zk, zk)
                        gelu_to(None, 512)
                        tt(zv16[:, 4 * bk:4 * bk + 4, :], yv, gtmp[:, :512].rearrange("p (i n) -> p i n", i=4), ALU.mult, zk, ['zT'])
            cols = [(i * 512, 512) for i in range(4)] + [(NP_, NS)]
            for co in range(2):
                for (c0_, n_) in cols:
                    bk = nxt(0, 2)
                    for k in range(2):
                        mm(pb[bk][:, :n_], wgB[:, k, co * 128:(co + 1) * 128], zT[:, k, c0_:c0_ + n_], k == 0, k == 1, ['wgB', 'zT'], ['pb%d' % bk])
                    act(gtmp[:, :n_], pb[bk][:, :n_], AF.Sigmoid, ['pb%d' % bk, 'bg5'] + zk, zk, bias=bg5[:, co:co + 1])
                    tt(uT[:, co, c0_:c0_ + n_], zT[:, co, c0_:c0_ + n_], gtmp[:, :n_], ALU.mult, ['zT'] + zk, ['uT'])
            for g in range(NG):
                t0, n = GROUPS[g]
                for m in range(8):
                    bk = nxt(2, 2)
                    for k in range(2):
                        mm(pb[bk][:, :n], wout_v[:, k, m * 128:(m + 1) * 128], uT[:, k, t0:t0 + n], k == 0, k == 1, ['wout', 'uT'], ['pb%d' % bk])
                    tt(xT[:, m, t0:t0 + n], xT[:, m, t0:t0 + n], pb[bk][:, :n], ALU.add, ['x%d_%d' % (m, g), 'pb%d' % bk], ['x%d_%d' % (m, g)])

        I16 = identf[:NS, :NS]

        def smp_update(heads, a_tile, k_ap, v_ap, q_ap, st_in, st_out, o_dst, rk):
            tiles = sorted(set(f for (_, f, _, _) in heads))
            vblk = kaT[:NS].rearrange("p a b c -> p (a b c)")
            kz = qgB[:NS, 0:128]
            aT = xtok[:, 0:NS]
            qzT = xtok[:, NS:2 * NS]
            for f in tiles:
                for (rows, src) in st_in(f):
                    dma('sp', smpS[rows, :, :], src, [], ['smpS'])
                mm(pb[4][:, 0:NS], a_tile(f), I16, True, True, rk + ['identf'], ['pb4'])
                cp(aT, pb[4][:, 0:NS], ['pb4'], ['xtok'])
                tt(smpS[:], smpS[:], aT.unsqueeze(2).to_broadcast([128, NS, 64]), ALU.mult, ['smpS', 'xtok'], ['smpS'])
                hs = [hh for hh in heads if hh[1] == f]
                for idx, (h, _, r0, K) in enumerate(hs):
                    P.op('pool', lambda e: e.memset(kz, 0.0), reads=['qgB'], writes=['qgB'])
                    cp(kz[:, r0:r0 + K], k_ap(h), rk, ['qgB'])
                    tt(vblk.rearrange("p (b v) -> p b v", b=NS), v_ap(h).unsqueeze(1).to_broadcast([NS, NS, 64]),
                       I16.unsqueeze(2).to_broadcast([NS, NS, 64]), ALU.mult, rk + ['identf'], ['kaT'])
                    for hf in range(2):
                        mm(pb[2 + hf][:, :512], kz, vblk[:, hf * 512:(hf + 1) * 512], idx == 0, idx == len(hs) - 1,
                           ['qgB', 'kaT'], ['pb%d' % (2 + hf)])
                sflat = smpS[:].rearrange("p b v -> p (b v)")
                for hf in range(2):
                    tt(sflat[:, hf * 512:(hf + 1) * 512], sflat[:, hf * 512:(hf + 1) * 512], pb[2 + hf][:, :512], ALU.add,
                       ['smpS', 'pb%d' % (2 + hf)], ['smpS'])
                for (rows, dst) in st_out(f):
                    dma('sp', dst, smpS[rows, :, :], ['smpS'], [])
                for (h, _, r0, K) in hs:
                    mm(pb[4][:K, 0:NS], q_ap(h), I16, True, True, rk + ['identf'], ['pb4'])
                    P.op('dve', lambda e: e.memset(qzT, 0.0), reads=['xtok'], writes=['xtok'])
                    cp(qzT[r0:r0 + K, :], pb[4][:K, 0:NS], ['pb4'], ['xtok'])
                    for hf in range(2):
                        mm(pb[2 + hf][:NS, :512], qzT, sflat[:, hf * 512:(hf + 1) * 512], True, True, ['xtok', 'smpS'], ['pb%d' % (2 + hf)])
                    for hf in range(2):
                        tt(dgT[:].rearrange("p (b v) -> p b v", b=8), pb[2 + hf][:NS, :512].rearrange("p (b v) -> p b v", b=8),
                           identf[:NS, hf * 8:hf * 8 + 8].unsqueeze(2).to_broadcast([NS, 8, 64]), ALU.mult, ['pb%d' % (2 + hf), 'identf'], ['dgT'])
                        dst = o_dst[:, h * 64:(h + 1) * 64] if hf == 0 else o2S[:NS, 0:64]
                        P.op('dve', lambda e, dst=dst: e.tensor_reduce(out=dst, in_=dgT[:].rearrange("p (b v) -> p v b", b=8), axis=AX.X, op=ALU.add),
                             reads=['dgT'], writes=['oS', 'o2S'])
                    tt(o_dst[:, h * 64:(h + 1) * 64], o_dst[:, h * 64:(h + 1) * 64], o2S[:NS, 0:64], ALU.add, ['oS', 'o2S'], ['oS'])

        def sample_tile(l):
            R_ = slice(0, NS)
            for (a, b, po) in TOK_GROUPS:
                bk = nxt(0, 2)
                for c in range(8):
                    mm(pb[bk][R_, :b - a], hnT[:, c, 0:NS], win_v[:, c, a:b], c == 0, c == 7, ['hnT%d' % c] + win_key(a, b), ['pb%d' % bk])
                cp(pj[R_, po:po + (b - a)], pb[bk][R_, :b - a], ['pb%d' % bk], ['pj'], eng='act')
            bk = nxt(0, 2)
            for c in range(8):
                mm(pb[bk][R_, :512], hnT[:, c, 0:NS], win_v[:, c, 512:1024], c == 0, c == 7, ['hnT%d' % c] + win_key(512, 1024), ['pb%d' % bk])
            xq = ['laS', 'qS']
            cp(sdR[R_, :], pb[bk][R_, :512], ['pb%d' % bk], xq, eng='act')
            dma('sp', o_conv_s[l, :, 0:2, :], si_conv[l, :, 1:3, :], [], [])
            dma('sp', o_conv_s[l, :, 2, :], sdR[R_, :], xq, [])
            xbf = xbcT[:].rearrange("p c n -> p (c n)")
            wrow, brow = xbf[R_, 0:512], xbf[R_, 512:1024]
            acc = sdD[R_, :]
            ak = ['ex0', 'ex1']
            dma('sp', wrow, ssd_cwr[l, 3], [], ['xbcT'])
            dma('sp', brow, ssd_cbr[l], [], ['xbcT'])
            tt(acc, sdR[R_, :], wrow, ALU.mult, xq + ['xbcT'], ak)
            tt(acc, acc, brow, ALU.add, ak + ['xbcT'], ak)
            for j in range(3):
                dma('sp', wrow, ssd_cwr[l, j], [], ['xbcT'])
                dma('sp', brow, si_conv[l, :, j, :], [], ['xbcT'])
                tt(brow, brow, wrow, ALU.mult, ['xbcT'], ['xbcT'])
                tt(acc, acc, brow, ALU.add, ak + ['xbcT'], ak)
            act(acc, acc, AF.Silu, ak, ak)
            tt(dtS[R_, 0:4], pj[R_, 256:260], srow[R_, l, 0, :], ALU.add, ['pj', 'srow'], ['dtS'])
            act(dtS[R_, 0:4], dtS[R_, 0:4], AF.Exp, ['dtS'], ['dtS'])
            act(dtS[R_, 0:4], dtS[R_, 0:4], AF.Ln, ['dtS'], ['dtS'], bias=1.0)
            tt(dtS[R_, 4:8], dtS[R_, 0:4], arow[R_, l, :], ALU.mult, ['dtS', 'arow'], ['dtS'])
            act(dtS[R_, 4:8], dtS[R_, 4:8], AF.Exp, ['dtS'], ['dtS'])
            vtok = kS[R_, :]
            tt(vtok.rearrange("p (h v) -> p h v", h=4), sdD[R_, 0:256].rearrange("p (h v) -> p h v", h=4),
               dtS[R_, 0:4].unsqueeze(2).to_broadcast([NS, 4, 64]), ALU.mult, ak + ['dtS'], ['kS'])
            aexp = sgS[R_, 0:128]

            def ssd_a(f):
                cp(aexp.rearrange("p (g n) -> p g n", g=2), dtS[R_, 4 + f:8:2].unsqueeze(2).to_broadcast([NS, 2, 64]), ['dtS'], ['sgS'])
                return aexp
            ssd_heads = [(2 * gg + r, r, 64 * gg, 64) for r in range(2) for gg in range(2)]
            smp_update(ssd_heads, ssd_a,
                       lambda h: sdD[R_, 256 + 64 * (h // 2):256 + 64 * (h // 2) + 64],
                       lambda h: vtok[:, h * 64:(h + 1) * 64],
                       lambda h: sdD[R_, 384 + 64 * (h // 2):384 + 64 * (h // 2) + 64],
                       lambda f: [(slice(64 * gg, 64 * gg + 64), si_ssd[l, :, 2 * gg + f].rearrange("b n p -> n b p")) for gg in range(2)],
                       lambda f: [(slice(64 * gg, 64 * gg + 64), o_ssd_s[l, :, 2 * gg + f].rearrange("b n p -> n b p")) for gg in range(2)],
                       oS[R_, :], ak + ['kS', 'sgS', 'dtS'])
            tt(o2S[R_, :].rearrange("p (h v) -> p h v", h=4), sdD[R_, 0:256].rearrange("p (h v) -> p h v", h=4),
               srow[R_, l, 2, :].unsqueeze(2).to_broadcast([NS, 4, 64]), ALU.mult, ak + ['srow'], ['o2S'])
            tt(oS[R_, :], oS[R_, :], o2S[R_, :], ALU.add, ['oS', 'o2S'], ['oS'])
            act(o2S[R_, :], pj[R_, 0:256], AF.Silu, ['pj'], ['o2S'])
            tt(oS[R_, :], oS[R_, :], o2S[R_, :], ALU.mult, ['oS', 'o2S'], ['oS'])
            tt(o2S[R_, :], oS[R_, :], oS[R_, :], ALU.mult, ['oS'], ['o2S'])
            P.op('dve', lambda e: e.tensor_reduce(out=ssq[R_, 0:1], in_=o2S[R_, :], axis=AX.X, op=ALU.add), reads=['o2S'], writes=['ssq'])
            act(ssq[R_, 0:1], ssq[R_, 0:1], AF.Sqrt, ['ssq'], ['ssq'], bias=EPS, scale=1.0 / 256)
            P.op('dve', lambda e: e.reciprocal(out=ssq[R_, 0:1], in_=ssq[R_, 0:1]), reads=['ssq'], writes=['ssq'])
            stt(mixtok[R_, 256:512], oS[R_, :], ssq[R_, 0:1], ssdng[R_, :], ALU.mult, ALU.mult, ['oS', 'ssq', 'ssdng'], ['mixtok'])
            qh = sdR[R_, 256:512]
            act(qh, pj[R_, 260:516], AF.Silu, ['pj'], xq)
            act(sgS[R_, :], pj[R_, 516:772], AF.Sigmoid, ['pj'], ['sgS'])
            tt(sgS[R_, :], sgS[R_, :], oml[R_, :], ALU.mult, ['sgS', 'oml'], ['sgS'])
            tt(sgS[R_, :], sgS[R_, :], lbl[R_, l, :], ALU.add, ['sgS', 'lbl'], ['sgS'])
            ts(kS[R_, :], sgS[R_, :], -1.0, 1.0, ALU.mult, ALU.add, ['sgS'], ['kS'])
            hg_heads = [(h, h // 2, 64 * (h % 2), 64) for h in range(4)]
            smp_update(hg_heads, lambda f: sgS[R_, f * 128:(f + 1) * 128],
                       lambda h: kS[R_, h * 64:(h + 1) * 64],
                       lambda h: pj[R_, 772 + h * 64:772 + (h + 1) * 64],
                       lambda h: qh[:, h * 64:(h + 1) * 64],
                       lambda f: [(slice(64 * r, 64 * r + 64), si_hg[l, :, 2 * f + r].rearrange("b k v -> k b v")) for r in range(2)],
                       lambda f: [(slice(64 * r, 64 * r + 64), o_hg_s[l, :, 2 * f + r].rearrange("b k v -> k b v")) for r in range(2)],
                       oS[R_, :], xq + ['sgS', 'kS', 'pj'])
            head_norm_gate(l, hgng[R_, :].rearrange("p (h v) -> p h v", h=4), 'hgng', pj[R_, 1028:1284], mixtok[R_, 512:768], npart=NS, from_psum=False)
            for c in range(8):
                mm(pb[4][:16, :NS], win_v[:, c, 2820:2836], hnT[:, c, 0:NS], c == 0, c == 7, win_key(2820, 2836) + ['hnT%d' % c], ['pb4'])
            cp(lrT[:, :NS], pb[4][:16, :NS], ['pb4'], ['lrT'])
            mm(pb[5][R_, :128], lrT[:, :NS], w2[:], True, True, ['lrT', 'w2'], ['pb5'])
            ga = sgS[R_, 0:128]
            tt(ga, pb[5][R_, :128], glab[R_, :], ALU.add, ['pb5', 'glab'], ['sgS'])
            act(ga, ga, AF.Exp, ['sgS'], ['sgS'], scale=-1.0)
            act(ga, ga, AF.Ln, ['sgS'], ['sgS'], bias=1.0)
            act(ga, ga, AF.Exp, ['sgS'], ['sgS'], scale=-1.0 / 16.0)
            gq = sdR[R_, 256:384]
            ts(gq, pj[R_, 1284:1412], 32.0 ** -0.5, None, ALU.mult, None, ['pj'], xq)
            gla_heads = [(h, 0, 32 * h, 32) for h in range(4)]
            smp_update(gla_heads, lambda f: ga,
                       lambda h: pj[R_, 1412 + h * 32:1412 + (h + 1) * 32],
                       lambda h: pj[R_, 1540 + h * 64:1540 + (h + 1) * 64],
                       lambda h: gq[:, h * 32:(h + 1) * 32],
                       lambda f: [(slice(0, 128), si_gla[l].rearrange("b h k v -> (h k) b v"))],
                       lambda f: [(slice(0, 128), o_gla_s[l].rearrange("b h k v -> (h k) b v"))],
                       oS[R_, :], xq + ['sgS', 'pj'])
            head_norm_gate(l, glang[R_, :].unsqueeze(1).to_broadcast([NS, 4, 64]), 'glang', pj[R_, 1796:2052], mixtok[R_, 768:1024], npart=NS, from_psum=False)
            for f in range(2, 8):
                mm(pb[6][:, f * NS:(f + 1) * NS], mixtok[R_, f * 128:(f + 1) * 128], identb[:NS, :NS], True, True, ['mixtok', 'identb'], ['pb6'])
            cp(mixT[:, 2:8, :NS], pb[6][:, 2 * NS:8 * NS].rearrange("p (f t) -> p f t", f=6), ['pb6'], ['mixT'])

        def out_proj(g, tc0, n):
            for m in range(8):
                bk = nxt(0, 2)
                for c in range(2, 8):
                    mm(pb[bk][:, :n], wout_v[:, c, m * 128:(m + 1) * 128], mixT[:, c, :n], c == 2, c == 7,
                       ['wout', 'mixT'], ['pb%d' % bk])
                tt(xT[:, m, tc0:tc0 + n], xT[:, m, tc0:tc0 + n], pb[bk][:, :n], ALU.add, ['x%d_%d' % (m, g), 'pb%d' % bk],
                   ['x%d_%d' % (m, g)])

        load_phaseA_weights(0)
        for l in range(RUN_LAYERS):
            dma('sp', hgng[:], hg_norm_g[:, l, :], [], ['hgng'])
            dma('sp', glang[:], gla_norm_g[:, l, :], [], ['glang'])
            dma('sp', glab[:], gla_b_gk[:, l, :], [], ['glab'])
            dma('sp', ssdng[:], ssd_ng[:, l, :], [], ['ssdng'])
            dma('sp', w2[:], gla_w_gk2[:, l, :], [], ['w2'])
            ts(oml[:], lbl[:, l, :], -1.0, 1.0, ALU.mult, ALU.add, ['lbl'], ['oml'])
            for nm in ('hg', 'gla'):
                P.op('pool', lambda e, nm=nm: e.memset(qlz[nm][:], 0.0), writes=['qlz_' + nm])
            P.op('pool', lambda e: e.memset(Cz[:], 0.0), writes=['Cz'])
            P.op('pool', lambda e: e.memset(BhatZ[:], 0.0), writes=['BhatZ'])
            P.op('dve', lambda e: e.memset(st_ssd[:], 0.0), writes=['st_ssd'])
            P.op('dve', lambda e: e.memset(stb_ssd[:], 0.0), writes=['stb_ssd'])
            for nm in ('hg', 'gla'):
                P.op('dve', lambda e, nm=nm: e.memset(hst[nm][:], 0.0), writes=['hst_' + nm])
                P.op('dve', lambda e, nm=nm: e.memset(stz[nm][:], 0.0), writes=['stz_' + nm])
            pro_done = set()
            pending_fin = []
            for g in range(RUN_GROUPS):
                t0, n = GROUPS[g]

                def prologue_gen(g, l=l):
                    t0_, n_ = GROUPS[g]
                    rmsnorm_group(g, lambda c, l=l: gmix[:, l, c:c + 1], 'gmix', hnT, 'hnT', 0)
                    yield
                    if g > 0 and g < SAMP:
                        cp(xbcT[:, :, 0:3], xbcT[:, :, GN:GN + 3], ['xbcT'], ['xbcT'])
                    elif g == 0:
                        P.op('dve', lambda e: e.memset(xbcT[:, :, 0:3], 0.0), reads=['xbcT'], writes=['xbcT'])
                    for m in ((0, 1) if g == SAMP else (0, 1, 4, 5, 6, 7)):
                        bk = nxt(0, 2)
                        for c in range(8):
                            mm(pb[bk][:, :n_], win_v[:, c, m * 128:(m + 1) * 128], hnT[:, c, :n_], c == 0, c == 7,
                               win_key(m * 128, (m + 1) * 128) + ['hnT%d' % c], ['pb%d' % bk])
                        if m < 2:
                            cp(uT[:, m, t0_:t0_ + n_], pb[bk][:, :n_], ['pb%d' % bk], ['uT'], eng='act')
                        else:
                            cp(xbcT[:, m - 4, 3:3 + n_], pb[bk][:, :n_], ['pb%d' % bk], ['xbcT'], eng='act')
                        yield

                if g not in pro_done:
                    for _ in prologue_gen(g):
                        pass
                    pro_done.add(g)
                if g < SAMP and RUN_SSD:
                    for cc in range(4):
                        ts(sdR[:, :n], xbcT[:, cc, 0:n], cwS[:, l, cc, 0:1], None, ALU.mult, None, ['xbcT', 'cwS'], ['laS', 'qS'])
                        for j in range(1, 4):
                            stt(sdR[:, :n], xbcT[:, cc, j:j + n], cwS[:, l, cc, j:j + 1], sdR[:, :n], ALU.mult, ALU.add,
                                ['xbcT', 'cwS', 'laS', 'qS'], ['laS', 'qS'])
                        act(xsT[:, cc, :n], sdR[:, :n], AF.Silu, ['laS', 'qS', 'cbS'], ['xsT'], bias=cbS[:, l, cc:cc + 1])
                    for gg in range(2):
                        cp(Cz[64 * gg:64 * gg + 64, gg, :n], xsT[64 * gg:64 * gg + 64, 3, :n], ['xsT'], ['Cz'], eng='pool')
                if g == LASTP and RUN_CONV_OUT:
                    for cc in range(4):
                        dma_nc('sp', o_conv_p[l][:, cc * 128:(cc + 1) * 128].rearrange("j p -> p j"), xbcT[:, cc, GN:GN + 3], ['xbcT'], [])
                if g == SAMP:
                    while pending_fin:
                        for _ in pending_fin.pop():
                            pass
                    if RUN_SAMPLE:
                        P.barrier()
                        sample_tile(l)
                    else:
                        P.op('pool', lambda e, n=n: e.memset(mixT[:, :, :n], 0.0), reads=['mixT'], writes=['mixT'])
                else:
                    for t in range(RUN_TILES):
                        c0 = t * 128
                        for (a, b, po) in TOK_GROUPS:
                            bk = nxt(0, 2)
                            for c in range(8):
                                mm(pb[bk][:, :b - a], hnT[:, c, c0:c0 + 128], win_v[:, c, a:b], c == 0, c == 7,
                                   ['hnT%d' % c] + win_key(a, b), ['pb%d' % bk])
                            cp(pj[:, po:po + (b - a)], pb[bk][:, :b - a], ['pb%d' % bk], ['pj'], eng='act')
                        def hg_gen(l=l, c0=c0):
                            act(qS[:], pj[:, 260:516], AF.Silu, ['pj'], ['qS'])
                            act(sgS[:], pj[:, 516:772], AF.Sigmoid, ['pj'], ['sgS'])
                            yield
                            tt(sgS[:], sgS[:], oml[:], ALU.mult, ['sgS', 'oml'], ['sgS'])
                            tt(sgS[:], sgS[:], lbl[:, l, :], ALU.add, ['sgS', 'lbl'], ['sgS'])
                            act(laS[:], sgS[:], AF.Ln, ['sgS'], ['laS'])
                            ts(kS[:], sgS[:], -1.0, 1.0, ALU.mult, ALU.add, ['sgS'], ['kS'])
                            cp(vB[:], pj[:, 772:1028], ['pj'], ['vB'])
                            yield
                            yield from gla_like(B_HG, 'hg', l, 4, 64, 128, qS[:], kS[:], vB, laS[:], 1.0, 4, ['qS', 'kS', 'laS'])
                            yield from head_norm_gate_gen(l, hgng[:].rearrange("p (h v) -> p h v", h=4), 'hgng', pj[:, 1028:1284], mixtok[:, 512:768], B=B_HG)
                            yield

                        def gla_gen(l=l, c0=c0):
                            gla_, gq_, gk_ = g_lqk[:, 0, :], g_lqk[:, 1, :], g_lqk[:, 2, :]
                            gk3 = ['g_la', 'g_q', 'g_k']
                            for c in range(8):
                                mm(pb[3][:16, :128], win_v[:, c, 2820:2836], hnT[:, c, c0:c0 + 128], c == 0, c == 7,
                                   win_key(2820, 2836) + ['hnT%d' % c], ['pb3'])
                            cp(lrT[:], pb[3][:16, :128], ['pb3'], ['lrT'])
                            yield
                            mm(pb[2][:, :128], lrT[:], w2[:], True, True, ['lrT', 'w2'], ['pb2'])
                            tt(gla_, pb[2][:, :128], glab[:], ALU.add, ['pb2', 'glab'], ['g_la'])
                            yield
                            act(gla_, gla_, AF.Exp, ['g_la'], ['g_la'], scale=-1.0)
                            act(gla_, gla_, AF.Ln, ['g_la'], ['g_la'], bias=1.0)
                            ts(gq_, pj[:, 1284:1412], 32.0 ** -0.5, None, ALU.mult, None, ['pj'], ['g_q'])
                            cp(gk_, pj[:, 1412:1540], ['pj'], ['g_k'])
                            cp(g_vB[:], pj[:, 1540:1796], ['pj'], ['g_vB'])
                            yield
                            yield from gla_like(B_GL, 'gla', l, 4, 32, 128, gq_, gk_, g_vB, gla_, -1.0 / 16.0, 1, gk3)
                            yield from head_norm_gate_gen(l, glang[:].unsqueeze(1).to_broadcast([128, 4, 64]), 'glang', pj[:, 1796:2052],
                                           mixtok[:, 768:1024], B=B_GL)
                            yield

                        gens = [ssd_tile(l, c0), gla_gen()]
                        while pending_fin:
                            gens.append(pending_fin.pop())
                        hg_started = False
                        if not INTERLEAVE:
                            for gi in gens + [hg_gen()]:
                                for _ in gi:
                                    pass
                            gens = []
                        prio = set()
                        while gens:
                            for gi in list(gens):
                                try:
                                    r_ = next(gi)
                                    if id(gi) in prio:
                                        r_ = next(gi)
                                except StopIteration:
                                    gens.remove(gi)
                                    continue
                                if r_ == 'front_done' and not hg_started:
                                    hg_it = hg_gen()
                                    gens.insert(0, hg_it)
                                    prio.add(id(hg_it))
                                    hg_started = True
                                    if PRO_AHEAD and t == RUN_TILES - 1 and g + 1 < RUN_GROUPS and (g + 1) not in pro_done:
                                        gens.append(prologue_gen(g + 1))
                                        pro_done.add(g + 1)
                        def finish_gen(g=g, tc0=t0 + c0):
                            for f in range(2, 8):
                                P.op('pe', lambda e, f=f: e.transpose(ptb[:, f * 128:(f + 1) * 128], mixtok[:, f * 128:(f + 1) * 128], identb[:]),
                                     reads=['mixtok', 'identb'], writes=['ptb'])
                            cp(mixT[:, 2:8, :], ptb[:, 256:1024].rearrange("p (f t) -> p f t", f=6), ['ptb'], ['mixT'])
                            yield
                            for m0 in range(0, 8, 2):
                                for m in (m0, m0 + 1):
                                    bk = nxt(0, 2)
                                    for c in range(2, 8):
                                        mm(pb[bk][:, :128], wout_v[:, c, m * 128:(m + 1) * 128], mixT[:, c, :128], c == 2, c == 7,
                                           ['wout', 'mixT'], ['pb%d' % bk])
                                    tt(xT[:, m, tc0:tc0 + 128], xT[:, m, tc0:tc0 + 128], pb[bk][:, :128], ALU.add,
                                       ['x%d_%d' % (m, g), 'pb%d' % bk], ['x%d_%d' % (m, g)])
                                yield

                        last_prompt_tile = (g == min(LASTP, RUN_GROUPS - 1) or g == LASTP) and t == RUN_TILES - 1
                        if FIN_DEFER and not last_prompt_tile and g < LASTP + 1:
                            pending_fin.append(finish_gen())
                        else:
                            for _ in finish_gen():
                                pass
                    if g == LASTP:
                        dma('sp', o_hg_p[l].rearrange("(f p) v -> p f v", p=128), hst['hg'][:], ['hst_hg'], [])
                        for gg in range(2):
                            dma('sp', o_ssd_p[l, 2 * gg:2 * gg + 2].rearrange("r n p -> n r p"), st_ssd[64 * gg:64 * gg + 64, :, :], ['st_ssd'], [])
                        dma('sp', o_gla_p[l], hst['gla'][:, 0, :], ['hst_gla'], [])
                if g == SAMP:
                    out_proj(g, t0, n)
            if RUN_S5:
                P.barrier()
                s5_phase(l)
            P.barrier()
            load_mlp_block(l, 0)
            load_mlp_block(l, 1)
            for g in range(NG):
                rmsnorm_group(g, lambda c, l=l: gmlp[:, l, c:c + 1], 'gmlp', hn2T, 'hn2T%d_' % g, GROUPS[g][0])
            for b in range(4 if RUN_MLP else 0):
                up, dn = mlp_views(b)
                hb = b % 2

                def mlp_up(g, hbuf, hk):
                    t0, n = GROUPS[g]
                    for m in range(8):
                        bk = nxt(0, 2)
                        for c in range(8):
                            mm(pb[bk][:, :n], up[:, c, m * 128:(m + 1) * 128], hn2T[:, c, t0:t0 + n], c == 0, c == 7,
                               ['mup%d' % hb, 'hn2T%d_%d' % (g, c)], ['pb%d' % bk])
                        act(hbuf[:, m, :n], pb[bk][:, :n], AF.Relu, ['pb%d' % bk], [hk + '%d' % m])
                        tt(hbuf[:, m, :n], hbuf[:, m, :n], hbuf[:, m, :n], ALU.mult, [hk + '%d' % m], [hk + '%d' % m])

                def mlp_down(g, hbuf, hk):
                    t0, n = GROUPS[g]
                    for m in range(8):
                        bk = nxt(2, 2)
                        for c in range(8):
                            mm(pb[bk][:, :n], dn[:, c, m * 128:(m + 1) * 128], hbuf[:, c, :n], c == 0, c == 7,
                               ['mdn%d' % hb, hk + '%d' % c], ['pb%d' % bk])
                        tt(xT[:, m, t0:t0 + n], xT[:, m, t0:t0 + n], pb[bk][:, :n], ALU.add,
                           ['x%d_%d' % (m, g), 'pb%d' % bk], ['x%d_%d' % (m, g)])

                hbufs = [(hT, 'hT'), (hT2, 'hU')]
                mlp_up(0, *hbufs[0])
                for g in range(NG):
                    if g + 1 < NG:
                        mlp_up(g + 1, *hbufs[(g + 1) % 2])
                    mlp_down(g, *hbufs[g % 2])
                if b + 2 < 4:
                    load_mlp_block(l, b + 2)
            P.barrier()
            if l + 1 < RUN_LAYERS:
                load_phaseA_weights(l + 1)
        for g in range(NG):
            t0, n = GROUPS[g]
            for c in range(8):
                act(hn2T[:, c, t0:t0 + n], xT[:, c, t0:t0 + n], AF.Square, ['x%d_%d' % (c, g)], ['hn2T%d_%d' % (g, c)])
            bk = nxt(0, 2)
            for c in range(8):
                mm(pb[bk][:, :n], onesb[:], hn2T[:, c, t0:t0 + n], c == 0, c == 7, ['onesb', 'hn2T%d_%d' % (g, c)], ['pb%d' % bk])
            act(rstd[:, :n], pb[bk][:, :n], AF.Sqrt, ['pb%d' % bk], ['rstd'], bias=EPS, scale=1.0 / D)
            P.op('dve', lambda e, n=n: e.reciprocal(out=rstd[:, :n], in_=rstd[:, :n]), reads=['rstd'], writes=['rstd'])
            for c in range(8):
                stt(xT[:, c, t0:t0 + n], xT[:, c, t0:t0 + n], gfin[:, c:c + 1], rstd[:, :n], ALU.mult, ALU.mult,
                    ['x%d_%d' % (c, g), 'gfin', 'rstd'], ['x%d_%d' % (c, g)])
        for c in range(8):
            dma('sp', yT[c * 128:(c + 1) * 128, :], xT[:, c, :], ['x%d_%d' % (c, g) for g in range(NG)], [])
        P.finish()
        P.emit()
    return nc


_CACHE = {}


def kernel(**inputs):
    f32 = np.float32
    x_prompt = np.asarray(inputs['x_prompt'], f32)
    x_sample = np.asarray(inputs['x_sample'], f32)
    B = x_prompt.shape[0]
    consts = host_consts()

    def pc(w):
        L, R, N = w.shape
        return np.ascontiguousarray(w.reshape(L, R // 128, 128, N).transpose(0, 2, 1, 3))

    def bc(v):
        v = np.asarray(v, f32)
        return np.ascontiguousarray(np.broadcast_to(v[None], (128,) + v.shape))

    def ql(v):
        v = np.asarray(v, f32)
        L = v.shape[0]
        rest = v.shape[3:]
        v = v.reshape((L, 8, 2, 64) + rest)
        nd = len(rest)
        v = v.transpose((2, 3, 0, 1) + tuple(range(4, 4 + nd)))
        return np.ascontiguousarray(v.reshape((128, L, 8) + rest))

    shared = {
        'w_in': pc(np.asarray(inputs['w_in'], f32)),
        'w_out': pc(np.asarray(inputs['w_out'], f32)),
        'w_up': pc(np.asarray(inputs['w_up'], f32)),
        'w_down': pc(np.asarray(inputs['w_down'], f32)),
        'g_mix': np.ascontiguousarray(np.asarray(inputs['norm_mix_g'], f32).reshape(DEPTH, 8, 128).transpose(2, 0, 1)),
        'g_mlp': np.ascontiguousarray(np.asarray(inputs['norm_mlp_g'], f32).reshape(DEPTH, 8, 128).transpose(2, 0, 1)),
        'g_fin': np.ascontiguousarray(np.asarray(inputs['norm_final_g'], f32).reshape(8, 128).T),
        'hg_lb_logits': bc(inputs['hg_lb_logits']),
        'hg_norm_g': bc(inputs['hg_norm_g']),
        'gla_w_gk2': np.ascontiguousarray(np.asarray(inputs['gla_w_gk2'], f32).transpose(1, 0, 2)),
        'gla_b_gk': bc(inputs['gla_b_gk']),
        'gla_norm_g': bc(inputs['gla_norm_g']),
        'ssd_cw': np.ascontiguousarray(np.asarray(inputs['ssd_conv_w'], f32).reshape(DEPTH, 4, 4, 128).transpose(3, 0, 2, 1)),
        'ssd_cb': np.ascontiguousarray(np.asarray(inputs['ssd_conv_b'], f32).reshape(DEPTH, 4, 128).transpose(2, 0, 1)),
        'ssd_rows': bc(np.stack([np.asarray(inputs['ssd_dt_bias'], f32), np.asarray(inputs['ssd_a_log'], f32),
                                 np.asarray(inputs['ssd_d'], f32)], axis=1)),
        'ssd_ng': bc(inputs['ssd_norm_g']),
        's5_lam': np.ascontiguousarray(np.stack([ql(inputs['s5_lam_re']), ql(inputs['s5_lam_im'])], axis=2)),
        's5_ldt': ql(np.repeat(np.asarray(inputs['s5_log_dt'], f32)[:, :, None], 64, axis=2)),
        's5_B': np.ascontiguousarray(np.stack([ql(inputs['s5_b_re']), ql(inputs['s5_b_im'])], axis=2)),
        's5_C': np.ascontiguousarray(np.stack([ql(np.asarray(inputs['s5_c_re'], f32).transpose(0, 1, 3, 2)),
                                              ql(np.asarray(inputs['s5_c_im'], f32).transpose(0, 1, 3, 2))], axis=2)),
        's5_d': np.ascontiguousarray(np.asarray(inputs['s5_d'], f32).reshape(DEPTH, 2, 128).transpose(2, 0, 1)),
        's5_bglu': np.ascontiguousarray(np.asarray(inputs['s5_b_glu'], f32).reshape(DEPTH, 2, 128).transpose(2, 0, 1)),
        's5_wglu': np.ascontiguousarray(np.asarray(inputs['s5_w_glu'], f32).reshape(DEPTH, 2, 128, 256).transpose(0, 2, 1, 3)),
        'ssd_cwr': np.ascontiguousarray(np.broadcast_to(np.asarray(inputs['ssd_conv_w'], f32)[:, :, None, :], (DEPTH, 4, NS, 512))),
        'ssd_cbr': np.ascontiguousarray(np.broadcast_to(np.asarray(inputs['ssd_conv_b'], f32)[:, None, :], (DEPTH, NS, 512))),
        'c_iota': np.ascontiguousarray(np.broadcast_to(np.arange(128, dtype=f32)[None], (128, 128))),
        'c_bd': np.kron(np.eye(4, dtype=f32), np.ones((32, 32), f32)),
        'c_ident': consts['ident'], 'c_ones': consts['ones'], 'c_tglob': consts['tglob'], 'c_tloc': consts['tloc'],
        'c_ltg': consts['ltg'], 'c_maskT': consts['tglob'],
        'c_ea': np.ascontiguousarray(consts['ea'].transpose(1, 0, 2)), 'c_ea0g': consts['ea0g'],
    }
    in_maps = []
    for i in range(8):
        xs = x_sample[i * NS:(i + 1) * NS, 0, :]
        xt = np.concatenate([x_prompt[i], xs], axis=0).T
        m = dict(shared)
        m['xT_in'] = np.ascontiguousarray(xt)
        h0 = []
        for nm in ('state_s5_re', 'state_s5_im'):
            v = np.asarray(inputs[nm], f32)[:, i * NS:(i + 1) * NS]
            v = v.reshape(DEPTH, NS, 8, 2, 64).transpose(0, 3, 4, 2, 1)
            h0.append(v.reshape(DEPTH, 128, 8, NS))
        m['s5_h0'] = np.ascontiguousarray(np.stack(h0, axis=1))
        bs = slice(i * NS, (i + 1) * NS)
        m['si_conv'] = np.ascontiguousarray(np.asarray(inputs['state_ssd_conv'], f32)[:, bs])
        m['si_ssd'] = np.ascontiguousarray(np.asarray(inputs['state_ssd'], f32)[:, bs])
        m['si_hg'] = np.ascontiguousarray(np.asarray(inputs['state_hgrn'], f32)[:, bs])
        m['si_gla'] = np.ascontiguousarray(np.asarray(inputs['state_gla'], f32)[:, bs])
        in_maps.append(m)
    if 'nc' not in _CACHE:
        _CACHE['nc'] = build_program()
    res = run_bass_kernel_spmd(_CACHE['nc'], in_maps, core_ids=list(range(8)))
    R = res.results
    y_prompt = np.stack([R[i]['yT'][:, :NP_].T for i in range(8)], 0)
    y_sample = np.concatenate([R[i]['yT'][:, NP_:].T for i in range(8)], 0)[:, None, :]
    cat = lambda k: np.ascontiguousarray(np.concatenate([R[i][k] for i in range(8)], axis=1))
    p_conv = np.stack([R[i]['o_conv_p'] for i in range(8)], 1)
    p_ssd = np.stack([R[i]['o_ssd_p'] for i in range(8)], 1)

    def unq(v):
        sh = v.shape[:-2]
        v = v.reshape(sh + (2, 64, 8))
        nd = len(sh)
        v = v.transpose(tuple(range(nd)) + (nd + 2, nd, nd + 1))
        return v.reshape(sh + (16, 64))
    p_s5 = [np.stack([unq(R[i]['o_s5_p'][:, ri]) for i in range(8)], 1) for ri in range(2)]
    s_s5 = []
    for ri in range(2):
        per = []
        for i in range(8):
            v = R[i]['o_s5_s'][:, ri]
            v = v.reshape(DEPTH, 2, 64, 8, NS).transpose(0, 4, 3, 1, 2)
            per.append(v.reshape(DEPTH, NS, 16, 64))
        s_s5.append(np.concatenate(per, axis=1))
    p_hg = np.stack([R[i]['o_hg_p'].reshape(DEPTH, 4, 64, 64) for i in range(8)], 1)
    p_gla = np.stack([R[i]['o_gla_p'].reshape(DEPTH, 4, 32, 64) for i in range(8)], 1)
    outs = (np.ascontiguousarray(y_prompt), np.ascontiguousarray(y_sample),
            np.ascontiguousarray(p_s5[0]), np.ascontiguousarray(p_s5[1]), np.ascontiguousarray(p_conv), np.ascontiguousarray(p_ssd),
            np.ascontiguousarray(p_hg), np.ascontiguousarray(p_gla),
            np.ascontiguousarray(s_s5[0]), np.ascontiguousarray(s_s5[1]), cat('o_conv_s'), cat('o_ssd_s'),
            cat('o_hg_s'), cat('o_gla_s'))
    return outs
```
